# Optimizing a Trainium2 kernel written in Bass

```python
import math
import jax, jax.numpy as jnp
from jax import lax
import numpy as np

D_MODEL = 1024
BATCH = 4
SEQ = 4096
DEPTH = 2
DEC_BATCH = 128
DEC_SEQ = 8
PAST_LEN = 2048
PAGE_SIZE = 128

H_A = 8
DH_A = 64
DV_A = 2 * DH_A
W_A = H_A * DV_A
G_B = 8
CHUNK_B = 128
W_B = D_MODEL
CG_B = W_B // G_B
H_C = 4
DK_C = D_MODEL // 2 // H_C
DV_C = D_MODEL // H_C
W_C = H_C * DV_C
GATE_RANK = 16
GLA_TAU = 16.0
GLA_CHUNK = 64
N_BUCKETS = 32
MAX_DISTANCE = 128
Q_BLOCK = 128
EPS = 1e-6
N_EVEN = (DEPTH + 1) // 2
N_ODD = DEPTH // 2
IN0 = 2 * H_A * 2 * DH_A + 2 * W_A + 3 * W_B
IN1 = 2 * H_C * DK_C + 2 * W_C + GATE_RANK

kernel_name = 'hybrid_diffattn_sgmlp_gla_step'


def rms_norm(x, g):
    xf = x.astype(jnp.float32)
    y = xf * lax.rsqrt(jnp.mean(xf * xf, axis=-1, keepdims=True) + EPS)
    return (y * g.astype(jnp.float32)).astype(x.dtype)


def layer_norm(x, g, b):
    xf = x.astype(jnp.float32)
    xc = xf - jnp.mean(xf, axis=-1, keepdims=True)
    y = xc * lax.rsqrt(jnp.mean(xc * xc, axis=-1, keepdims=True) + EPS)
    return (y * g.astype(jnp.float32) + b.astype(jnp.float32)).astype(x.dtype)


def t5_bucket(dist):
    n = jnp.maximum(dist, 0)
    max_exact = N_BUCKETS // 2
    nf = jnp.maximum(n, 1).astype(jnp.float32)
    large = max_exact + (jnp.log(nf / max_exact) / math.log(MAX_DISTANCE / max_exact)
                         * (N_BUCKETS - max_exact)).astype(jnp.int32)
    large = jnp.minimum(large, N_BUCKETS - 1)
    return jnp.where(n < max_exact, n, large)


def diff_attn_block(q, k, v, q_pos, k_pos, rel_bias, lam):
    s = jnp.einsum('bqhcd,bkhcd->bhcqk', q, k).astype(jnp.float32)
    bias = rel_bias[t5_bucket(q_pos[:, None] - k_pos[None, :])]
    s = s + jnp.transpose(bias, (2, 0, 1)).astype(jnp.float32)[None, :, None]
    causal = k_pos[None, :] <= q_pos[:, None]
    s = jnp.where(causal, s, -jnp.inf)
    p = jax.nn.softmax(s, axis=-1)
    a = p[:, :, 0] - lam * p[:, :, 1]
    return jnp.einsum('bhqk,bkhe->bqhe', a.astype(v.dtype), v)


def diff_attention(q, k, v, q_pos, k_pos, rel_bias, lam):
    b, lq = q.shape[0], q.shape[1]
    if lq <= Q_BLOCK:
        return diff_attn_block(q, k, v, q_pos, k_pos, rel_bias, lam)
    nb = -(-lq // Q_BLOCK)
    pad = nb * Q_BLOCK - lq
    qp = jnp.pad(q, ((0, 0), (0, pad), (0, 0), (0, 0), (0, 0)))
    pp = jnp.pad(q_pos, (0, pad), mode='edge')
    qb = jnp.moveaxis(qp.reshape(b, nb, Q_BLOCK, *q.shape[2:]), 1, 0)
    pb = pp.reshape(nb, Q_BLOCK)
    ob = lax.map(lambda blk: diff_attn_block(blk[0], k, v, blk[1], k_pos, rel_bias, lam), (qb, pb))
    o = jnp.moveaxis(ob, 0, 1).reshape(b, nb * Q_BLOCK, *ob.shape[3:])
    return o[:, :lq]


def chunk_spatial(v, sp_w, sp_b):
    b, l, g, cg = v.shape
    nc = -(-l // CHUNK_B)
    vp = jnp.pad(v, ((0, 0), (0, nc * CHUNK_B - l), (0, 0), (0, 0)))
    vc = vp.reshape(b, nc, CHUNK_B, g, cg)
    w = sp_w * jnp.tril(jnp.ones((CHUNK_B, CHUNK_B), sp_w.dtype))
    out = jnp.einsum('gts,bcsge->bctge', w, vc) + jnp.transpose(sp_b)[None, None, :, :, None]
    return out.reshape(b, nc * CHUNK_B, g, cg)[:, :l]


def gla_chunked(q, k, v, log_a, s0):
    b, l, h, dk = q.shape
    dv = v.shape[-1]
    c = min(GLA_CHUNK, l)
    nc = -(-l // c)
    pad = nc * c - l

    def prep(t):
        t = jnp.pad(t.astype(jnp.float32), ((0, 0), (0, pad), (0, 0), (0, 0)))
        return jnp.moveaxis(t.reshape(b, nc, c, h, t.shape[-1]), 1, 0)

    qc, kc, vc, ac = prep(q), prep(k), prep(v), prep(log_a)
    tri = jnp.tril(jnp.ones((c, c), bool))[None, :, :, None, None]

    def step(s, inp):
        qi, ki, vi, ai = inp
        bc = jnp.cumsum(ai, axis=1)
        o_inter = jnp.einsum('bthk,bhkv->bthv', qi * jnp.exp(bc), s)
        diff = bc[:, :, None] - bc[:, None, :]
        decay = jnp.exp(jnp.where(tri, diff, -jnp.inf))
        att = jnp.einsum('bthk,btshk,bshk->bhts', qi, decay, ki)
        o_intra = jnp.einsum('bhts,bshv->bthv', att, vi)
        b_last = bc[:, -1]
        s_new = jnp.exp(b_last)[..., None] * s + jnp.einsum(
            'bshk,bshv->bhkv', ki * jnp.exp(b_last[:, None] - bc), vi)
        return s_new, o_inter + o_intra

    s_fin, oc = lax.scan(step, s0.astype(jnp.float32), (qc, kc, vc, ac))
    o = jnp.moveaxis(oc, 0, 1).reshape(b, nc * c, h, dv)[:, :l]
    return o, s_fin


def even_layer(x, k_past, v_past, offset, rel_bias, lam_init, norm_g, w_in, q_g, k_g, lam_p,
               sub_g, ln_g, ln_b, sp_w, sp_b, w_out):
    b, l, _ = x.shape
    hq = H_A * 2 * DH_A
    z = rms_norm(x, norm_g) @ w_in
    cuts = np.cumsum([hq, hq, W_A, W_A, W_B, W_B]).tolist()
    zq, zk, zv, ga, zu, zvb, gb = jnp.split(z, cuts, axis=-1)
    q = rms_norm(zq.reshape(b, l, H_A, 2, DH_A), q_g) * (DH_A ** -0.5)
    k = rms_norm(zk.reshape(b, l, H_A, 2, DH_A), k_g)
    v = zv.reshape(b, l, H_A, DV_A)
    if k_past is None:
        k_all, v_all = k, v
    else:
        k_all = jnp.concatenate([k_past.astype(k.dtype), k], axis=1)
        v_all = jnp.concatenate([v_past.astype(v.dtype), v], axis=1)
    q_pos = offset + jnp.arange(l, dtype=jnp.int32)
    k_pos = jnp.arange(k_all.shape[1], dtype=jnp.int32)
    lf = lam_p.astype(jnp.float32)
    lam = jnp.exp(jnp.sum(lf[0] * lf[1])) - jnp.exp(jnp.sum(lf[2] * lf[3])) + lam_init
    o = diff_attention(q, k_all, v_all, q_pos, k_pos, rel_bias, lam)
    o = rms_norm(o, sub_g) * (1.0 - lam_init)
    o_a = o.reshape(b, l, W_A) * jax.nn.silu(ga)
    u = jax.nn.gelu(zu, approximate=False)
    vb = layer_norm(jax.nn.gelu(zvb, approximate=False), ln_g, ln_b)
    mixed = chunk_spatial(vb.reshape(b, l, G_B, CG_B), sp_w, sp_b).reshape(b, l, W_B)
    o_b = u * mixed * jax.nn.silu(gb)
    y = x + jnp.concatenate([o_a, o_b], axis=-1) @ w_out
    return y, k, v, vb


def odd_layer(x, s0, norm_g, w_in, w_gate, b_gate, o_g, w_out):
    b, l, _ = x.shape
    hk = H_C * DK_C
    z = rms_norm(x, norm_g) @ w_in
    zq, zk, zv, g, za = jnp.split(z, [hk, 2 * hk, 2 * hk + W_C, 2 * hk + 2 * W_C], axis=-1)
    q = zq.reshape(b, l, H_C, DK_C) * (DK_C ** -0.5)
    k = zk.reshape(b, l, H_C, DK_C)
    v = zv.reshape(b, l, H_C, DV_C)
    log_a = jax.nn.log_sigmoid((za @ w_gate + b_gate).astype(jnp.float32)) / GLA_TAU
    o, s = gla_chunked(q, k, v, log_a.reshape(b, l, H_C, DK_C), s0)
    o = rms_norm(o.astype(x.dtype), o_g).reshape(b, l, W_C) * jax.nn.silu(g)
    y = x + o @ w_out
    return y, s


def setup_inputs(seed: int = 0) -> dict:
    key = jax.random.key(seed)
    ks = jax.random.split(key, 32)
    f = jnp.float32
    n_pages = PAST_LEN // PAGE_SIZE
    used = DEC_BATCH * n_pages
    n_pool = used + max(1, used // 4)

    def nrm(k, shape, s):
        return jax.random.normal(k, shape, f) * s

    page_table = jax.random.permutation(ks[5], n_pool)[:used].reshape(DEC_BATCH, n_pages).astype(jnp.int32)
    return {
        'x_prompt': nrm(ks[0], (BATCH, SEQ, D_MODEL), 1.0),
        'x_sample': nrm(ks[1], (DEC_BATCH, DEC_SEQ, D_MODEL), 1.0),
        'cache_k': nrm(ks[2], (N_EVEN, n_pool, PAGE_SIZE, H_A, 2, DH_A), 1.0),
        'cache_v': nrm(ks[3], (N_EVEN, n_pool, PAGE_SIZE, H_A, DV_A), 1.0),
        'state_gla': nrm(ks[4], (N_ODD, DEC_BATCH, H_C, DK_C, DV_C), 1.0),
        'page_table': page_table,
        'rel_bias': nrm(ks[6], (N_BUCKETS, H_A), 0.2),
        'norm0_g': 1.0 + nrm(ks[7], (N_EVEN, D_MODEL), 0.05),
        'w_in0': nrm(ks[8], (N_EVEN, D_MODEL, IN0), D_MODEL ** -0.5),
        'q_norm_g': 1.0 + nrm(ks[9], (N_EVEN, 2, DH_A), 0.05),
        'k_norm_g': 1.0 + nrm(ks[10], (N_EVEN, 2, DH_A), 0.05),
        'lam': nrm(ks[11], (N_EVEN, 4, DH_A), 0.1),
        'subln_g': 1.0 + nrm(ks[12], (N_EVEN, DV_A), 0.05),
        'ln_v_g': 1.0 + nrm(ks[13], (N_EVEN, W_B), 0.05),
        'ln_v_b': nrm(ks[14], (N_EVEN, W_B), 0.02),
        'spatial_w': nrm(ks[15], (N_EVEN, G_B, CHUNK_B, CHUNK_B), CHUNK_B ** -0.5),
        'spatial_b': 1.0 + nrm(ks[16], (N_EVEN, G_B, CHUNK_B), 0.02),
        'w_out0': nrm(ks[17], (N_EVEN, W_A + W_B, D_MODEL), (W_A + W_B) ** -0.5),
        'norm1_g': 1.0 + nrm(ks[18], (N_ODD, D_MODEL), 0.05),
        'w_in1': nrm(ks[19], (N_ODD, D_MODEL, IN1), D_MODEL ** -0.5),
        'w_gate': nrm(ks[20], (N_ODD, GATE_RANK, H_C * DK_C), GATE_RANK ** -0.5),
        'b_gate': nrm(ks[21], (N_ODD, H_C * DK_C), 0.1),
        'gla_norm_g': 1.0 + nrm(ks[22], (N_ODD, DV_C), 0.05),
        'w_out1': nrm(ks[23], (N_ODD, W_C, D_MODEL), W_C ** -0.5),
    }


def reference(x_prompt, x_sample, cache_k, cache_v, state_gla, page_table, rel_bias, norm0_g, w_in0,
              q_norm_g, k_norm_g, lam, subln_g, ln_v_g, ln_v_b, spatial_w, spatial_b, w_out0,
              norm1_g, w_in1, w_gate, b_gate, gla_norm_g, w_out1):
    xp, xs = x_prompt, x_sample
    db = xs.shape[0]
    past_len = page_table.shape[1] * cache_k.shape[2]
    kp_l, vp_l, ks_l, vs_l, vb_l, sp_l, ss_l = [], [], [], [], [], [], []
    for layer in range(DEPTH):
        j = layer // 2
        if layer % 2 == 0:
            lam_init = 0.8 - 0.6 * math.exp(-0.3 * layer)
            wts = (norm0_g[j], w_in0[j], q_norm_g[j], k_norm_g[j], lam[j], subln_g[j],
                   ln_v_g[j], ln_v_b[j], spatial_w[j], spatial_b[j], w_out0[j])
            xp, k_new, v_new, _ = even_layer(xp, None, None, 0, rel_bias, lam_init, *wts)
            k_past = cache_k[j, page_table].reshape(db, past_len, H_A, 2, DH_A)
            v_past = cache_v[j, page_table].reshape(db, past_len, H_A, DV_A)
            xs, k_s, v_s, vb_s = even_layer(xs, k_past, v_past, past_len, rel_bias, lam_init, *wts)
            kp_l.append(k_new)
            vp_l.append(v_new)
            ks_l.append(k_s)
            vs_l.append(v_s)
            vb_l.append(vb_s)
        else:
            wts = (norm1_g[j], w_in1[j], w_gate[j], b_gate[j], gla_norm_g[j], w_out1[j])
            s0 = jnp.zeros((xp.shape[0], H_C, DK_C, DV_C), jnp.float32)
            xp, s_p = odd_layer(xp, s0, *wts)
            xs, s_s = odd_layer(xs, state_gla[j], *wts)
            sp_l.append(s_p.astype(state_gla.dtype))
            ss_l.append(s_s.astype(state_gla.dtype))
    k_prompt = jnp.stack(kp_l)
    v_prompt = jnp.stack(vp_l)
    k_sample = jnp.stack(ks_l)
    v_sample = jnp.stack(vs_l)
    vb_sample = jnp.stack(vb_l)
    gla_prompt = jnp.stack(sp_l)
    gla_sample = jnp.stack(ss_l)
    return (xp, xs, k_prompt, v_prompt, k_sample, v_sample, vb_sample, gla_prompt, gla_sample)
```

```python
import math
import os
from contextlib import ExitStack

import numpy as np
import concourse.bass as bass
import concourse.mybir as mybir
from concourse.bass_utils import run_bass_kernel_spmd

F32 = mybir.dt.float32
BF16 = mybir.dt.bfloat16
I32 = mybir.dt.int32
AF = mybir.ActivationFunctionType
ALU = mybir.AluOpType
AX = mybir.AxisListType

NCORES = 4
NST = 2
D = 1024
SEQ = 4096
NPT = SEQ // 128
NT = NPT + NST
NTOK = NT * 128
H_A = 8
IN0 = 7168
IN1 = 3088
EPS = 1e-6
LAM_INIT = 0.8 - 0.6 * math.exp(-0.3 * 0)
NPAGES = 16
SSEQ = 16
GOFF = 639
NG = 1152 + 128

COMPUTE = ("pe", "act", "dve", "pool")
NDMA_SEMS = 10


class Op:
    __slots__ = ("eng", "fn", "dma", "deps", "marked", "sem", "val", "prewait")

    def __init__(self, eng, fn, dma):
        self.eng = eng
        self.fn = fn
        self.dma = dma
        self.deps = None
        self.marked = False
        self.sem = None
        self.val = 0
        self.prewait = None


class Sched:
    def __init__(self, nc, stack):
        self.nc = nc
        self.ops = []
        self.res = {}
        self.stack = stack
        self.sem_eng = None
        self.dma_ring = {q: [stack.enter_context(nc.semaphore("d_%s%d" % (q, n))) for n in range(NDMA_SEMS)]
                         for q in ("sp", "act", "pool")}
        self.cnt = {e: 0 for e in COMPUTE}
        self.dcnt = {q: 0 for q in self.dma_ring}
        self.seen = {s: {} for s in ("pe", "act", "dve", "pool", "sp")}
        self.barrier = None
        self.nphase = 0

    EXCL = ("pT", "pz", "pqk", "pmx", "bpS", "bpO", "bpqk", "cptr", "cpz", "cpA", "cpOg", "cpD", "gps", "spA", "spO",
            "spT")

    def add(self, eng, fn, reads=(), writes=(), dma=False):
        ex = [k for k in reads if k[0] in self.EXCL]
        if ex:
            writes = list(writes) + [k for k in ex if k not in writes]
        i = len(self.ops)
        op = Op(eng, fn, dma)
        deps = {}
        res = self.res
        for k in reads:
            r = res.get(k)
            if r is not None and r[0] is not None:
                deps[r[0]] = True
        for k in writes:
            r = res.get(k)
            if r is not None:
                if r[0] is not None and r[0] not in deps:
                    deps[r[0]] = False
                for j in r[1]:
                    if j not in deps:
                        deps[j] = False
        for k in reads:
            r = res.get(k)
            if r is None:
                res[k] = [None, [i]]
            else:
                r[1].append(i)
        for k in writes:
            res[k] = [i, []]
        deps.pop(i, None)
        op.deps = deps
        self.ops.append(op)
        return i

    @staticmethod
    def _skip(p, op, raw):
        return (not p.dma) and p.eng == op.eng and (not op.dma) and (p.eng == "pe" or not raw)

    def emit(self, final=False):
        nc = self.nc
        ops = self.ops
        self.sem_eng = {e: self.stack.enter_context(nc.semaphore("s%d_%s" % (self.nphase, e))) for e in COMPUTE}
        self.cnt = {e: 0 for e in COMPUTE}
        streams = {"pe": [], "act": [], "dve": [], "pool": [], "sp": []}
        for op in ops:
            streams[op.eng].append(op)
            for j, raw in op.deps.items():
                p = ops[j]
                if p.dma or self._skip(p, op, raw):
                    continue
                p.marked = True
        for e in COMPUTE:
            for op in reversed(streams[e]):
                if not op.dma:
                    op.marked = True
                    break
        for op in ops:
            if op.dma:
                n = self.dcnt[op.eng]
                self.dcnt[op.eng] = n + 1
                op.sem = self.dma_ring[op.eng][n % NDMA_SEMS]
                op.val = 16 * (n // NDMA_SEMS + 1)
                if n >= NDMA_SEMS:
                    op.prewait = (op.sem, op.val - 16)
            elif op.marked:
                self.cnt[op.eng] += 1
                op.sem = self.sem_eng[op.eng]
                op.val = self.cnt[op.eng]
        barrier_in = self.barrier

        def run_stream(name, eng):
            seen = self.seen[name]
            if barrier_in:
                for s, v in barrier_in:
                    if v > 0 and seen.get(s, 0) < v:
                        eng.wait_ge(s, v)
                        seen[s] = v
            for op in streams[name]:
                waits = {}
                if op.prewait is not None:
                    waits[op.prewait[0]] = op.prewait[1]
                for j, raw in op.deps.items():
                    p = ops[j]
                    if self._skip(p, op, raw) or p.sem is None:
                        continue
                    if waits.get(p.sem, 0) < p.val:
                        waits[p.sem] = p.val
                for s, v in waits.items():
                    if seen.get(s, 0) < v:
                        eng.wait_ge(s, v)
                        seen[s] = v
                ins = op.fn(eng)
                if op.dma:
                    ins.then_inc(op.sem, 16)
                elif op.marked:
                    ins.then_inc(op.sem, 1)
            if final and name == "sp":
                for q, ring in self.dma_ring.items():
                    n = self.dcnt[q]
                    for r, s in enumerate(ring):
                        v = 16 * ((n - r + NDMA_SEMS - 1) // NDMA_SEMS) if n > r else 0
                        if v > 0 and seen.get(s, 0) < v:
                            eng.wait_ge(s, v)
                            seen[s] = v

        with nc.Block() as block:
            @block.sync
            def _(e):
                run_stream("sp", e)

            @block.tensor
            def _(e):
                run_stream("pe", e)

            @block.scalar
            def _(e):
                run_stream("act", e)

            @block.vector
            def _(e):
                run_stream("dve", e)

            @block.gpsimd
            def _(e):
                run_stream("pool", e)

        bar = [(self.sem_eng[e], self.cnt[e]) for e in COMPUTE]
        for q, ring in self.dma_ring.items():
            n = self.dcnt[q]
            for r, s in enumerate(ring):
                v = 16 * ((n - r + NDMA_SEMS - 1) // NDMA_SEMS) if n > r else 0
                bar.append((s, v))
        self.barrier = bar
        self.ops = []
        self.res = {}
        self.nphase += 1


class Ring:
    def __init__(self, tiles, name):
        self.tiles = tiles
        self.name = name
        self.i = 0

    def next(self):
        k = self.i % len(self.tiles)
        self.i += 1
        return self.tiles[k], (self.name, k)


def t5_bucket_np(n):
    n = np.maximum(n, 0)
    nf = np.maximum(n, 1).astype(np.float32)
    large = 16 + (np.log(nf / np.float32(16)) / np.float32(math.log(128 / 16)) * np.float32(16)).astype(np.int32)
    large = np.minimum(large, 31)
    return np.where(n < 16, n, large)


PAR = {}
_off = 0
for _n, _w in (("g0T", 8), ("g1T", 8), ("qg", 128), ("kg", 128), ("lng", 1024), ("lnb", 1024), ("subg", 128),
               ("spbT", 8), ("spbTs", 8), ("lam", 256), ("bgate", 512), ("glag", 1024), ("relb", 8),
               ("wgate", 512)):
    PAR[_n] = (_off, _w)
    _off += _w
NPAR = _off

CST = {}
_off = 0
for _n, _w in (("ident", 128), ("mtp", 128), ("mts", 128), ("Lp", 128), ("Ls", 128), ("Ap", 128), ("As", 128),
               ("selp", 2), ("sels", 16), ("ohs", 16), ("sel8", 8), ("c01", 1), ("oh1", NG), ("gvalid", NG), ("iota", 1)):
    CST[_n] = (_off, _w)
    _off += _w
NCST = _off


def host_consts():
    c = np.zeros((128, NCST), np.float32)

    def put(name, a):
        o, w = CST[name]
        c[:a.shape[0], o:o + w] = a

    s = np.arange(128)[:, None]
    t = np.arange(128)[None, :]
    put("ident", np.eye(128, dtype=np.float32))
    put("mtp", (s <= t).astype(np.float32))
    put("mts", ((s // 8 == t // 8) & (s <= t)).astype(np.float32))
    same64 = (s // 64 == t // 64) & (s <= t)
    same8 = (s // 8 == t // 8) & (s <= t)
    put("Lp", same64.astype(np.float32) * (-1.0 / 16.0))
    put("Ls", same8.astype(np.float32) * (-1.0 / 16.0))
    put("Ap", same64.astype(np.float32))
    put("As", same8.astype(np.float32))
    selp = np.zeros((128, 2), np.float32)
    selp[63, 0] = 1
    selp[127, 1] = 1
    put("selp", selp)
    sels = np.zeros((128, 16), np.float32)
    sels[np.arange(16) * 8 + 7, np.arange(16)] = 1
    put("sels", sels)
    put("ohs", (s // 8 == np.arange(16)[None, :]).astype(np.float32))
    n = GOFF - np.arange(NG)
    bk = t5_bucket_np(n)
    oh1 = (bk[None, :] == np.arange(32)[:, None]).astype(np.float32)
    oh1[31, :] -= 1.0
    put("oh1", oh1)
    put("gvalid", np.broadcast_to((n >= 0).astype(np.float32)[None, :], (128, NG)))
    put("iota", np.arange(128, dtype=np.float32)[:, None])
    sel8 = np.zeros((128, 8), np.float32)
    sel8[np.arange(16), np.arange(16) % 8] = 1
    put("sel8", sel8)
    c01 = np.zeros((128, 1), np.float32)
    c01[8:16] = 1
    put("c01", c01)
    return c


def host_params(inp):
    p = np.zeros((128, NPAR), np.float32)

    def put(name, a):
        o, w = PAR[name]
        a = np.asarray(a, np.float32)
        p[:a.shape[0], o:o + w] = a

    def bc(v):
        return np.broadcast_to(np.asarray(v, np.float32).reshape(1, -1), (128, np.asarray(v).size))

    put("g0T", inp["norm0_g"][0].reshape(8, 128).T)
    put("g1T", inp["norm1_g"][0].reshape(8, 128).T)
    put("qg", bc(inp["q_norm_g"][0].reshape(128)))
    put("kg", bc(inp["k_norm_g"][0].reshape(128)))
    put("lng", bc(inp["ln_v_g"][0]))
    put("lnb", bc(inp["ln_v_b"][0]))
    put("subg", bc(inp["subln_g"][0]))
    put("spbT", inp["spatial_b"][0].T)
    put("spbTs", np.tile(inp["spatial_b"][0][:, :8].T, (16, 1)))
    put("lam", bc(inp["lam"][0].reshape(-1)))
    put("bgate", bc(inp["b_gate"][0]))
    put("glag", bc(np.tile(inp["gla_norm_g"][0], 4)))
    put("relb", inp["rel_bias"])
    put("wgate", inp["w_gate"][0])
    return p


def build_nc(phases=("A", "B", "C"), PP=320):
    nc = bass.Bass("TRN2", target_bir_lowering=False)

    def din(name, shape, dt=F32):
        return nc.dram_tensor(name, list(shape), dt, kind="ExternalInput").ap()

    def dout(name, shape, dt=F32):
        return nc.dram_tensor(name, list(shape), dt, kind="ExternalOutput").ap()

    def dscr(name, shape, dt):
        return nc.dram_tensor(name, list(shape), dt, kind="Internal").ap()

    xin = din("xin", [NTOK, D])
    par = din("par", [128, NPAR])
    cst = din("cst", [128, NCST])
    w_in0 = din("w_in0", [D, IN0])
    w_out0 = din("w_out0", [2 * D, D])
    w_in1 = din("w_in1", [D, IN1])
    w_out1 = din("w_out1", [D, D])
    spwT = din("spwT", [128, 8, 128])
    spwTs = din("spwTs", [128, 8, 128])
    cache_k = din("cache_k", [PP * 128, 1024])
    cache_v = din("cache_v", [PP * 128, 1024])
    state = din("state", [NST * SSEQ, 4, 128, 256])
    ptab = din("ptab", [128, NST * SSEQ * NPAGES], I32)

    o_y = dout("o_y", [NTOK, D])
    o_k = dout("o_k", [NTOK, D])
    o_v = dout("o_v", [NTOK, D])
    o_vb = dout("o_vb", [NST * 128, D])
    o_gp = dout("o_gp", [4, 128, 256])
    o_gs = dout("o_gs", [NST * SSEQ, 4, 128, 256])

    QT = dscr("QT", [128, NT, 8, 128], BF16)
    KT = dscr("KT", [128, 8, NTOK], BF16)
    VS = dscr("VS", [8, 128, NT, 130], BF16)
    SGA = dscr("SGA", [NTOK, D], BF16)
    OBT = dscr("OBT", [128, NT, 8, 128], BF16)
    X1 = dscr("X1", [NTOK, D], F32)
    GV = dscr("GV", [8, NG], F32)

    with ExitStack() as top:
        top.enter_context(nc.allow_low_precision("bf16 matmul operands, fp32 accumulation"))
        top.enter_context(nc.allow_non_contiguous_dma("small strided scratch layouts"))
        S = Sched(nc, top)

        uid = [0]

        def sb(st, name, shape, dt):
            uid[0] += 1
            return st.enter_context(nc.sbuf_tensor("sb%d_%s" % (uid[0], name), list(shape), dt))

        def ps(st, name, shape, dt):
            uid[0] += 1
            return st.enter_context(nc.psum_tensor("ps%d_%s" % (uid[0], name), list(shape), dt))

        identb = sb(top, "identb", [128, 128], BF16)
        eps_t = sb(top, "eps_t", [128, 1], F32)
        neglam = sb(top, "neglam", [128, 1], F32)

        def ldpar(st, name, rows=128, q="sp"):
            o, w = PAR[name]
            t = sb(st, "par_" + name, [rows, w], F32)
            S.add(q, lambda e: e.dma_start(out=t[:], in_=par[0:rows, o:o + w]), writes=[("par", name)], dma=True)
            return t, ("par", name)

        def ldcst(st, name, rows=128, q="sp"):
            o, w = CST[name]
            t = sb(st, "cst_" + name, [rows, w], F32)
            S.add(q, lambda e: e.dma_start(out=t[:], in_=cst[0:rows, o:o + w]), writes=[("cst", name)], dma=True)
            return t, ("cst", name)

        with ExitStack() as st:
            ident, identk = ldcst(st, "ident")
            lamt, lamk = ldpar(st, "lam")
            relb, relbk = ldpar(st, "relb", 32)
            oh1, oh1k = ldcst(st, "oh1", 32)
            gval, gvalk = ldcst(st, "gvalid", 8)
            S.add("dve", lambda e: e.tensor_copy(out=identb[:], in_=ident[:]), reads=[identk], writes=[("identb",)])
            S.add("dve", lambda e: e.memset(eps_t[:], EPS), writes=[("eps",)])
            lt = sb(st, "lt", [128, 128], F32)
            ls = sb(st, "ls", [128, 2], F32)
            le = sb(st, "le", [128, 2], F32)
            lamv = lamt[:].rearrange("p (a b d) -> p a b d", a=2, b=2)
            S.add("dve", lambda e: e.tensor_tensor(out=lt[:].rearrange("p (a d) -> p a d", a=2), in0=lamv[:, :, 0, :],
                                                   in1=lamv[:, :, 1, :], op=ALU.mult), reads=[lamk], writes=[("lt",)])
            S.add("dve", lambda e: e.tensor_reduce(out=ls[:], in_=lt[:].rearrange("p (a d) -> p a d", a=2), axis=AX.X,
                                                   op=ALU.add), reads=[("lt",)], writes=[("ls",)])
            S.add("act", lambda e: e.activation(out=le[:], in_=ls[:], func=AF.Exp), reads=[("ls",)], writes=[("le",)])
            S.add("dve", lambda e: e.tensor_tensor(out=neglam[:], in0=le[:, 1:2], in1=le[:, 0:1], op=ALU.subtract),
                  reads=[("le",)], writes=[("neglam",)])
            S.add("dve", lambda e: e.tensor_scalar(out=neglam[:], in0=neglam[:], scalar1=-LAM_INIT, scalar2=None,
                                                   op0=ALU.add), reads=[("neglam",)], writes=[("neglam",)])
            gps = ps(st, "gps", [8, 3, 512], F32)
            gsb = sb(st, "gsb", [8, NG], F32)
            for i in range(3):
                w = min(512, NG - i * 512)
                S.add("pe", lambda e, i=i, w=w: e.matmul(gps[:, i, 0:w], lhsT=relb[:], rhs=oh1[:, i * 512:i * 512 + w],
                                                         start=True, stop=True),
                      reads=[relbk, oh1k], writes=[("gps", i)])
                S.add("act", lambda e, i=i, w=w: e.activation(out=gsb[:, i * 512:i * 512 + w], in_=gps[:, i, 0:w],
                                                              func=AF.Exp), reads=[("gps", i)], writes=[("gsb", i)])
            S.add("dve", lambda e: e.tensor_tensor(out=gsb[:], in0=gsb[:], in1=gval[:], op=ALU.mult),
                  reads=[("gsb", 0), ("gsb", 1), ("gsb", 2), gvalk], writes=[("gsb",)])
            S.add("sp", lambda e: e.dma_start(out=GV, in_=gsb[:]), reads=[("gsb",)], writes=[("GV",)], dma=True)
            S.emit()

        if "A" in phases:
            phase_a(nc, S, sb, ps, ldpar, ldcst, identb, eps_t, locals())
        if "B" in phases:
            phase_b(nc, S, sb, ps, ldpar, ldcst, identb, eps_t, neglam, locals(), write_y=("C" not in phases))
        if "S" in phases:
            phase_b2(nc, S, sb, ps, ldpar, ldcst, identb, eps_t, neglam, locals(), write_y=("C" not in phases))
        if "C" in phases:
            phase_c(nc, S, sb, ps, ldpar, ldcst, identb, eps_t, locals())
        S_final_dummy(nc, S, sb, locals())
    return nc


def S_final_dummy(nc, S, sb, g):
    with ExitStack() as st:
        t = sb(st, "fin", [128, 1], F32)
        S.add("dve", lambda e: e.memset(t[:], 0.0), writes=["fin"])
        S.emit(final=True)


def S_final_dummy(nc, S, sb, g):
    with ExitStack() as st:
        t = sb(st, "fin", [128, 1], F32)
        S.add("dve", lambda e: e.memset(t[:], 0.0), writes=[("fin",)])
        S.emit(final=True)


def rstd_from_ss(S, ss, sd, n, eps_t, kin, kout):
    S.add("act", lambda e: e.activation(out=sd, in_=ss, func=AF.Sqrt, bias=eps_t[:], scale=1.0 / n),
          reads=[kin, ("eps",)], writes=[kout + ("sd",)])
    S.add("dve", lambda e: e.reciprocal(out=sd, in_=sd), reads=[kout + ("sd",)], writes=[kout])


def phase_a(nc, S, sb, ps, ldpar, ldcst, identb, eps_t, g):
    xin, w_in0, spwT, spwTs = g["xin"], g["w_in0"], g["spwT"], g["spwTs"]
    o_k, o_v, o_vb = g["o_k"], g["o_v"], g["o_vb"]
    QT, KT, VS, SGA, OBT = g["QT"], g["KT"], g["VS"], g["SGA"], g["OBT"]
    with ExitStack() as st:
        W = sb(st, "W0", [128, 8, IN0], BF16)
        wspb = sb(st, "wspb", [128, 8, 128], BF16)
        wspsb = sb(st, "wspsb", [128, 8, 128], BF16)
        qgs = sb(st, "qgs", [128, 128], F32)
        for kc in range(8):
            for cb in range(4):
                S.add("pool", lambda e, kc=kc, cb=cb: e.dma_start(
                    out=W[:, kc, cb * 1792:(cb + 1) * 1792],
                    in_=w_in0[kc * 128:(kc + 1) * 128, cb * 1792:(cb + 1) * 1792]),
                    writes=[("W", kc, cb)], dma=True)
        wkeys = [("W", kc, cb) for kc in range(8) for cb in range(4)]

        def ring(name, n, shape, dt):
            return Ring([sb(st, "%s%d" % (name, i), shape, dt) for i in range(n)], name)

        xr = ring("x", 2, [128, D], F32)
        xsr = ring("xs", 2, [128, D], BF16)
        xtr = ring("xT", 2, [128, 8, 128], BF16)
        junk = sb(st, "junk", [128, D], BF16)
        stat = ring("stat", 8, [128, 64], F32)
        sqr = ring("sq", 2, [128, 512], F32)
        tmpr = ring("tmp", 2, [128, 512], F32)
        qbr = ring("qb", 2, [128, 1024], BF16)
        kbr = ring("kb", 2, [128, 1024], BF16)
        kfr = ring("kf", 2, [128, 512], F32)
        vfr = ring("vf", 2, [128, 512], F32)
        ver = ring("ve", 2, [128, 8, 130], BF16)
        qtr = ring("qt", 1, [128, 8, 128], BF16)
        ktr = ring("kt", 1, [128, 8, 128], BF16)
        sgr = ring("sg", 1, [128, 1024], BF16)
        ur = ring("u", 2, [128, 1024], BF16)
        gbr = ring("gb", 2, [128, 1024], BF16)
        gvr = ring("gv", 1, [128, 1024], F32)
        vbbr = ring("vbb", 2, [128, 1024], BF16)
        m1r = ring("m1", 1, [128, 1024], F32)
        obr = ring("ob", 2, [128, 1024], BF16)
        obtr = ring("obt", 1, [128, 8, 128], BF16)
        for i, r in enumerate(ver.tiles):
            S.add("pool", lambda e, r=r: e.memset(r[:, :, 128:130], 1.0), writes=[("ve", i, "ones")])

        g0T, g0Tk = ldpar(st, "g0T")
        qg_t, qgk = ldpar(st, "qg")
        kg_t, kgk = ldpar(st, "kg")
        lng_t, lngk = ldpar(st, "lng")
        lnb_t, lnbk = ldpar(st, "lnb")
        spbT_t, spbTk = ldpar(st, "spbT")
        spbTs_t, spbTsk = ldpar(st, "spbTs")
        mtp, mtpk = ldcst(st, "mtp")
        mts, mtsk = ldcst(st, "mts")
        lng, lnb = lng_t[:], lnb_t[:]
        gv0, gv0k = gvr.tiles[0], ("gv", 0)
        m10, m10k = m1r.tiles[0], ("m1", 0)
        S.add("sp", lambda e: e.dma_start(out=gv0[:].rearrange("p (g t) -> p g t", g=8), in_=spwT),
              writes=[gv0k + ("vb",)], dma=True)
        S.add("sp", lambda e: e.dma_start(out=m10[:].rearrange("p (g t) -> p g t", g=8), in_=spwTs),
              writes=[m10k + ("u",)], dma=True)
        S.add("dve", lambda e: e.tensor_tensor(out=wspb[:], in0=gv0[:].rearrange("p (g t) -> p g t", g=8),
                                               in1=mtp[:].unsqueeze(1).to_broadcast([128, 8, 128]), op=ALU.mult),
              reads=[gv0k + ("vb",), mtpk], writes=[("wspb",)])
        S.add("dve", lambda e: e.tensor_tensor(out=wspsb[:], in0=m10[:].rearrange("p (g t) -> p g t", g=8),
                                               in1=mts[:].unsqueeze(1).to_broadcast([128, 8, 128]), op=ALU.mult),
              reads=[m10k + ("u",), mtsk], writes=[("wspsb",)])
        S.add("dve", lambda e: e.tensor_scalar(out=qgs[:], in0=qg_t[:], scalar1=0.125, scalar2=None, op0=ALU.mult),
              reads=[qgk], writes=[("qgs",)])

        pT = ps(st, "pT", [128, 8, 128], BF16)
        pzr = Ring([ps(st, "pz%d" % i, [128, 512], F32) for i in range(4)], "pz")
        pqk = ps(st, "pqk", [128, 8, 128], BF16)
        pmx = ps(st, "pmx", [128, 1024], F32)
        P_K = g0Tk

        def load_x(ti):
            xt, xk = xr.next()
            S.add("sp", lambda e: e.dma_start(out=xt[:], in_=xin[ti * 128:(ti + 1) * 128, :]), writes=[xk], dma=True)
            return xt, xk

        xq = [load_x(0)]
        pending_ob = None

        def emit_ob_transposes(pend):
            ob, obk, ti = pend
            obt, obtk = obtr.next()
            for kc in range(8):
                S.add("pe", lambda e, kc=kc: e.transpose(out=pqk[:, kc, :], in_=ob[:, kc * 128:(kc + 1) * 128],
                                                         identity=identb[:]),
                      reads=[obk, ("identb",)], writes=[("pqk",)])
            S.add("dve", lambda e: e.tensor_copy(out=obt[:], in_=pqk[:]), reads=[("pqk",)], writes=[obtk])
            S.add("sp", lambda e: e.dma_start(out=OBT[:, ti, :, :], in_=obt[:]), reads=[obtk], writes=[("OBT", ti)],
                  dma=True)

        prepped = {}

        def prep(ti):
            if ti + 1 < NT:
                xq.append(load_x(ti + 1))
            xt, xk = xq.pop(0)
            sta, sk = stat.next()
            xs, xsk = xsr.next()
            xT, xTk = xtr.next()
            S.add("act", lambda e: e.activation(out=junk[:], in_=xt[:], func=AF.Square, accum_out=sta[:, 0:1]),
                  reads=[xk], writes=[sk + ("ss",), ("junk",)])
            rstd_from_ss(S, sta[:, 0:1], sta[:, 8:9], D, eps_t, sk + ("ss",), sk + ("r",))
            S.add("act", lambda e: e.activation(out=xs[:], in_=xt[:], func=AF.Copy, scale=sta[:, 8:9]),
                  reads=[xk, sk + ("r",)], writes=[xsk])
            for kc in range(8):
                S.add("pe", lambda e, kc=kc: e.transpose(out=pT[:, kc, :], in_=xs[:, kc * 128:(kc + 1) * 128],
                                                         identity=identb[:]),
                      reads=[xsk, ("identb",)], writes=[("pT",)])
            S.add("dve", lambda e: e.tensor_tensor(out=xT[:], in0=pT[:], in1=g0T[:].unsqueeze(2).to_broadcast([128, 8, 128]),
                                                   op=ALU.mult), reads=[("pT",), g0Tk], writes=[xTk])
            prepped[ti] = (xT, xTk)

        def tile_body(ti):
            nonlocal pending_ob
            sample = ti >= NPT
            xT, xTk = prepped.pop(ti)

            def proj(cb, xT=xT, xTk=xTk):
                pz, pzk = pzr.next()
                for kc in range(8):
                    S.add("pe", lambda e, kc=kc, pz=pz: e.matmul(pz[:], lhsT=xT[:, kc, :],
                                                                rhs=W[:, kc, cb * 512:(cb + 1) * 512],
                                                                start=(kc == 0), stop=(kc == 7)),
                          reads=[xTk] + (wkeys if ti == 0 else []), writes=[pzk])
                return pz, pzk

            def qk_norm(cbs, gain, gaink, is_k, out_bf, okb):
                for b, cb in enumerate(cbs):
                    pz, pzk = proj(cb)
                    sq, sqk = sqr.next()
                    tmp, tmpk = tmpr.next()
                    sta2, sk2 = stat.next()
                    S.add("act", lambda e, sq=sq, pz=pz: e.activation(out=sq[:], in_=pz[:], func=AF.Square),
                          reads=[pzk], writes=[sqk])
                    S.add("dve", lambda e, sq=sq, sta2=sta2: e.tensor_reduce(
                        out=sta2[:, 0:8], in_=sq[:].rearrange("p (g d) -> p g d", d=64), axis=AX.X, op=ALU.add),
                        reads=[sqk], writes=[sk2 + ("ss",)])
                    rstd_from_ss(S, sta2[:, 0:8], sta2[:, 8:16], 64, eps_t, sk2 + ("ss",), sk2 + ("r",))
                    S.add("dve", lambda e, tmp=tmp, pz=pz, sta2=sta2: e.tensor_tensor(
                        out=tmp[:].rearrange("p (g d) -> p g d", d=64), in0=pz[:].rearrange("p (g d) -> p g d", d=64),
                        in1=sta2[:, 8:16].unsqueeze(2).to_broadcast([128, 8, 64]), op=ALU.mult),
                        reads=[pzk, sk2 + ("r",)], writes=[tmpk])
                    cs = slice(b * 512, (b + 1) * 512)
                    gb_ = gain[:].unsqueeze(1).to_broadcast([128, 4, 128])
                    if is_k:
                        kf, kfk = kfr.next()
                        S.add("pool", lambda e, tmp=tmp, kf=kf, gb_=gb_: e.tensor_tensor(
                            out=kf[:].rearrange("p (h c) -> p h c", h=4), in0=tmp[:].rearrange("p (h c) -> p h c", h=4),
                            in1=gb_, op=ALU.mult), reads=[tmpk, gaink], writes=[kfk])
                        S.add("pool", lambda e, cs=cs, kf=kf: e.tensor_copy(out=out_bf[:, cs], in_=kf[:]),
                              reads=[kfk], writes=[okb + (b,)])
                        S.add("sp", lambda e, kf=kf, b=b: e.dma_start(
                            out=o_k[ti * 128:(ti + 1) * 128, b * 512:(b + 1) * 512], in_=kf[:]),
                            reads=[kfk], writes=[("o_k", ti, b)], dma=True)
                    else:
                        S.add("pool", lambda e, tmp=tmp, cs=cs, gb_=gb_: e.tensor_tensor(
                            out=out_bf[:, cs].rearrange("p (h c) -> p h c", h=4),
                            in0=tmp[:].rearrange("p (h c) -> p h c", h=4), in1=gb_, op=ALU.mult),
                            reads=[tmpk, gaink], writes=[okb + (b,)])

            def transposes_to(src, srck, dstring, dram_ap, dkey):
                dt_, dtk = dstring.next()
                for h in range(8):
                    S.add("pe", lambda e, h=h: e.transpose(out=pqk[:, h, :], in_=src[:, h * 128:(h + 1) * 128],
                                                           identity=identb[:]),
                          reads=[srck + (0,), srck + (1,), ("identb",)], writes=[("pqk",)])
                S.add("dve", lambda e: e.tensor_copy(out=dt_[:], in_=pqk[:]), reads=[("pqk",)], writes=[dtk])
                S.add("sp", lambda e: e.dma_start(out=dram_ap, in_=dt_[:]), reads=[dtk], writes=[dkey], dma=True)

            qb, qbk = qbr.next()
            qk_norm((0, 1), qgs, ("qgs",), False, qb, qbk)
            kb, kbk = kbr.next()
            qk_norm((2, 3), kg_t, kgk, True, kb, kbk)
            if pending_ob is not None:
                emit_ob_transposes(pending_ob)
                pending_ob = None
            gv, gvk = gvr.next()
            for b, cb in enumerate((10, 11)):
                pz, pzk = proj(cb)
                S.add("act", lambda e, pz=pz, b=b, gv=gv: e.activation(out=gv[:, b * 512:(b + 1) * 512], in_=pz[:],
                                                                      func=AF.Gelu),
                      reads=[pzk], writes=[gvk + (b,), gvk + ("n",), gvk + ("g",), gvk + ("vb",)] if b == 0 else [gvk + (b,)])
            sta3, sk3 = stat.next()
            for b in range(2):
                S.add("dve", lambda e, b=b, gv=gv, sta3=sta3: e.bn_stats(out=sta3[:, 32 + b * 6:32 + (b + 1) * 6],
                                                                        in_=gv[:, b * 512:(b + 1) * 512]),
                      reads=[gvk + (b,)], writes=[sk3 + ("bs", b)])
            S.add("dve", lambda e, sta3=sta3: e.bn_aggr(out=sta3[:, 48:50], in_=sta3[:, 32:44]),
                  reads=[sk3 + ("bs", 0), sk3 + ("bs", 1)], writes=[sk3 + ("mv",)])
            rstd_from_ss(S, sta3[:, 49:50], sta3[:, 50:51], 1, eps_t, sk3 + ("mv",), sk3 + ("lr",))
            S.add("dve", lambda e, gv=gv, sta3=sta3: e.tensor_scalar(out=gv[:], in0=gv[:], scalar1=sta3[:, 48:49],
                                                                    scalar2=sta3[:, 50:51], op0=ALU.subtract,
                                                                    op1=ALU.mult),
                  reads=[gvk + (0,), gvk + (1,), sk3 + ("lr",), sk3 + ("mv",)], writes=[gvk + ("n",)])
            S.add("pool", lambda e, gv=gv: e.tensor_tensor(out=gv[:], in0=gv[:], in1=lng, op=ALU.mult),
                  reads=[gvk + ("n",), lngk], writes=[gvk + ("g",)])
            S.add("pool", lambda e, gv=gv: e.tensor_tensor(out=gv[:], in0=gv[:], in1=lnb, op=ALU.add),
                  reads=[gvk + ("g",), lnbk], writes=[gvk + ("vb",)])
            vbb, vbbk = vbbr.next()
            S.add("pool", lambda e, gv=gv, vbb=vbb: e.tensor_copy(out=vbb[:], in_=gv[:]), reads=[gvk + ("vb",)],
                  writes=[vbbk])
            if sample:
                S.add("sp", lambda e, gv=gv: e.dma_start(out=o_vb[(ti - NPT) * 128:(ti - NPT + 1) * 128, :], in_=gv[:]),
                      reads=[gvk + ("vb",)], writes=[("o_vb", ti)], dma=True)
            u, uk = ur.next()
            for b, cb in enumerate((8, 9)):
                pz, pzk = proj(cb)
                S.add("act", lambda e, pz=pz, b=b, u=u: e.activation(out=u[:, b * 512:(b + 1) * 512], in_=pz[:],
                                                                    func=AF.Gelu),
                      reads=[pzk], writes=[uk + (b,)])
            transposes_to(qb, qbk, qtr, QT[:, ti, :, :], ("QT", ti))
            if ti + 1 < NT:
                prep(ti + 1)
            gbt, gbk = gbr.next()
            for b, cb in enumerate((12, 13)):
                pz, pzk = proj(cb)
                S.add("act", lambda e, pz=pz, b=b, gbt=gbt: e.activation(out=gbt[:, b * 512:(b + 1) * 512], in_=pz[:],
                                                                        func=AF.Silu),
                      reads=[pzk], writes=[gbk + (b,)])
            transposes_to(kb, kbk, ktr, KT[:, :, ti * 128:(ti + 1) * 128], ("KT", ti))
            ve, vek = ver.next()
            for b, cb in enumerate((4, 5)):
                pz, pzk = proj(cb)
                vf, vfk = vfr.next()
                S.add("act", lambda e, pz=pz, vf=vf: e.activation(out=vf[:], in_=pz[:], func=AF.Copy),
                      reads=[pzk], writes=[vfk])
                S.add("sp", lambda e, vf=vf, b=b: e.dma_start(out=o_v[ti * 128:(ti + 1) * 128, b * 512:(b + 1) * 512],
                                                             in_=vf[:]),
                      reads=[vfk], writes=[("o_v", ti, b)], dma=True)
                S.add("pool", lambda e, vf=vf, ve=ve, b=b: e.tensor_copy(
                    out=ve[:, b * 4:(b + 1) * 4, 0:128], in_=vf[:].rearrange("p (h e) -> p h e", h=4)),
                    reads=[vfk], writes=[vek + (b,)])
            S.add("sp", lambda e, ve=ve: e.dma_start(out=VS[:, :, ti, :].rearrange("h p e -> p h e"), in_=ve[:]),
                  reads=[vek + (0,), vek + (1,), vek + ("ones",)], writes=[("VS", ti)], dma=True)
            wm = wspsb if sample else wspb
            for gi in range(8):
                S.add("pe", lambda e, gi=gi, vbb=vbb: e.matmul(pmx[:, gi * 128:(gi + 1) * 128], lhsT=wm[:, gi, :],
                                                               rhs=vbb[:, gi * 128:(gi + 1) * 128], start=True,
                                                               stop=True),
                      reads=[vbbk, ("wspb",), ("wspsb",)], writes=[("pmx", gi // 4)])
            m1, m1k = m1r.next()
            spb = spbTs_t[:] if sample else spbT_t[:]
            S.add("dve", lambda e, m1=m1: e.tensor_tensor(out=m1[:].rearrange("p (g c) -> p g c", g=8),
                                                          in0=pmx[:].rearrange("p (g c) -> p g c", g=8),
                                                          in1=spb.unsqueeze(2).to_broadcast([128, 8, 128]), op=ALU.add),
                  reads=[("pmx", 0), ("pmx", 1), spbTk, spbTsk], writes=[m1k, m1k + ("u",)])
            S.add("pool", lambda e, m1=m1, u=u: e.tensor_tensor(out=m1[:], in0=m1[:], in1=u[:], op=ALU.mult),
                  reads=[m1k, uk + (0,), uk + (1,)], writes=[m1k + ("u",)])
            ob, obk = obr.next()
            S.add("pool", lambda e, m1=m1, gbt=gbt, ob=ob: e.tensor_tensor(out=ob[:], in0=m1[:], in1=gbt[:],
                                                                           op=ALU.mult),
                  reads=[m1k + ("u",), gbk + (0,), gbk + (1,)], writes=[obk])
            pending_ob = (ob, obk, ti)
            sg, sgk = sgr.next()
            for b, cb in enumerate((6, 7)):
                pz, pzk = proj(cb)
                S.add("act", lambda e, pz=pz, b=b, sg=sg: e.activation(out=sg[:, b * 512:(b + 1) * 512], in_=pz[:],
                                                                      func=AF.Silu),
                      reads=[pzk], writes=[sgk + (b,)])
            S.add("sp", lambda e, sg=sg: e.dma_start(out=SGA[ti * 128:(ti + 1) * 128, :], in_=sg[:]),
                  reads=[sgk + (0,), sgk + (1,)], writes=[("SGA", ti)], dma=True)

        prep(0)
        for ti_ in range(NT):
            tile_body(ti_)
        emit_ob_transposes(pending_ob)
        S.emit()


def dram_view(ap, offset, pattern):
    return bass.AP(tensor=ap.tensor, offset=offset, ap=[list(x) for x in pattern])


def phase_b(nc, S, sb, ps, ldpar, ldcst, identb, eps_t, neglam, g, write_y):
    xin, w_out0, o_y = g["xin"], g["w_out0"], g["o_y"]
    QT, KT, VS, SGA, OBT, X1, GV = g["QT"], g["KT"], g["VS"], g["SGA"], g["OBT"], g["X1"], g["GV"]
    NQT = NPT // 4
    with ExitStack() as st:
        Wo = sb(st, "Wo", [128, 16, D], BF16)
        for kc in range(16):
            S.add("pool", lambda e, kc=kc: e.dma_start(out=Wo[:, kc, :], in_=w_out0[kc * 128:(kc + 1) * 128, :]),
                  writes=[("Wo", kc)], dma=True)
        wokeys = [("Wo", kc) for kc in range(16)]
        M = sb(st, "M", [128, 8, 1024], BF16)
        mstage = sb(st, "mstage", [128, 1024], F32)
        for h in range(8):
            S.add("sp", lambda e, h=h: e.dma_start(out=mstage[:], in_=dram_view(GV, h * NG + 1023, [[1, 128], [-1, 1024]])),
                  writes=[("mstage",)], dma=True)
            S.add("dve", lambda e, h=h: e.tensor_copy(out=M[:, h, :], in_=mstage[:]), reads=[("mstage",)],
                  writes=[("M", h)])
        subg, subgk = ldpar(st, "subg")
        S.add("dve", lambda e: e.tensor_scalar(out=subg[:], in0=subg[:], scalar1=1.0 - LAM_INIT, scalar2=None,
                                               op0=ALU.mult), reads=[subgk], writes=[subgk])

        def ring(name, n, shape, dt):
            return Ring([sb(st, "%s%d" % (name, i), shape, dt) for i in range(n)], name)

        qTr = ring("bqT", 2, [128, 4, 8, 128], BF16)
        sgar = ring("bsga", 1, [128, 4, D], BF16)
        kTr = ring("bkT", 2, [128, SEQ], BF16)
        vEr = ring("bvE", 2, [128, NPT, 130], BF16)
        Ptr = ring("bPt", 3, [128, 2, 512], BF16)
        osbr = ring("bosb", 2, [128, 3, 512], F32)
        rlr = ring("brl", 2, [128, 16], F32)
        Ar = ring("bA", 4, [128, 128], F32)
        A2r = ring("bA2", 4, [128, 128], F32)
        str_ = ring("bst", 8, [128, 4], F32)
        junk = sb(st, "bjunk", [128, 128], F32)
        oar = ring("boa", 1, [128, 4, D], BF16)
        oaTr = ring("boaT", 2, [128, 8, 128], BF16)
        obTr = ring("bobT", 2, [128, 8, 128], BF16)
        xr = ring("bx", 2, [128, D], F32)
        x1r = ring("bx1", 2, [128, 512], F32)

        pS2 = Ring([ps(st, "bpS%d" % i, [128, 2, 512], F32) for i in range(2)], "bpS")
        pO = ps(st, "bpO", [128, 3, 512], F32)

        class _Half:
            def __init__(self):
                self.cur = None
                self.h = 2

            def next(self):
                if self.h == 2:
                    self.cur = pS2.next()
                    self.h = 0
                t_, k_ = self.cur
                r = (t_[:, self.h, :], k_)
                self.h += 1
                return r
        pS = _Half()
        pqk = ps(st, "bpqk", [128, 8, 128], BF16)

        def acc(a):
            return pO[:, a // 3, (a % 3) * 129:(a % 3) * 129 + 129]

        def unit(t, h, qT, qTk, sga, sgak, oa, oak):
            nkb = 4 * t + 4
            kT, kTk = kTr.next()
            vE, vEk = vEr.next()
            S.add("sp", lambda e: e.dma_start(out=kT[:, 0:nkb * 128], in_=KT[:, h, 0:nkb * 128]),
                  reads=[("KTs",)], writes=[kTk], dma=True)
            S.add("sp", lambda e: e.dma_start(out=vE[:, 0:nkb, :], in_=VS[h, :, 0:nkb, :]),
                  reads=[("VSs",)], writes=[vEk], dma=True)
            def qk(j):
                near = j >= 4 * t - 1
                pz, pzk = pS2.next()
                for c in range(2):
                    S.add("pe", lambda e, c=c, pz=pz, j=j: e.matmul(
                        pz[:, c, :], lhsT=kT[c * 64:(c + 1) * 64, j * 128:(j + 1) * 128],
                        rhs=qT[c * 64:(c + 1) * 64, :, h, :], start=True, stop=True),
                        reads=[kTk, qTk], writes=[pzk])
                Pt, Ptk = Ptr.next()
                S.add("act", lambda e, pz=pz, Pt=Pt: e.activation(out=Pt[:], in_=pz[:], func=AF.Exp),
                      reads=[pzk], writes=[Ptk + (0,), Ptk + (1,)])
                if near:
                    u0 = 512 * t - 128 * j + 384
                    S.add("dve", lambda e, Pt=Pt, u0=u0: e.tensor_tensor(
                        out=Pt[:], in0=Pt[:], in1=M[:, h, u0:u0 + 512].unsqueeze(1).to_broadcast([128, 2, 512]),
                        op=ALU.mult), reads=[Ptk + (0,), Ptk + (1,), ("M", h)], writes=[Ptk + (0,), Ptk + (1,)])
                return Pt, Ptk

            def av(j, Pt, Ptk):
                for c in range(2):
                    for s_ in range(4):
                        if j > 4 * t + s_:
                            continue
                        a = c * 4 + s_
                        S.add("pe", lambda e, c=c, s_=s_, a=a, Pt=Pt, j=j: e.matmul(
                            acc(a), lhsT=Pt[:, c, s_ * 128:(s_ + 1) * 128], rhs=vE[:, j, 0:129],
                            start=(j == 0 and a % 3 == 0), stop=(j == 4 * t + s_), skip_group_check=True),
                            reads=[Ptk + (c,), vEk], writes=[("bpO", a // 3)])

            pend = qk(0)
            for j in range(nkb):
                nxt = qk(j + 1) if j + 1 < nkb else None
                av(j, *pend)
                pend = nxt
            osb, osbk = osbr.next()
            rl, rlk = rlr.next()
            for b in range(3):
                eng = "dve" if b != 1 else "pool_na"
                w = 387 if b < 2 else 258
                S.add("dve", lambda e, b=b, w=w, osb=osb: e.tensor_copy(out=osb[:, b, 0:w], in_=pO[:, b, 0:w]),
                      reads=[("bpO", b)], writes=[osbk + (b,)])
                n = 3 if b < 2 else 2
                S.add("dve", lambda e, b=b, n=n, osb=osb, rl=rl: e.reciprocal(
                    out=rl[:, b * 3:b * 3 + n],
                    in_=osb[:, b, 0:n * 129].rearrange("p (a e) -> p a e", e=129)[:, :, 128]),
                    reads=[osbk + (b,)], writes=[rlk + (b,)])
            S.add("dve", lambda e, rl=rl: e.tensor_scalar(out=rl[:, 8:12], in0=rl[:, 4:8], scalar1=neglam[:, 0:1],
                                                         scalar2=None, op0=ALU.mult),
                  reads=[rlk + (1,), rlk + (2,), ("neglam",)], writes=[rlk + ("n",)])

            def oview(a, osb=osb):
                return osb[:, a // 3, (a % 3) * 129:(a % 3) * 129 + 128]

            for s_ in range(4):
                A, Ak = Ar.next()
                A2, A2k = A2r.next()
                sta, stk = str_.next()
                a1, a2 = s_, 4 + s_
                S.add("dve", lambda e, A=A, a1=a1, s_=s_, rl=rl: e.tensor_scalar(
                    out=A[:], in0=oview(a1), scalar1=rl[:, a1:a1 + 1], scalar2=None, op0=ALU.mult),
                    reads=[osbk + (a1 // 3,), rlk + (a1 // 3,)], writes=[Ak])
                S.add("dve", lambda e, A=A, A2=A2, a2=a2, s_=s_, rl=rl: e.scalar_tensor_tensor(
                    out=A2[:], in0=oview(a2), scalar=rl[:, 8 + s_:9 + s_], in1=A[:], op0=ALU.mult, op1=ALU.add),
                    reads=[osbk + (a2 // 3,), rlk + ("n",), Ak], writes=[A2k, A2k + ("n",)])
                S.add("act", lambda e, A2=A2, sta=sta: e.activation(out=junk[:], in_=A2[:], func=AF.Square,
                                                                    accum_out=sta[:, 0:1]),
                      reads=[A2k], writes=[stk + ("ss",), ("bjunk",)])
                rstd_from_ss(S, sta[:, 0:1], sta[:, 1:2], 128, eps_t, stk + ("ss",), stk + ("r",))
                S.add("dve", lambda e, A2=A2, sta=sta: e.scalar_tensor_tensor(
                    out=A2[:], in0=A2[:], scalar=sta[:, 1:2], in1=subg[:], op0=ALU.mult, op1=ALU.mult),
                    reads=[A2k, stk + ("r",), subgk], writes=[A2k + ("n",)])
                S.add("pool", lambda e, A2=A2, s_=s_: e.tensor_tensor(
                    out=oa[:, s_, h * 128:(h + 1) * 128], in0=A2[:], in1=sga[:, s_, h * 128:(h + 1) * 128],
                    op=ALU.mult), reads=[A2k + ("n",), sgak], writes=[oak + (s_, h)])

        def out_proj(t, oa, oak):
            for s_ in range(4):
                ti = 4 * t + s_
                oaT, oaTk = oaTr.next()
                obT, obTk = obTr.next()
                xt, xk = xr.next()
                S.add("sp", lambda e, ti=ti, obT=obT: e.dma_start(out=obT[:], in_=OBT[:, ti, :, :]),
                      reads=[("OBTs",)], writes=[obTk], dma=True)
                S.add("sp", lambda e, ti=ti, xt=xt: e.dma_start(out=xt[:], in_=xin[ti * 128:(ti + 1) * 128, :]),
                      writes=[xk], dma=True)
                for h in range(8):
                    S.add("pe", lambda e, h=h, s_=s_: e.transpose(out=pqk[:, h, :], in_=oa[:, s_, h * 128:(h + 1) * 128],
                                                                  identity=identb[:]),
                          reads=[oak + (s_, h), ("identb",)], writes=[("bpqk",)])
                S.add("dve", lambda e, oaT=oaT: e.tensor_copy(out=oaT[:], in_=pqk[:]), reads=[("bpqk",)],
                      writes=[oaTk])
                for nb in range(2):
                    pz, pzk = pS.next()
                    for kc in range(16):
                        lt = oaT[:, kc, :] if kc < 8 else obT[:, kc - 8, :]
                        S.add("pe", lambda e, kc=kc, lt=lt, pz=pz, nb=nb: e.matmul(
                            pz[:], lhsT=lt, rhs=Wo[:, kc, nb * 512:(nb + 1) * 512], start=(kc == 0), stop=(kc == 15)),
                            reads=[oaTk, obTk] + (wokeys if (t == 0 and s_ == 0) else []), writes=[pzk])
                    x1, x1k = x1r.next()
                    S.add("dve", lambda e, pz=pz, nb=nb, xt=xt, x1=x1: e.tensor_tensor(
                        out=x1[:], in0=pz[:], in1=xt[:, nb * 512:(nb + 1) * 512], op=ALU.add),
                        reads=[pzk, xk], writes=[x1k])
                    S.add("sp", lambda e, x1=x1, ti=ti, nb=nb: e.dma_start(
                        out=X1[ti * 128:(ti + 1) * 128, nb * 512:(nb + 1) * 512], in_=x1[:]),
                        reads=[x1k], writes=[("X1", ti, nb)], dma=True)
                    if write_y:
                        S.add("sp", lambda e, x1=x1, ti=ti, nb=nb: e.dma_start(
                            out=o_y[ti * 128:(ti + 1) * 128, nb * 512:(nb + 1) * 512], in_=x1[:]),
                            reads=[x1k], writes=[("o_y", ti, nb)], dma=True)

        for t in range(NQT):
            qT, qTk = qTr.next()
            sga, sgak = sgar.next()
            oa, oak = oar.next()
            S.add("sp", lambda e, t=t, qT=qT: e.dma_start(out=qT[:], in_=QT[:, 4 * t:4 * t + 4, :, :]),
                  writes=[qTk], dma=True)
            S.add("sp", lambda e, t=t, sga=sga: e.dma_start(
                out=sga[:], in_=SGA[t * 512:(t + 1) * 512, :].rearrange("(s p) f -> p s f", p=128)),
                writes=[sgak], dma=True)
            for h in range(8):
                unit(t, h, qT, qTk, sga, sgak, oa, oak)
            out_proj(t, oa, oak)
        S.emit()


def phase_b2(nc, S, sb, ps, ldpar, ldcst, identb, eps_t, neglam, g, write_y):
    xin, w_out0, o_y, cache_k, cache_v, ptab = g["xin"], g["w_out0"], g["o_y"], g["cache_k"], g["cache_v"], g["ptab"]
    QT, KT, VS, SGA, OBT, X1, GV = g["QT"], g["KT"], g["VS"], g["SGA"], g["OBT"], g["X1"], g["GV"]
    NSEQ = NST * SSEQ
    with ExitStack() as st:
        Wo = sb(st, "sWo", [128, 16, D], BF16)
        for kc in range(16):
            S.add("pool", lambda e, kc=kc: e.dma_start(out=Wo[:, kc, :], in_=w_out0[kc * 128:(kc + 1) * 128, :]),
                  writes=[("Wo", kc)], dma=True)
        wokeys = [("Wo", kc) for kc in range(16)]
        subg, subgk = ldpar(st, "subg", 8)
        S.add("dve", lambda e: e.tensor_scalar(out=subg[:], in0=subg[:], scalar1=1.0 - LAM_INIT, scalar2=None,
                                               op0=ALU.mult), reads=[subgk], writes=[subgk])
        iot, iotk = ldcst(st, "iota")
        sel8, sel8k = ldcst(st, "sel8", 16)
        c01, c01k = ldcst(st, "c01", 16)
        SMn = sb(st, "SMn", [128, 8, 8], F32)
        SN = sb(st, "SN", [8, 8, 8], F32)
        for h in range(8):
            S.add("sp", lambda e, h=h: e.dma_start(out=SMn[:, h, :],
                                                  in_=dram_view(GV, h * NG + GOFF - 128, [[1, 128], [-1, 8]])),
                  writes=[("SMn", h)], dma=True)
            S.add("sp", lambda e, h=h: e.dma_start(out=SN[:, h, :],
                                                  in_=dram_view(GV, h * NG + GOFF, [[1, 8], [-1, 8]])),
                  writes=[("SN", h)], dma=True)
        mkeys = [("SMn", h) for h in range(8)] + [("SN", h) for h in range(8)]
        cw = sb(st, "cw", [16, 1], F32)
        S.add("dve", lambda e: e.tensor_scalar(out=cw[:], in0=neglam[0:16, :], scalar1=-1.0, scalar2=None, op0=ALU.add),
              reads=[("neglam",)], writes=[("cw",)])
        S.add("dve", lambda e: e.tensor_tensor(out=cw[:], in0=cw[:], in1=c01[:], op=ALU.mult), reads=[("cw",), c01k],
              writes=[("cw",)])
        S.add("dve", lambda e: e.tensor_scalar(out=cw[:], in0=cw[:], scalar1=1.0, scalar2=None, op0=ALU.add),
              reads=[("cw",)], writes=[("cw",)])
        pt_i = sb(st, "pt_i", [128, NSEQ * NPAGES], I32)
        idx = sb(st, "idx", [128, NSEQ * NPAGES], I32)
        S.add("sp", lambda e: e.dma_start(out=pt_i[:], in_=ptab), writes=[("pt_i",)], dma=True)
        S.add("dve", lambda e: e.tensor_scalar(out=idx[:], in0=pt_i[:], scalar1=128.0, scalar2=iot[:, 0:1], op0=ALU.mult,
                                               op1=ALU.add), reads=[("pt_i",), iotk], writes=[("idx",)])

        def ring(name, n, shape, dt):
            return Ring([sb(st, "%s%d" % (name, i), shape, dt) for i in range(n)], name)

        kpfr = ring("skpf", 2, [128, 1024], F32)
        vpfr = ring("svpf", 2, [128, 1024], F32)
        kpgr = ring("skpg", 3, [128, 1024], BF16)
        vpgr = ring("svpg", 3, [128, 8, 130], BF16)
        for i, r in enumerate(vpgr.tiles):
            S.add("pool", lambda e, r=r: e.memset(r[:, :, 128:130], 1.0), writes=[("svpg", i, "ones")])
        kTpr = ring("skTp", 2, [128, 8, 128], BF16)
        Ppr = ring("sPp", 3, [128, 128], BF16)
        Qblk = sb(st, "sQblk", [128, SSEQ, 8, 16], BF16)
        qTs = sb(st, "sqTs", [128, 8, 128], BF16)
        kTs = sb(st, "skTs", [128, 8, 128], BF16)
        vnr = ring("svn", 2, [8, 8, 130], BF16)
        sgsr = ring("ssg", 2, [8, D], BF16)
        Pnr = ring("sPn", 2, [8, 128], BF16)
        osr = ring("sos", 2, [16, 8, 129], F32)
        rlr = ring("srl", 2, [16, 8], F32)
        os2r = ring("sos2", 2, [16, 8, 128], F32)
        ar = ring("sa", 2, [8, D], F32)
        sqr = ring("ssq", 1, [8, D], F32)
        str_ = ring("sst", 4, [8, 16], F32)
        oar = ring("soa", 2, [8, D], BF16)
        oaTr = ring("soaT", 1, [128, 8, 128], BF16)
        obTr = ring("sobT", 1, [128, 8, 128], BF16)
        xr = ring("sx", 1, [128, D], F32)
        x1r = ring("sx1", 2, [128, 512], F32)

        pS = Ring([ps(st, "spS%d" % i, [128, 512], F32) for i in range(4)], "spA")
        pO = ps(st, "spO", [128, 3, 512], F32)
        pqk = ps(st, "spT", [128, 8, 128], BF16)

        def acc(a):
            return pO[0:16, a // 3, (a % 3) * 129:(a % 3) * 129 + 129]

        for stl in range(NST):
            ti = NPT + stl
            S.add("sp", lambda e, ti=ti: e.dma_start(out=qTs[:], in_=QT[:, ti, :, :]), writes=[("qTs",)], dma=True)
            S.add("sp", lambda e, ti=ti: e.dma_start(out=kTs[:], in_=KT[:, :, ti * 128:(ti + 1) * 128]),
                  writes=[("kTs",)], dma=True)
            S.add("pool", lambda e: e.memset(Qblk[:], 0.0), writes=[("Qblk",)])
            for c in range(2):
                for b in range(SSEQ):
                    S.add("pool", lambda e, c=c, b=b: e.tensor_copy(
                        out=Qblk[c * 64:(c + 1) * 64, b, :, c * 8:(c + 1) * 8],
                        in_=qTs[c * 64:(c + 1) * 64, :, b * 8:(b + 1) * 8]), reads=[("qTs",), ("Qblk",)],
                        writes=[("Qblk",)])
            oaT, oaTk = oaTr.next()
            for b in range(SSEQ):
                sq_ = stl * SSEQ + b
                vn, vnk = vnr.next()
                sgs, sgsk = sgsr.next()
                S.add("sp", lambda e, ti=ti, b=b, vn=vn: e.dma_start(
                    out=vn[:], in_=VS[:, b * 8:(b + 1) * 8, ti, :].rearrange("h p e -> p h e")), writes=[vnk], dma=True)
                S.add("sp", lambda e, ti=ti, b=b, sgs=sgs: e.dma_start(
                    out=sgs[:], in_=SGA[ti * 128 + b * 8:ti * 128 + b * 8 + 8, :]), writes=[sgsk], dma=True)
                pz = None
                for j in range(NPAGES):
                    n = sq_ * NPAGES + j
                    kpg, kpgk = kpgr.next()
                    vpg, vpgk = vpgr.next()
                    kpf, kpfk = kpfr.next()
                    vpf, vpfk = vpfr.next()
                    S.add("pool", lambda e, n=n, kpf=kpf: e.indirect_dma_start(
                        out=kpf[:], out_offset=None, in_=cache_k,
                        in_offset=bass.IndirectOffsetOnAxis(ap=idx[:, n:n + 1], axis=0)),
                        reads=[("idx",)], writes=[kpfk], dma=True)
                    S.add("pool", lambda e, n=n, vpf=vpf: e.indirect_dma_start(
                        out=vpf[:], out_offset=None, in_=cache_v,
                        in_offset=bass.IndirectOffsetOnAxis(ap=idx[:, n:n + 1], axis=0)),
                        reads=[("idx",)], writes=[vpfk], dma=True)
                    S.add("act", lambda e, kpf=kpf, kpg=kpg: e.activation(out=kpg[:], in_=kpf[:], func=AF.Copy),
                          reads=[kpfk], writes=[kpgk])
                    S.add("dve", lambda e, vpf=vpf, vpg=vpg: e.tensor_copy(
                        out=vpg[:, :, 0:128], in_=vpf[:].rearrange("p (h e) -> p h e", h=8)), reads=[vpfk],
                        writes=[vpgk])
                    kTp, kTpk = kTpr.next()
                    for h in range(8):
                        S.add("pe", lambda e, h=h, kpg=kpg: e.transpose(out=pqk[:, h, :], in_=kpg[:, h * 128:(h + 1) * 128],
                                                                        identity=identb[:]),
                              reads=[kpgk, ("identb",)], writes=[("spT",)])
                    S.add("dve" if j % 2 == 0 else "act",
                          (lambda e, kTp=kTp: e.tensor_copy(out=kTp[:], in_=pqk[:])) if j % 2 == 0 else
                          (lambda e, kTp=kTp: e.activation(out=kTp[:], in_=pqk[:], func=AF.Copy)),
                          reads=[("spT",)], writes=[kTpk])
                    if j % 4 == 0:
                        pz, pzk = pS.next()
                    for h in range(8):
                        S.add("pe", lambda e, h=h, j=j, b=b, kTp=kTp, pz=pz: e.matmul(
                            pz[:, (j % 4) * 128 + h * 16:(j % 4) * 128 + (h + 1) * 16], lhsT=kTp[:, h, :],
                            rhs=Qblk[:, b, h, :], start=True, stop=True, skip_group_check=True),
                            reads=[kTpk, ("Qblk",)], writes=[pzk])
                    Pp, Ppk = Ppr.next()
                    S.add("act", lambda e, j=j, pz=pz, Pp=Pp: e.activation(out=Pp[:], in_=pz[:, (j % 4) * 128:(j % 4 + 1) * 128],
                                                                          func=AF.Exp), reads=[pzk], writes=[Ppk])
                    if j == NPAGES - 1:
                        S.add("dve", lambda e, Pp=Pp: e.tensor_tensor(
                            out=Pp[:].rearrange("p (h c i) -> p h c i", h=8, c=2),
                            in0=Pp[:].rearrange("p (h c i) -> p h c i", h=8, c=2),
                            in1=SMn[:].unsqueeze(2).to_broadcast([128, 8, 2, 8]), op=ALU.mult),
                            reads=[Ppk] + mkeys, writes=[Ppk])
                    for h in range(8):
                        S.add("pe", lambda e, h=h, j=j, Pp=Pp, vpg=vpg: e.matmul(
                            acc(h), lhsT=Pp[:, h * 16:(h + 1) * 16], rhs=vpg[:, h, 0:129],
                            start=(j == 0 and h % 3 == 0), stop=False, skip_group_check=True),
                            reads=[Ppk, vpgk, vpgk + ("ones",)], writes=[("spO", h // 3)])
                pz, pzk = pS.next()
                for h in range(8):
                    S.add("pe", lambda e, h=h, b=b, pz=pz: e.matmul(pz[0:8, h * 16:(h + 1) * 16],
                                                                   lhsT=kTs[:, h, b * 8:(b + 1) * 8], rhs=Qblk[:, b, h, :],
                                                                   start=True, stop=True, skip_group_check=True),
                          reads=[("kTs",), ("Qblk",)], writes=[pzk])
                Pn, Pnk = Pnr.next()
                S.add("act", lambda e, pz=pz, Pn=Pn: e.activation(out=Pn[:], in_=pz[0:8, 0:128], func=AF.Exp),
                      reads=[pzk], writes=[Pnk])
                S.add("dve", lambda e, Pn=Pn: e.tensor_tensor(
                    out=Pn[:].rearrange("p (h c i) -> p h c i", h=8, c=2),
                    in0=Pn[:].rearrange("p (h c i) -> p h c i", h=8, c=2),
                    in1=SN[:].unsqueeze(2).to_broadcast([8, 8, 2, 8]), op=ALU.mult),
                      reads=[Pnk] + mkeys, writes=[Pnk])
                for h in range(8):
                    S.add("pe", lambda e, h=h, Pn=Pn, vn=vn: e.matmul(acc(h), lhsT=Pn[:, h * 16:(h + 1) * 16],
                                                                      rhs=vn[:, h, 0:129], start=False, stop=True,
                                                                      skip_group_check=True),
                          reads=[Pnk, vnk], writes=[("spO", h // 3)])
                osb, osbk = osr.next()
                rl, rlk = rlr.next()
                for bk in range(3):
                    n_ = 3 if bk < 2 else 2
                    S.add("dve", lambda e, bk=bk, n_=n_, osb=osb: e.tensor_copy(
                        out=osb[:, bk * 3:bk * 3 + n_, :],
                        in_=pO[0:16, bk, 0:n_ * 129].rearrange("p (a e) -> p a e", e=129)),
                        reads=[("spO", bk)], writes=[osbk + (bk,)])
                S.add("dve", lambda e, osb=osb, rl=rl: e.reciprocal(out=rl[:], in_=osb[:, :, 128]),
                      reads=[osbk + (0,), osbk + (1,), osbk + (2,)], writes=[rlk])
                S.add("dve", lambda e, rl=rl: e.tensor_scalar(out=rl[:], in0=rl[:], scalar1=cw[:, 0:1], scalar2=None,
                                                              op0=ALU.mult), reads=[rlk, ("cw",)], writes=[rlk + ("w",)])
                os2, os2k = os2r.next()
                S.add("dve", lambda e, osb=osb, rl=rl, os2=os2: e.tensor_tensor(
                    out=os2[:], in0=osb[:, :, 0:128], in1=rl[:].unsqueeze(2).to_broadcast([16, 8, 128]), op=ALU.mult),
                    reads=[osbk + (0,), osbk + (1,), osbk + (2,), rlk + ("w",)], writes=[os2k])
                a_, ak = ar.next()
                for nb in range(2):
                    pa, pak = pS.next()
                    S.add("pe", lambda e, nb=nb, pa=pa, os2=os2: e.matmul(
                        pa[0:8, :], lhsT=sel8[:], rhs=os2[:].rearrange("p h e -> p (h e)")[:, nb * 512:(nb + 1) * 512],
                        start=True, stop=True), reads=[os2k, sel8k], writes=[pak])
                    S.add("dve", lambda e, nb=nb, pa=pa, a_=a_: e.tensor_copy(out=a_[:, nb * 512:(nb + 1) * 512],
                                                                            in_=pa[0:8, :]),
                          reads=[pak], writes=[ak + (nb,)])
                sq, sqk = sqr.next()
                sta, stk = str_.next()
                S.add("act", lambda e, a_=a_, sq=sq: e.activation(out=sq[:], in_=a_[:], func=AF.Square),
                      reads=[ak + (0,), ak + (1,)], writes=[sqk])
                S.add("dve", lambda e, sq=sq, sta=sta: e.tensor_reduce(out=sta[:, 0:8],
                                                                       in_=sq[:].rearrange("p (h e) -> p h e", h=8),
                                                                       axis=AX.X, op=ALU.add), reads=[sqk],
                      writes=[stk + ("ss",)])
                S.add("act", lambda e, sta=sta: e.activation(out=sta[:, 8:16], in_=sta[:, 0:8], func=AF.Sqrt,
                                                             bias=eps_t[0:8, :], scale=1.0 / 128),
                      reads=[stk + ("ss",), ("eps",)], writes=[stk + ("sd",)])
                S.add("dve", lambda e, sta=sta: e.reciprocal(out=sta[:, 8:16], in_=sta[:, 8:16]), reads=[stk + ("sd",)],
                      writes=[stk + ("r",)])
                S.add("dve", lambda e, a_=a_, sta=sta: e.tensor_tensor(
                    out=a_[:].rearrange("p (h e) -> p h e", h=8), in0=a_[:].rearrange("p (h e) -> p h e", h=8),
                    in1=sta[:, 8:16].unsqueeze(2).to_broadcast([8, 8, 128]), op=ALU.mult),
                    reads=[ak + (0,), ak + (1,), stk + ("r",)], writes=[ak + ("n",)])
                S.add("pool", lambda e, a_=a_: e.tensor_tensor(
                    out=a_[:].rearrange("p (h e) -> p h e", h=8), in0=a_[:].rearrange("p (h e) -> p h e", h=8),
                    in1=subg[:].unsqueeze(1).to_broadcast([8, 8, 128]), op=ALU.mult),
                    reads=[ak + ("n",), subgk], writes=[ak + ("g",)])
                oa, oak = oar.next()
                S.add("pool", lambda e, a_=a_, sgs=sgs, oa=oa: e.tensor_tensor(out=oa[:], in0=a_[:], in1=sgs[:],
                                                                              op=ALU.mult),
                      reads=[ak + ("g",), sgsk], writes=[oak])
                for h in range(8):
                    S.add("pe", lambda e, h=h, oa=oa: e.transpose(out=pqk[:, h, 0:8], in_=oa[:, h * 128:(h + 1) * 128],
                                                                  identity=identb[0:8, 0:8]),
                          reads=[oak, ("identb",)], writes=[("spT",)])
                S.add("dve", lambda e, b=b: e.tensor_copy(out=oaT[:, :, b * 8:(b + 1) * 8], in_=pqk[:, :, 0:8]),
                      reads=[("spT",)], writes=[oaTk + (b,)])
            obT, obTk = obTr.next()
            xt, xk = xr.next()
            S.add("sp", lambda e, ti=ti, obT=obT: e.dma_start(out=obT[:], in_=OBT[:, ti, :, :]), writes=[obTk], dma=True)
            S.add("sp", lambda e, ti=ti, xt=xt: e.dma_start(out=xt[:], in_=xin[ti * 128:(ti + 1) * 128, :]), writes=[xk],
                  dma=True)
            for nb in range(2):
                pz, pzk = pS.next()
                for kc in range(16):
                    lt = oaT[:, kc, :] if kc < 8 else obT[:, kc - 8, :]
                    S.add("pe", lambda e, kc=kc, lt=lt, pz=pz, nb=nb: e.matmul(
                        pz[:], lhsT=lt, rhs=Wo[:, kc, nb * 512:(nb + 1) * 512], start=(kc == 0), stop=(kc == 15)),
                        reads=[oaTk + (b_,) for b_ in range(SSEQ)] + [obTk] + (wokeys if stl == 0 else []), writes=[pzk])
                x1, x1k = x1r.next()
                S.add("dve", lambda e, pz=pz, nb=nb, xt=xt, x1=x1: e.tensor_tensor(
                    out=x1[:], in0=pz[:], in1=xt[:, nb * 512:(nb + 1) * 512], op=ALU.add), reads=[pzk, xk], writes=[x1k])
                S.add("sp", lambda e, x1=x1, ti=ti, nb=nb: e.dma_start(
                    out=X1[ti * 128:(ti + 1) * 128, nb * 512:(nb + 1) * 512], in_=x1[:]), reads=[x1k],
                    writes=[("X1", ti, nb)], dma=True)
                if write_y:
                    S.add("sp", lambda e, x1=x1, ti=ti, nb=nb: e.dma_start(
                        out=o_y[ti * 128:(ti + 1) * 128, nb * 512:(nb + 1) * 512], in_=x1[:]), reads=[x1k],
                        writes=[("o_y", ti, nb)], dma=True)
        S.emit()


def phase_c(nc, S, sb, ps, ldpar, ldcst, identb, eps_t, g):
    KCUT = float(os.environ.get("KCUT", "99"))
    w_in1, w_out1, o_y, o_gp, o_gs, state, X1 = (g["w_in1"], g["w_out1"], g["o_y"], g["o_gp"], g["o_gs"], g["state"],
                                                  g["X1"])
    par = g["par"]
    with ExitStack() as st:
        W1 = sb(st, "W1", [128, 8, IN1], BF16)
        Wo1 = sb(st, "Wo1", [128, 8, D], BF16)
        wg = sb(st, "wg", [16, 512], BF16)
        for kc in range(8):
            for cb in range(2):
                S.add("pool", lambda e, kc=kc, cb=cb: e.dma_start(
                    out=W1[:, kc, cb * 1544:(cb + 1) * 1544],
                    in_=w_in1[kc * 128:(kc + 1) * 128, cb * 1544:(cb + 1) * 1544]), writes=[("W1", kc, cb)], dma=True)
            S.add("pool", lambda e, kc=kc: e.dma_start(out=Wo1[:, kc, :], in_=w_out1[kc * 128:(kc + 1) * 128, :]),
                  writes=[("Wo1", kc)], dma=True)
        o_wg = PAR["wgate"][0]
        S.add("pool", lambda e: e.dma_start(out=wg[:], in_=par[0:16, o_wg:o_wg + 512]), writes=[("wg",)], dma=True)
        wkeys = [("W1", kc, cb) for kc in range(8) for cb in range(2)] + [("Wo1", kc) for kc in range(8)] + [("wg",)]
        g1T, g1Tk = ldpar(st, "g1T")
        bgate, bgk = ldpar(st, "bgate")
        glag, glk = ldpar(st, "glag")
        Lp, Lpk = ldcst(st, "Lp")
        Ls, Lsk = ldcst(st, "Ls")
        Ap, Apk = ldcst(st, "Ap")
        As, Ask = ldcst(st, "As")
        selp, selpk = ldcst(st, "selp")
        sels, selsk = ldcst(st, "sels")
        ohs, ohsk = ldcst(st, "ohs")
        cks = [g1Tk, bgk, glk, Lpk, Lsk, Apk, Ask, selpk, selsk, ohsk]

        def ring(name, n, shape, dt):
            return Ring([sb(st, "%s%d" % (name, i), shape, dt) for i in range(n)], name)

        Sst = sb(st, "Sst", [128, 4, 256], F32)
        Sb = sb(st, "Sbb", [128, 4, 256], BF16)
        S.add("dve", lambda e: e.memset(Sst[:], 0.0), writes=[("S",)])
        S.add("dve", lambda e: e.memset(Sb[:], 0.0), writes=[("Sb",)])
        xr = ring("cx", 2, [128, D], F32)
        xsr = ring("cxs", 1, [128, D], BF16)
        xtr = ring("cxT", 2, [128, 8, 128], BF16)
        junk = sb(st, "cjunk", [128, D], BF16)
        stat = ring("cst", 6, [128, 16], F32)
        zaTr = ring("czaT", 1, [16, 128], BF16)
        t1r = ring("ct1", 1, [128, 512], F32)
        bcsr = ring("cbcs", 1, [128, 512], F32)
        ebr = ring("ceb", 1, [128, 512], F32)
        enbr = ring("cenb", 1, [128, 512], F32)
        qtr_ = ring("cqt", 1, [128, 512], BF16)
        ktr_ = ring("ckt", 1, [128, 512], BF16)
        qTr = ring("cqT", 1, [128, 4, 128], BF16)
        kTr = ring("ckT", 1, [128, 4, 128], BF16)
        ELr = ring("cEL", 1, [128, 4, 16], F32)
        vr = ring("cv", 1, [128, D], BF16)
        sgr = ring("csg", 1, [128, D], BF16)
        attr_ = ring("catt", 1, [128, 4, 128], BF16)
        tmpSr = ring("ctmpS", 1, [128, 4, 256], F32)
        sqr = ring("csq", 1, [128, D], F32)
        o1r = ring("co1", 1, [128, D], BF16)
        oTr = ring("coT", 1, [128, 8, 128], BF16)
        yr = ring("cy", 2, [128, 512], F32)
        QZ = sb(st, "cQZ", [128, 16 * 136], BF16)
        KZr = ring("cKZ", 2, [128, 16, 128], BF16)
        s0r = ring("cs0", 2, [128, 4, 256], F32)
        s0br = ring("cs0b", 2, [128, 4, 256], BF16)
        snr = ring("csn", 2, [128, 4, 256], F32)
        S.add("pool", lambda e: e.memset(QZ[:], 0.0), writes=[("QZ",)])

        ptr = ps(st, "cptr", [128, 8, 128], BF16)
        pzr = Ring([ps(st, "cpz%d" % i, [128, 512], F32) for i in range(2)], "cpz")
        pA = ps(st, "cpA", [128, 4, 128], F32)
        pOg = ps(st, "cpOg", [128, 4, 256], F32)
        pD = ps(st, "cpD", [128, 4, 256], F32)

        def load_x(ti):
            xt, xk = xr.next()
            S.add("sp", lambda e: e.dma_start(out=xt[:], in_=X1[ti * 128:(ti + 1) * 128, :]), writes=[xk], dma=True)
            return xt, xk

        xq = [load_x(0)]

        def tile_body(ti):
            sample = ti >= NPT
            first = ti == 0
            sq0 = (ti - NPT) * SSEQ
            if ti + 1 < NT:
                xq.append(load_x(ti + 1))
            xt, xk = xq.pop(0)
            sta, sk = stat.next()
            xs, xsk = xsr.next()
            xT, xTk = xtr.next()
            S.add("act", lambda e: e.activation(out=junk[:], in_=xt[:], func=AF.Square, accum_out=sta[:, 0:1]),
                  reads=[xk], writes=[sk + ("ss",), ("cjunk",)])
            rstd_from_ss(S, sta[:, 0:1], sta[:, 8:9], D, eps_t, sk + ("ss",), sk + ("r",))
            S.add("act", lambda e: e.activation(out=xs[:], in_=xt[:], func=AF.Copy, scale=sta[:, 8:9]),
                  reads=[xk, sk + ("r",)], writes=[xsk])
            for kc in range(8):
                S.add("pe", lambda e, kc=kc: e.transpose(out=ptr[:, kc, :], in_=xs[:, kc * 128:(kc + 1) * 128],
                                                         identity=identb[:]),
                      reads=[xsk, ("identb",)], writes=[("cptr",)])
            S.add("dve", lambda e: e.tensor_tensor(out=xT[:], in0=ptr[:], in1=g1T[:].unsqueeze(2).to_broadcast([128, 8, 128]),
                                                   op=ALU.mult), reads=[("cptr",), g1Tk], writes=[xTk])

            def proj(c0, w, M_=128):
                pz, pzk = pzr.next()
                for kc in range(8):
                    S.add("pe", lambda e, kc=kc: e.matmul(pz[0:M_, 0:w], lhsT=xT[:, kc, :], rhs=W1[:, kc, c0:c0 + w],
                                                          start=(kc == 0), stop=(kc == 7)),
                          reads=[xTk] + (wkeys + cks if first else []), writes=[pzk])
                return pz, pzk

            if KCUT <= 1:
                return
            pz, pzk = pzr.next()
            for kc in range(8):
                S.add("pe", lambda e, kc=kc: e.matmul(pz[0:16, 0:128], lhsT=W1[:, kc, 3072:3088], rhs=xT[:, kc, :],
                                                      start=(kc == 0), stop=(kc == 7)),
                      reads=[xTk] + (wkeys + cks if first else []), writes=[pzk])
            zaT, zaTk = zaTr.next()
            S.add("dve", lambda e: e.tensor_copy(out=zaT[:], in_=pz[0:16, 0:128]), reads=[pzk], writes=[zaTk])
            if KCUT <= 1.2:
                return
            pg, pgk = pzr.next()
            S.add("pe", lambda e: e.matmul(pg[:], lhsT=zaT[:], rhs=wg[:], start=True, stop=True), reads=[zaTk],
                  writes=[pgk])
            if KCUT <= 1.3:
                return
            t1, t1k = t1r.next()
            S.add("dve", lambda e: e.tensor_tensor(out=t1[:], in0=pg[:], in1=bgate[:], op=ALU.add), reads=[pgk, bgk],
                  writes=[t1k])
            if KCUT <= 1.4:
                return
            S.add("act", lambda e: e.activation(out=t1[:], in_=t1[:], func=AF.Exp, scale=-1.0), reads=[t1k],
                  writes=[t1k + ("e",)])
            S.add("act", lambda e: e.activation(out=t1[:], in_=t1[:], func=AF.Ln, bias=1.0), reads=[t1k + ("e",)],
                  writes=[t1k + ("l",)])
            if KCUT <= 1.6:
                return
            Lm = Ls if sample else Lp
            pb_, pbk = pzr.next()
            S.add("pe", lambda e: e.matmul(pb_[:], lhsT=Lm[:], rhs=t1[:], start=True, stop=True),
                  reads=[t1k + ("l",)], writes=[pbk])
            if KCUT <= 1.8:
                return
            bcs, bcsk = bcsr.next()
            eb, ebk = ebr.next()
            enb, enbk = enbr.next()
            S.add("dve", lambda e: e.tensor_copy(out=bcs[:], in_=pb_[:]), reads=[pbk], writes=[bcsk])
            S.add("act", lambda e: e.activation(out=eb[:], in_=pb_[:], func=AF.Exp), reads=[pbk], writes=[ebk])
            S.add("act", lambda e: e.activation(out=enb[:], in_=pb_[:], func=AF.Exp, scale=-1.0), reads=[pbk],
                  writes=[enbk])
            if KCUT <= 2:
                return
            pq, pqk_ = proj(0, 512)
            qt, qtk = qtr_.next()
            S.add("dve", lambda e: e.scalar_tensor_tensor(out=qt[:], in0=pq[:], scalar=128.0 ** -0.5, in1=eb[:],
                                                          op0=ALU.mult, op1=ALU.mult), reads=[pqk_, ebk], writes=[qtk])
            pk, pkk = proj(512, 512)
            kt, ktk = ktr_.next()
            S.add("dve", lambda e: e.tensor_tensor(out=kt[:], in0=pk[:], in1=enb[:], op=ALU.mult), reads=[pkk, enbk],
                  writes=[ktk])
            if KCUT <= 3:
                return
            nl = 16 if sample else 2
            selm = sels if sample else selp
            pl, plk = pzr.next()
            for h in range(4):
                S.add("pe", lambda e, h=h: e.matmul(pl[:, h * 16:h * 16 + nl], lhsT=bcs[:, h * 128:(h + 1) * 128],
                                                    rhs=selm[:, 0:nl], start=True, stop=True),
                      reads=[bcsk], writes=[plk])
            EL, ELk = ELr.next()
            S.add("act", lambda e: e.activation(out=EL[:, :, 0:nl],
                                                in_=pl[:, 0:64].rearrange("p (h n) -> p h n", h=4)[:, :, 0:nl],
                                                func=AF.Exp), reads=[plk], writes=[ELk])
            if KCUT <= 4:
                return
            qT, qTk = qTr.next()
            kT, kTk = kTr.next()
            for h in range(4):
                S.add("pe", lambda e, h=h: e.transpose(out=ptr[:, h, :], in_=qt[:, h * 128:(h + 1) * 128],
                                                       identity=identb[:]), reads=[qtk, ("identb",)],
                      writes=[("cptr",)])
            for h in range(4):
                S.add("pe", lambda e, h=h: e.transpose(out=ptr[:, 4 + h, :], in_=kt[:, h * 128:(h + 1) * 128],
                                                       identity=identb[:]), reads=[ktk, ("identb",)],
                      writes=[("cptr",)])
            S.add("dve", lambda e: e.tensor_copy(out=qT[:], in_=ptr[:, 0:4, :]), reads=[("cptr",)], writes=[qTk])
            S.add("dve", lambda e: e.tensor_copy(out=kT[:], in_=ptr[:, 4:8, :]), reads=[("cptr",)], writes=[kTk])
            v, vk = vr.next()
            sg, sgk = sgr.next()
            for b in range(2):
                pz, pzk = proj(1024 + b * 512, 512)
                S.add("act", lambda e, b=b, pz=pz: e.activation(out=v[:, b * 512:(b + 1) * 512], in_=pz[:],
                                                                func=AF.Copy), reads=[pzk], writes=[vk + (b,)])
            for b in range(2):
                pz, pzk = proj(2048 + b * 512, 512)
                S.add("act", lambda e, b=b, pz=pz: e.activation(out=sg[:, b * 512:(b + 1) * 512], in_=pz[:],
                                                                func=AF.Silu), reads=[pzk], writes=[sgk + (b,)])
            vks = [vk + (0,), vk + (1,)]
            if KCUT <= 5:
                return
            for h in range(4):
                S.add("pe", lambda e, h=h: e.matmul(pA[:, h, :], lhsT=kT[:, h, :], rhs=qT[:, h, :], start=True,
                                                    stop=True), reads=[kTk, qTk], writes=[("cpA",)])
            att, attk = attr_.next()
            Am = As if sample else Ap
            S.add("dve", lambda e: e.tensor_tensor(out=att[:], in0=pA[:], in1=Am[:].unsqueeze(1).to_broadcast([128, 4, 128]),
                                                   op=ALU.mult), reads=[("cpA",)], writes=[attk])
            for h in range(4):
                S.add("pe", lambda e, h=h: e.matmul(pOg[:, h, :], lhsT=att[:, h, :], rhs=v[:, h * 256:(h + 1) * 256],
                                                    start=(h % 2 == 0), stop=False, skip_group_check=True),
                      reads=[attk] + vks, writes=[("cpOg", h // 2)])
            if KCUT <= 6:
                return
            if not sample:
                for c in range(2):
                    cs = slice(c * 64, (c + 1) * 64)
                    for h in range(4):
                        S.add("pe", lambda e, h=h, cs=cs: e.matmul(pOg[cs, h, :], lhsT=qT[:, h, cs], rhs=Sb[:, h, :],
                                                                   start=False, stop=(c == 1), skip_group_check=True),
                              reads=[qTk, ("Sb",)], writes=[("cpOg", h // 2)])
                    for h in range(4):
                        S.add("pe", lambda e, h=h, cs=cs: e.matmul(pD[:, h, :], lhsT=kt[cs, h * 128:(h + 1) * 128],
                                                                   rhs=v[cs, h * 256:(h + 1) * 256], start=(h % 2 == 0),
                                                                   stop=True, skip_group_check=True),
                              reads=[ktk] + vks, writes=[("cpD", h // 2)])
                    tmpS, tmpSk = tmpSr.next()
                    S.add("dve", lambda e, tmpS=tmpS: e.tensor_tensor(out=tmpS[:], in0=pD[:], in1=Sst[:], op=ALU.add),
                          reads=[("cpD", 0), ("cpD", 1), ("S",)], writes=[tmpSk])
                    for h in range(4):
                        S.add("dve", lambda e, h=h, c=c, tmpS=tmpS: e.tensor_scalar(
                            out=Sst[:, h, :], in0=tmpS[:, h, :], scalar1=EL[:, h, c:c + 1], scalar2=None, op0=ALU.mult),
                            reads=[tmpSk, ELk], writes=[("S",)])
                    S.add("pool", lambda e: e.tensor_copy(out=Sb[:], in_=Sst[:]), reads=[("S",)], writes=[("Sb",)])
                if ti == NPT - 1:
                    S.add("sp", lambda e: e.dma_start(out=o_gp.rearrange("h k v -> k h v"), in_=Sst[:]),
                          reads=[("S",)], writes=[("o_gp",)], dma=True)
            elif KCUT > 7:
                for h in range(4):
                    S.add("pool", lambda e, h=h: e.tensor_copy(
                        out=QZ[:].rearrange("p (b x) -> p b x", x=136)[:, :, 0:8],
                        in_=qT[:, h, :].rearrange("p (b i) -> p b i", i=8)), reads=[qTk, ("QZ",)], writes=[("QZ",)])
                    KZ, KZk = KZr.next()
                    S.add("dve", lambda e, h=h, KZ=KZ: e.tensor_tensor(
                        out=KZ[:], in0=kt[:, h * 128:(h + 1) * 128].unsqueeze(1).to_broadcast([128, 16, 128]),
                        in1=ohs[:].unsqueeze(2).to_broadcast([128, 16, 128]), op=ALU.mult), reads=[ktk], writes=[KZk])
                    for b in range(SSEQ):
                        s0, s0k = s0r.next()
                        s0b, s0bk = s0br.next()
                        S.add("sp", lambda e, b=b, h=h, s0=s0: e.dma_start(out=s0[:, 0, :], in_=state[sq0 + b, h, :, :]),
                              writes=[s0k], dma=True)
                        S.add("pool", lambda e, s0=s0, s0b=s0b: e.tensor_copy(out=s0b[:, 0, :], in_=s0[:, 0, :]),
                              reads=[s0k], writes=[s0bk])
                        S.add("pe", lambda e, b=b, h=h, s0b=s0b: e.matmul(
                            pOg[:, h, :], lhsT=QZ[:, b * 128:(b + 1) * 128], rhs=s0b[:, 0, :], start=False,
                            stop=(b == SSEQ - 1), skip_group_check=True),
                            reads=[("QZ",), s0bk], writes=[("cpOg", h // 2)])
                        S.add("pe", lambda e, b=b, h=h, KZ=KZ: e.matmul(
                            pD[:, b % 4, :], lhsT=KZ[:, b, :], rhs=v[:, h * 256:(h + 1) * 256], start=True, stop=True,
                            skip_group_check=True), reads=[KZk] + vks, writes=[("cpD", (b % 4) // 2)])
                        sn, snk = snr.next()
                        S.add("dve", lambda e, b=b, s0=s0, sn=sn: e.tensor_tensor(out=sn[:, 0, :], in0=pD[:, b % 4, :],
                                                                                in1=s0[:, 0, :], op=ALU.add),
                              reads=[("cpD", (b % 4) // 2), s0k], writes=[snk, snk + ("f",)])
                        S.add("dve", lambda e, b=b, h=h, sn=sn: e.tensor_scalar(
                            out=sn[:, 0, :], in0=sn[:, 0, :], scalar1=EL[:, h, b:b + 1], scalar2=None, op0=ALU.mult),
                            reads=[snk, ELk], writes=[snk + ("f",)])
                        S.add("sp", lambda e, b=b, h=h, sn=sn: e.dma_start(out=o_gs[sq0 + b, h, :, :], in_=sn[:, 0, :]),
                              reads=[snk + ("f",)], writes=[("o_gs", ti, b, h)], dma=True)
            if KCUT <= 8:
                return
            sq, sqk = sqr.next()
            sta2, sk2 = stat.next()
            S.add("act", lambda e: e.activation(out=sq[:], in_=pOg[:].rearrange("p h v -> p (h v)"), func=AF.Square),
                  reads=[("cpOg", 0), ("cpOg", 1)], writes=[sqk, sqk + ("n",), sqk + ("g",)])
            S.add("dve", lambda e: e.tensor_reduce(out=sta2[:, 0:4], in_=sq[:].rearrange("p (h v) -> p h v", h=4),
                                                   axis=AX.X, op=ALU.add), reads=[sqk], writes=[sk2 + ("ss",)])
            rstd_from_ss(S, sta2[:, 0:4], sta2[:, 8:12], 256, eps_t, sk2 + ("ss",), sk2 + ("r",))
            S.add("dve", lambda e: e.tensor_tensor(out=sq[:].rearrange("p (h v) -> p h v", h=4), in0=pOg[:],
                                                   in1=sta2[:, 8:12].unsqueeze(2).to_broadcast([128, 4, 256]),
                                                   op=ALU.mult),
                  reads=[("cpOg", 0), ("cpOg", 1), sk2 + ("r",), sqk], writes=[sqk + ("n",)])
            S.add("pool", lambda e: e.tensor_tensor(out=sq[:], in0=sq[:], in1=glag[:], op=ALU.mult),
                  reads=[sqk + ("n",), glk], writes=[sqk + ("g",)])
            o1, o1k = o1r.next()
            S.add("pool", lambda e: e.tensor_tensor(out=o1[:], in0=sq[:], in1=sg[:], op=ALU.mult),
                  reads=[sqk + ("g",), sgk + (0,), sgk + (1,)], writes=[o1k])
            oT, oTk = oTr.next()
            for kc in range(8):
                S.add("pe", lambda e, kc=kc: e.transpose(out=ptr[:, kc, :], in_=o1[:, kc * 128:(kc + 1) * 128],
                                                         identity=identb[:]), reads=[o1k, ("identb",)],
                      writes=[("cptr",)])
            S.add("dve", lambda e: e.tensor_copy(out=oT[:], in_=ptr[:]), reads=[("cptr",)], writes=[oTk])
            for nb in range(2):
                pz, pzk = pzr.next()
                for kc in range(8):
                    S.add("pe", lambda e, kc=kc, nb=nb, pz=pz: e.matmul(pz[:], lhsT=oT[:, kc, :],
                                                                       rhs=Wo1[:, kc, nb * 512:(nb + 1) * 512],
                                                                       start=(kc == 0), stop=(kc == 7)),
                          reads=[oTk], writes=[pzk])
                y, yk = yr.next()
                S.add("dve", lambda e, nb=nb, pz=pz, y=y: e.tensor_tensor(out=y[:], in0=pz[:],
                                                                        in1=xt[:, nb * 512:(nb + 1) * 512], op=ALU.add),
                      reads=[pzk, xk], writes=[yk])
                S.add("sp", lambda e, nb=nb, y=y: e.dma_start(out=o_y[ti * 128:(ti + 1) * 128, nb * 512:(nb + 1) * 512],
                                                              in_=y[:]), reads=[yk], writes=[("o_y", ti, nb)], dma=True)

        for ti_ in range(NT):
            tile_body(ti_)
        S.emit()


_NC_CACHE = {}


def kernel(**inp):
    phases = inp.pop("_phases", ("A", "B", "S", "C"))
    ck = np.asarray(inp["cache_k"]).reshape(-1, 1024)
    cvv = np.asarray(inp["cache_v"]).reshape(-1, 1024)
    PP = ck.shape[0] // 128
    key = (tuple(phases), PP)
    if key not in _NC_CACHE:
        _NC_CACHE[key] = build_nc(phases, PP)
    nc = _NC_CACHE[key]
    xp = np.asarray(inp["x_prompt"], np.float32)
    xs = np.asarray(inp["x_sample"], np.float32)
    par = host_params(inp)
    cst = host_consts()
    spw = np.asarray(inp["spatial_w"], np.float32)[0]
    spwT = np.ascontiguousarray(spw.transpose(2, 0, 1))
    small = spw[:, :8, :8]
    spwTs = np.ascontiguousarray(np.tile(small.transpose(2, 0, 1), (16, 1, 16)))
    ck = np.ascontiguousarray(ck, np.float32)
    cvv = np.ascontiguousarray(cvv, np.float32)
    pt = np.asarray(inp["page_table"], np.int32)
    st = np.asarray(inp["state_gla"], np.float32)[0]
    NSQ = NST * SSEQ
    in_maps = []
    for c in range(NCORES):
        xin = np.concatenate([xp[c], xs[c * NSQ:(c + 1) * NSQ].reshape(NST * 128, D)], axis=0)
        in_maps.append({
            "xin": np.ascontiguousarray(xin), "par": par, "cst": cst,
            "w_in0": np.asarray(inp["w_in0"], np.float32)[0], "w_out0": np.asarray(inp["w_out0"], np.float32)[0],
            "w_in1": np.asarray(inp["w_in1"], np.float32)[0], "w_out1": np.asarray(inp["w_out1"], np.float32)[0],
            "spwT": spwT, "spwTs": spwTs, "cache_k": ck, "cache_v": cvv,
            "state": np.ascontiguousarray(st[c * NSQ:(c + 1) * NSQ]),
            "ptab": np.ascontiguousarray(np.broadcast_to(pt[c * NSQ:(c + 1) * NSQ].reshape(1, -1), (128, NSQ * NPAGES))),
        })
    res = run_bass_kernel_spmd(nc, in_maps, core_ids=list(range(NCORES)))
    R = res.results

    def prompt(name):
        return np.stack([R[c][name][:SEQ] for c in range(NCORES)])

    def samp(name):
        return np.concatenate([R[c][name][SEQ:].reshape(NSQ, 8, D) for c in range(NCORES)])

    y_p = prompt("o_y")
    y_s = samp("o_y")
    k_p = prompt("o_k").reshape(1, 4, SEQ, 8, 2, 64)
    v_p = prompt("o_v").reshape(1, 4, SEQ, 8, 128)
    k_s = samp("o_k").reshape(1, 128, 8, 8, 2, 64)
    v_s = samp("o_v").reshape(1, 128, 8, 8, 128)
    vb_s = np.concatenate([R[c]["o_vb"].reshape(NSQ, 8, D) for c in range(NCORES)])[None]
    g_p = np.stack([R[c]["o_gp"] for c in range(NCORES)])[None]
    g_s = np.concatenate([R[c]["o_gs"] for c in range(NCORES)])[None]
    return (y_p, y_s, k_p, v_p, k_s, v_s, vb_s, g_p, g_s)
```

```python
import math
import os
from contextlib import ExitStack

import numpy as np
import concourse.bass as bass
import concourse.mybir as mybir
from concourse.bass_utils import run_bass_kernel_spmd

F32 = mybir.dt.float32
BF16 = mybir.dt.bfloat16
I32 = mybir.dt.int32
AF = mybir.ActivationFunctionType
ALU = mybir.AluOpType
AX = mybir.AxisListType

NCORES = 4
NST = 2
D = 1024
SEQ = 4096
NPT = SEQ // 128
NT = NPT + NST
NTOK = NT * 128
H_A = 8
IN0 = 7168
IN1 = 3088
EPS = 1e-6
LAM_INIT = 0.8 - 0.6 * math.exp(-0.3 * 0)
NPAGES = 16
SSEQ = 16
GOFF = 639
NG = 1152 + 128

COMPUTE = ("pe", "act", "dve", "pool")
NDMA_SEMS = 10


class Op:
    __slots__ = ("eng", "fn", "dma", "deps", "marked", "sem", "val", "prewait")

    def __init__(self, eng, fn, dma):
        self.eng = eng
        self.fn = fn
        self.dma = dma
        self.deps = None
        self.marked = False
        self.sem = None
        self.val = 0
        self.prewait = None


class Sched:
    def __init__(self, nc, stack):
        self.nc = nc
        self.ops = []
        self.res = {}
        self.stack = stack
        self.sem_eng = None
        self.dma_ring = {q: [stack.enter_context(nc.semaphore("d_%s%d" % (q, n))) for n in range(NDMA_SEMS)]
                         for q in ("sp", "act", "pool")}
        self.cnt = {e: 0 for e in COMPUTE}
        self.dcnt = {q: 0 for q in self.dma_ring}
        self.seen = {s: {} for s in ("pe", "act", "dve", "pool", "sp")}
        self.barrier = None
        self.nphase = 0

    EXCL = ("pT", "pz", "pqk", "pmx", "bpS", "bpO", "bpqk", "cptr", "cpz", "cpA", "cpOg", "cpD", "gps", "spA", "spO",
            "spT")

    def add(self, eng, fn, reads=(), writes=(), dma=False):
        ex = [k for k in reads if k[0] in self.EXCL]
        if ex:
            writes = list(writes) + [k for k in ex if k not in writes]
        i = len(self.ops)
        op = Op(eng, fn, dma)
        deps = {}
        res = self.res
        for k in reads:
            r = res.get(k)
            if r is not None and r[0] is not None:
                deps[r[0]] = True
        for k in writes:
            r = res.get(k)
            if r is not None:
                if r[0] is not None and r[0] not in deps:
                    deps[r[0]] = False
                for j in r[1]:
                    if j not in deps:
                        deps[j] = False
        for k in reads:
            r = res.get(k)
            if r is None:
                res[k] = [None, [i]]
            else:
                r[1].append(i)
        for k in writes:
            res[k] = [i, []]
        deps.pop(i, None)
        op.deps = deps
        self.ops.append(op)
        return i

    @staticmethod
    def _skip(p, op, raw):
        return (not p.dma) and p.eng == op.eng and (not op.dma) and (p.eng == "pe" or not raw)

    def emit(self, final=False):
        nc = self.nc
        ops = self.ops
        self.sem_eng = {e: self.stack.enter_context(nc.semaphore("s%d_%s" % (self.nphase, e))) for e in COMPUTE}
        self.cnt = {e: 0 for e in COMPUTE}
        streams = {"pe": [], "act": [], "dve": [], "pool": [], "sp": []}
        for op in ops:
            streams[op.eng].append(op)
            for j, raw in op.deps.items():
                p = ops[j]
                if p.dma or self._skip(p, op, raw):
                    continue
                p.marked = True
        for e in COMPUTE:
            for op in reversed(streams[e]):
                if not op.dma:
                    op.marked = True
                    break
        for op in ops:
            if op.dma:
                n = self.dcnt[op.eng]
                self.dcnt[op.eng] = n + 1
                op.sem = self.dma_ring[op.eng][n % NDMA_SEMS]
                op.val = 16 * (n // NDMA_SEMS + 1)
                if n >= NDMA_SEMS:
                    op.prewait = (op.sem, op.val - 16)
            elif op.marked:
                self.cnt[op.eng] += 1
                op.sem = self.sem_eng[op.eng]
                op.val = self.cnt[op.eng]
        barrier_in = self.barrier

        def run_stream(name, eng):
            seen = self.seen[name]
            if barrier_in:
                for s, v in barrier_in:
                    if v > 0 and seen.get(s, 0) < v:
                        eng.wait_ge(s, v)
                        seen[s] = v
            for op in streams[name]:
                waits = {}
                if op.prewait is not None:
                    waits[op.prewait[0]] = op.prewait[1]
                for j, raw in op.deps.items():
                    p = ops[j]
                    if self._skip(p, op, raw) or p.sem is None:
                        continue
                    if waits.get(p.sem, 0) < p.val:
                        waits[p.sem] = p.val
                for s, v in waits.items():
                    if seen.get(s, 0) < v:
                        eng.wait_ge(s, v)
                        seen[s] = v
                ins = op.fn(eng)
                if op.dma:
                    ins.then_inc(op.sem, 16)
                elif op.marked:
                    ins.then_inc(op.sem, 1)
            if final and name == "sp":
                for q, ring in self.dma_ring.items():
                    n = self.dcnt[q]
                    for r, s in enumerate(ring):
                        v = 16 * ((n - r + NDMA_SEMS - 1) // NDMA_SEMS) if n > r else 0
                        if v > 0 and seen.get(s, 0) < v:
                            eng.wait_ge(s, v)
                            seen[s] = v

        with nc.Block() as block:
            @block.sync
            def _(e):
                run_stream("sp", e)

            @block.tensor
            def _(e):
                run_stream("pe", e)

            @block.scalar
            def _(e):
                run_stream("act", e)

            @block.vector
            def _(e):
                run_stream("dve", e)

            @block.gpsimd
            def _(e):
                run_stream("pool", e)

        bar = [(self.sem_eng[e], self.cnt[e]) for e in COMPUTE]
        for q, ring in self.dma_ring.items():
            n = self.dcnt[q]
            for r, s in enumerate(ring):
                v = 16 * ((n - r + NDMA_SEMS - 1) // NDMA_SEMS) if n > r else 0
                bar.append((s, v))
        self.barrier = bar
        self.ops = []
        self.res = {}
        self.nphase += 1


class Ring:
    def __init__(self, tiles, name):
        self.tiles = tiles
        self.name = name
        self.i = 0

    def next(self):
        k = self.i % len(self.tiles)
        self.i += 1
        return self.tiles[k], (self.name, k)


def t5_bucket_np(n):
    n = np.maximum(n, 0)
    nf = np.maximum(n, 1).astype(np.float32)
    large = 16 + (np.log(nf / np.float32(16)) / np.float32(math.log(128 / 16)) * np.float32(16)).astype(np.int32)
    large = np.minimum(large, 31)
    return np.where(n < 16, n, large)


PAR = {}
_off = 0
for _n, _w in (("g0T", 8), ("g1T", 8), ("qg", 128), ("kg", 128), ("lng", 1024), ("lnb", 1024), ("subg", 128),
               ("spbT", 8), ("spbTs", 8), ("lam", 256), ("bgate", 512), ("glag", 1024), ("relb", 8),
               ("wgate", 512)):
    PAR[_n] = (_off, _w)
    _off += _w
NPAR = _off

CST = {}
_off = 0
for _n, _w in (("ident", 128), ("mtp", 128), ("mts", 128), ("Lp", 128), ("Ls", 128), ("Ap", 128), ("As", 128),
               ("selp", 2), ("sels", 16), ("ohs", 16), ("sel8", 8), ("c01", 1), ("oh1", NG), ("gvalid", NG), ("iota", 1)):
    CST[_n] = (_off, _w)
    _off += _w
NCST = _off


def host_consts():
    c = np.zeros((128, NCST), np.float32)

    def put(name, a):
        o, w = CST[name]
        c[:a.shape[0], o:o + w] = a

    s = np.arange(128)[:, None]
    t = np.arange(128)[None, :]
    put("ident", np.eye(128, dtype=np.float32))
    put("mtp", (s <= t).astype(np.float32))
    put("mts", ((s // 8 == t // 8) & (s <= t)).astype(np.float32))
    same64 = (s // 64 == t // 64) & (s <= t)
    same8 = (s // 8 == t // 8) & (s <= t)
    put("Lp", same64.astype(np.float32) * (-1.0 / 16.0))
    put("Ls", same8.astype(np.float32) * (-1.0 / 16.0))
    put("Ap", same64.astype(np.float32))
    put("As", same8.astype(np.float32))
    selp = np.zeros((128, 2), np.float32)
    selp[63, 0] = 1
    selp[127, 1] = 1
    put("selp", selp)
    sels = np.zeros((128, 16), np.float32)
    sels[np.arange(16) * 8 + 7, np.arange(16)] = 1
    put("sels", sels)
    put("ohs", (s // 8 == np.arange(16)[None, :]).astype(np.float32))
    n = GOFF - np.arange(NG)
    bk = t5_bucket_np(n)
    oh1 = (bk[None, :] == np.arange(32)[:, None]).astype(np.float32)
    oh1[31, :] -= 1.0
    put("oh1", oh1)
    put("gvalid", np.broadcast_to((n >= 0).astype(np.float32)[None, :], (128, NG)))
    put("iota", np.arange(128, dtype=np.float32)[:, None])
    sel8 = np.zeros((128, 8), np.float32)
    sel8[np.arange(16), np.arange(16) % 8] = 1
    put("sel8", sel8)
    c01 = np.zeros((128, 1), np.float32)
    c01[8:16] = 1
    put("c01", c01)
    return c


def host_params(inp):
    p = np.zeros((128, NPAR), np.float32)

    def put(name, a):
        o, w = PAR[name]
        a = np.asarray(a, np.float32)
        p[:a.shape[0], o:o + w] = a

    def bc(v):
        return np.broadcast_to(np.asarray(v, np.float32).reshape(1, -1), (128, np.asarray(v).size))

    put("g0T", inp["norm0_g"][0].reshape(8, 128).T)
    put("g1T", inp["norm1_g"][0].reshape(8, 128).T)
    put("qg", bc(inp["q_norm_g"][0].reshape(128)))
    put("kg", bc(inp["k_norm_g"][0].reshape(128)))
    put("lng", bc(inp["ln_v_g"][0]))
    put("lnb", bc(inp["ln_v_b"][0]))
    put("subg", bc(inp["subln_g"][0]))
    put("spbT", inp["spatial_b"][0].T)
    put("spbTs", np.tile(inp["spatial_b"][0][:, :8].T, (16, 1)))
    put("lam", bc(inp["lam"][0].reshape(-1)))
    put("bgate", bc(inp["b_gate"][0]))
    put("glag", bc(np.tile(inp["gla_norm_g"][0], 4)))
    put("relb", inp["rel_bias"])
    put("wgate", inp["w_gate"][0])
    return p


def build_nc(phases=("A", "B", "C"), PP=320):
    nc = bass.Bass("TRN2", target_bir_lowering=False)

    def din(name, shape, dt=F32):
        return nc.dram_tensor(name, list(shape), dt, kind="ExternalInput").ap()

    def dout(name, shape, dt=F32):
        return nc.dram_tensor(name, list(shape), dt, kind="ExternalOutput").ap()

    def dscr(name, shape, dt):
        return nc.dram_tensor(name, list(shape), dt, kind="Internal").ap()

    xin = din("xin", [NTOK, D])
    par = din("par", [128, NPAR])
    cst = din("cst", [128, NCST])
    w_in0 = din("w_in0", [D, IN0])
    w_out0 = din("w_out0", [2 * D, D])
    w_in1 = din("w_in1", [D, IN1])
    w_out1 = din("w_out1", [D, D])
    spwT = din("spwT", [128, 8, 128])
    spwTs = din("spwTs", [128, 8, 128])
    cache_k = din("cache_k", [PP * 128, 1024])
    cache_v = din("cache_v", [PP * 128, 1024])
    state = din("state", [NST * SSEQ, 4, 128, 256])
    ptab = din("ptab", [128, NST * SSEQ * NPAGES], I32)

    o_y = dout("o_y", [NTOK, D])
    o_k = dout("o_k", [NTOK, D])
    o_v = dout("o_v", [NTOK, D])
    o_vb = dout("o_vb", [NST * 128, D])
    o_gp = dout("o_gp", [4, 128, 256])
    o_gs = dout("o_gs", [NST * SSEQ, 4, 128, 256])

    QT = dscr("QT", [128, NT, 8, 128], BF16)
    KT = dscr("KT", [128, 8, NTOK], BF16)
    VS = dscr("VS", [8, 128, NT, 130], BF16)
    SGA = dscr("SGA", [NTOK, D], BF16)
    OBT = dscr("OBT", [128, NT, 8, 128], BF16)
    X1 = dscr("X1", [NTOK, D], F32)
    GV = dscr("GV", [8, NG], F32)

    with ExitStack() as top:
        top.enter_context(nc.allow_low_precision("bf16 matmul operands, fp32 accumulation"))
        top.enter_context(nc.allow_non_contiguous_dma("small strided scratch layouts"))
        S = Sched(nc, top)

        uid = [0]

        def sb(st, name, shape, dt):
            uid[0] += 1
            return st.enter_context(nc.sbuf_tensor("sb%d_%s" % (uid[0], name), list(shape), dt))

        def ps(st, name, shape, dt):
            uid[0] += 1
            return st.enter_context(nc.psum_tensor("ps%d_%s" % (uid[0], name), list(shape), dt))

        identb = sb(top, "identb", [128, 128], BF16)
        eps_t = sb(top, "eps_t", [128, 1], F32)
        neglam = sb(top, "neglam", [128, 1], F32)

        def ldpar(st, name, rows=128, q="sp"):
            o, w = PAR[name]
            t = sb(st, "par_" + name, [rows, w], F32)
            S.add(q, lambda e: e.dma_start(out=t[:], in_=par[0:rows, o:o + w]), writes=[("par", name)], dma=True)
            return t, ("par", name)

        def ldcst(st, name, rows=128, q="sp"):
            o, w = CST[name]
            t = sb(st, "cst_" + name, [rows, w], F32)
            S.add(q, lambda e: e.dma_start(out=t[:], in_=cst[0:rows, o:o + w]), writes=[("cst", name)], dma=True)
            return t, ("cst", name)

        with ExitStack() as st:
            ident, identk = ldcst(st, "ident")
            lamt, lamk = ldpar(st, "lam")
            relb, relbk = ldpar(st, "relb", 32)
            oh1, oh1k = ldcst(st, "oh1", 32)
            gval, gvalk = ldcst(st, "gvalid", 8)
            S.add("dve", lambda e: e.tensor_copy(out=identb[:], in_=ident[:]), reads=[identk], writes=[("identb",)])
            S.add("dve", lambda e: e.memset(eps_t[:], EPS), writes=[("eps",)])
            lt = sb(st, "lt", [128, 128], F32)
            ls = sb(st, "ls", [128, 2], F32)
            le = sb(st, "le", [128, 2], F32)
            lamv = lamt[:].rearrange("p (a b d) -> p a b d", a=2, b=2)
            S.add("dve", lambda e: e.tensor_tensor(out=lt[:].rearrange("p (a d) -> p a d", a=2), in0=lamv[:, :, 0, :],
                                                   in1=lamv[:, :, 1, :], op=ALU.mult), reads=[lamk], writes=[("lt",)])
            S.add("dve", lambda e: e.tensor_reduce(out=ls[:], in_=lt[:].rearrange("p (a d) -> p a d", a=2), axis=AX.X,
                                                   op=ALU.add), reads=[("lt",)], writes=[("ls",)])
            S.add("act", lambda e: e.activation(out=le[:], in_=ls[:], func=AF.Exp), reads=[("ls",)], writes=[("le",)])
            S.add("dve", lambda e: e.tensor_tensor(out=neglam[:], in0=le[:, 1:2], in1=le[:, 0:1], op=ALU.subtract),
                  reads=[("le",)], writes=[("neglam",)])
            S.add("dve", lambda e: e.tensor_scalar(out=neglam[:], in0=neglam[:], scalar1=-LAM_INIT, scalar2=None,
                                                   op0=ALU.add), reads=[("neglam",)], writes=[("neglam",)])
            gps = ps(st, "gps", [8, 3, 512], F32)
            gsb = sb(st, "gsb", [8, NG], F32)
            for i in range(3):
                w = min(512, NG - i * 512)
                S.add("pe", lambda e, i=i, w=w: e.matmul(gps[:, i, 0:w], lhsT=relb[:], rhs=oh1[:, i * 512:i * 512 + w],
                                                         start=True, stop=True),
                      reads=[relbk, oh1k], writes=[("gps", i)])
                S.add("act", lambda e, i=i, w=w: e.activation(out=gsb[:, i * 512:i * 512 + w], in_=gps[:, i, 0:w],
                                                              func=AF.Exp), reads=[("gps", i)], writes=[("gsb", i)])
            S.add("dve", lambda e: e.tensor_tensor(out=gsb[:], in0=gsb[:], in1=gval[:], op=ALU.mult),
                  reads=[("gsb", 0), ("gsb", 1), ("gsb", 2), gvalk], writes=[("gsb",)])
            S.add("sp", lambda e: e.dma_start(out=GV, in_=gsb[:]), reads=[("gsb",)], writes=[("GV",)], dma=True)
            S.emit()

        if "A" in phases:
            phase_a(nc, S, sb, ps, ldpar, ldcst, identb, eps_t, locals())
        if "B" in phases:
            phase_b(nc, S, sb, ps, ldpar, ldcst, identb, eps_t, neglam, locals(), write_y=("C" not in phases))
        if "S" in phases:
            phase_b2(nc, S, sb, ps, ldpar, ldcst, identb, eps_t, neglam, locals(), write_y=("C" not in phases))
        if "C" in phases:
            phase_c(nc, S, sb, ps, ldpar, ldcst, identb, eps_t, locals())
        S_final_dummy(nc, S, sb, locals())
    return nc


def S_final_dummy(nc, S, sb, g):
    with ExitStack() as st:
        t = sb(st, "fin", [128, 1], F32)
        S.add("dve", lambda e: e.memset(t[:], 0.0), writes=["fin"])
        S.emit(final=True)


def S_final_dummy(nc, S, sb, g):
    with ExitStack() as st:
        t = sb(st, "fin", [128, 1], F32)
        S.add("dve", lambda e: e.memset(t[:], 0.0), writes=[("fin",)])
        S.emit(final=True)


def rstd_from_ss(S, ss, sd, n, eps_t, kin, kout):
    S.add("act", lambda e: e.activation(out=sd, in_=ss, func=AF.Sqrt, bias=eps_t[:], scale=1.0 / n),
          reads=[kin, ("eps",)], writes=[kout + ("sd",)])
    S.add("dve", lambda e: e.reciprocal(out=sd, in_=sd), reads=[kout + ("sd",)], writes=[kout])


def phase_a(nc, S, sb, ps, ldpar, ldcst, identb, eps_t, g):
    xin, w_in0, spwT, spwTs = g["xin"], g["w_in0"], g["spwT"], g["spwTs"]
    o_k, o_v, o_vb = g["o_k"], g["o_v"], g["o_vb"]
    QT, KT, VS, SGA, OBT = g["QT"], g["KT"], g["VS"], g["SGA"], g["OBT"]
    with ExitStack() as st:
        W = sb(st, "W0", [128, 8, IN0], BF16)
        wspb = sb(st, "wspb", [128, 8, 128], BF16)
        wspsb = sb(st, "wspsb", [128, 8, 128], BF16)
        qgs = sb(st, "qgs", [128, 128], F32)
        for kc in range(8):
            for cb in range(4):
                S.add("pool", lambda e, kc=kc, cb=cb: e.dma_start(
                    out=W[:, kc, cb * 1792:(cb + 1) * 1792],
                    in_=w_in0[kc * 128:(kc + 1) * 128, cb * 1792:(cb + 1) * 1792]),
                    writes=[("W", kc, cb)], dma=True)
        wkeys = [("W", kc, cb) for kc in range(8) for cb in range(4)]

        def ring(name, n, shape, dt):
            return Ring([sb(st, "%s%d" % (name, i), shape, dt) for i in range(n)], name)

        xr = ring("x", 2, [128, D], F32)
        xsr = ring("xs", 2, [128, D], BF16)
        xtr = ring("xT", 2, [128, 8, 128], BF16)
        junk = sb(st, "junk", [128, D], BF16)
        stat = ring("stat", 8, [128, 64], F32)
        sqr = ring("sq", 2, [128, 512], F32)
        tmpr = ring("tmp", 2, [128, 512], F32)
        qbr = ring("qb", 2, [128, 1024], BF16)
        kbr = ring("kb", 2, [128, 1024], BF16)
        kfr = ring("kf", 2, [128, 512], F32)
        vfr = ring("vf", 2, [128, 512], F32)
        ver = ring("ve", 2, [128, 8, 130], BF16)
        qtr = ring("qt", 1, [128, 8, 128], BF16)
        ktr = ring("kt", 1, [128, 8, 128], BF16)
        sgr = ring("sg", 1, [128, 1024], BF16)
        ur = ring("u", 2, [128, 1024], BF16)
        gbr = ring("gb", 2, [128, 1024], BF16)
        gvr = ring("gv", 1, [128, 1024], F32)
        vbbr = ring("vbb", 2, [128, 1024], BF16)
        m1r = ring("m1", 1, [128, 1024], F32)
        obr = ring("ob", 2, [128, 1024], BF16)
        obtr = ring("obt", 1, [128, 8, 128], BF16)
        for i, r in enumerate(ver.tiles):
            S.add("pool", lambda e, r=r: e.memset(r[:, :, 128:130], 1.0), writes=[("ve", i, "ones")])

        g0T, g0Tk = ldpar(st, "g0T")
        qg_t, qgk = ldpar(st, "qg")
        kg_t, kgk = ldpar(st, "kg")
        lng_t, lngk = ldpar(st, "lng")
        lnb_t, lnbk = ldpar(st, "lnb")
        spbT_t, spbTk = ldpar(st, "spbT")
        spbTs_t, spbTsk = ldpar(st, "spbTs")
        mtp, mtpk = ldcst(st, "mtp")
        mts, mtsk = ldcst(st, "mts")
        lng, lnb = lng_t[:], lnb_t[:]
        gv0, gv0k = gvr.tiles[0], ("gv", 0)
        m10, m10k = m1r.tiles[0], ("m1", 0)
        S.add("sp", lambda e: e.dma_start(out=gv0[:].rearrange("p (g t) -> p g t", g=8), in_=spwT),
              writes=[gv0k + ("vb",)], dma=True)
        S.add("sp", lambda e: e.dma_start(out=m10[:].rearrange("p (g t) -> p g t", g=8), in_=spwTs),
              writes=[m10k + ("u",)], dma=True)
        S.add("dve", lambda e: e.tensor_tensor(out=wspb[:], in0=gv0[:].rearrange("p (g t) -> p g t", g=8),
                                               in1=mtp[:].unsqueeze(1).to_broadcast([128, 8, 128]), op=ALU.mult),
              reads=[gv0k + ("vb",), mtpk], writes=[("wspb",)])
        S.add("dve", lambda e: e.tensor_tensor(out=wspsb[:], in0=m10[:].rearrange("p (g t) -> p g t", g=8),
                                               in1=mts[:].unsqueeze(1).to_broadcast([128, 8, 128]), op=ALU.mult),
              reads=[m10k + ("u",), mtsk], writes=[("wspsb",)])
        S.add("dve", lambda e: e.tensor_scalar(out=qgs[:], in0=qg_t[:], scalar1=0.125, scalar2=None, op0=ALU.mult),
              reads=[qgk], writes=[("qgs",)])

        pT = ps(st, "pT", [128, 8, 128], BF16)
        pzr = Ring([ps(st, "pz%d" % i, [128, 512], F32) for i in range(4)], "pz")
        pqk = ps(st, "pqk", [128, 8, 128], BF16)
        pmx = ps(st, "pmx", [128, 1024], F32)
        P_K = g0Tk

        def load_x(ti):
            xt, xk = xr.next()
            S.add("sp", lambda e: e.dma_start(out=xt[:], in_=xin[ti * 128:(ti + 1) * 128, :]), writes=[xk], dma=True)
            return xt, xk

        xq = [load_x(0)]
        pending_ob = None

        def emit_ob_transposes(pend):
            ob, obk, ti = pend
            obt, obtk = obtr.next()
            for kc in range(8):
                S.add("pe", lambda e, kc=kc: e.transpose(out=pqk[:, kc, :], in_=ob[:, kc * 128:(kc + 1) * 128],
                                                         identity=identb[:]),
                      reads=[obk, ("identb",)], writes=[("pqk",)])
            S.add("dve", lambda e: e.tensor_copy(out=obt[:], in_=pqk[:]), reads=[("pqk",)], writes=[obtk])
            S.add("sp", lambda e: e.dma_start(out=OBT[:, ti, :, :], in_=obt[:]), reads=[obtk], writes=[("OBT", ti)],
                  dma=True)

        prepped = {}

        def prep(ti):
            if ti + 1 < NT:
                xq.append(load_x(ti + 1))
            xt, xk = xq.pop(0)
            sta, sk = stat.next()
            xs, xsk = xsr.next()
            xT, xTk = xtr.next()
            S.add("act", lambda e: e.activation(out=junk[:], in_=xt[:], func=AF.Square, accum_out=sta[:, 0:1]),
                  reads=[xk], writes=[sk + ("ss",), ("junk",)])
            rstd_from_ss(S, sta[:, 0:1], sta[:, 8:9], D, eps_t, sk + ("ss",), sk + ("r",))
            S.add("act", lambda e: e.activation(out=xs[:], in_=xt[:], func=AF.Copy, scale=sta[:, 8:9]),
                  reads=[xk, sk + ("r",)], writes=[xsk])
            for kc in range(8):
                S.add("pe", lambda e, kc=kc: e.transpose(out=pT[:, kc, :], in_=xs[:, kc * 128:(kc + 1) * 128],
                                                         identity=identb[:]),
                      reads=[xsk, ("identb",)], writes=[("pT",)])
            S.add("dve", lambda e: e.tensor_tensor(out=xT[:], in0=pT[:], in1=g0T[:].unsqueeze(2).to_broadcast([128, 8, 128]),
                                                   op=ALU.mult), reads=[("pT",), g0Tk], writes=[xTk])
            prepped[ti] = (xT, xTk)

        def tile_body(ti):
            nonlocal pending_ob
            sample = ti >= NPT
            xT, xTk = prepped.pop(ti)

            def proj(cb, xT=xT, xTk=xTk):
                pz, pzk = pzr.next()
                for kc in range(8):
                    S.add("pe", lambda e, kc=kc, pz=pz: e.matmul(pz[:], lhsT=xT[:, kc, :],
                                                                rhs=W[:, kc, cb * 512:(cb + 1) * 512],
                                                                start=(kc == 0), stop=(kc == 7)),
                          reads=[xTk] + (wkeys if ti == 0 else []), writes=[pzk])
                return pz, pzk

            def qk_norm(cbs, gain, gaink, is_k, out_bf, okb):
                for b, cb in enumerate(cbs):
                    pz, pzk = proj(cb)
                    sq, sqk = sqr.next()
                    tmp, tmpk = tmpr.next()
                    sta2, sk2 = stat.next()
                    S.add("act", lambda e, sq=sq, pz=pz: e.activation(out=sq[:], in_=pz[:], func=AF.Square),
                          reads=[pzk], writes=[sqk])
                    S.add("dve", lambda e, sq=sq, sta2=sta2: e.tensor_reduce(
                        out=sta2[:, 0:8], in_=sq[:].rearrange("p (g d) -> p g d", d=64), axis=AX.X, op=ALU.add),
                        reads=[sqk], writes=[sk2 + ("ss",)])
                    rstd_from_ss(S, sta2[:, 0:8], sta2[:, 8:16], 64, eps_t, sk2 + ("ss",), sk2 + ("r",))
                    S.add("dve", lambda e, tmp=tmp, pz=pz, sta2=sta2: e.tensor_tensor(
                        out=tmp[:].rearrange("p (g d) -> p g d", d=64), in0=pz[:].rearrange("p (g d) -> p g d", d=64),
                        in1=sta2[:, 8:16].unsqueeze(2).to_broadcast([128, 8, 64]), op=ALU.mult),
                        reads=[pzk, sk2 + ("r",)], writes=[tmpk])
                    cs = slice(b * 512, (b + 1) * 512)
                    gb_ = gain[:].unsqueeze(1).to_broadcast([128, 4, 128])
                    if is_k:
                        kf, kfk = kfr.next()
                        S.add("pool", lambda e, tmp=tmp, kf=kf, gb_=gb_: e.tensor_tensor(
                            out=kf[:].rearrange("p (h c) -> p h c", h=4), in0=tmp[:].rearrange("p (h c) -> p h c", h=4),
                            in1=gb_, op=ALU.mult), reads=[tmpk, gaink], writes=[kfk])
                        S.add("pool", lambda e, cs=cs, kf=kf: e.tensor_copy(out=out_bf[:, cs], in_=kf[:]),
                              reads=[kfk], writes=[okb + (b,)])
                        S.add("sp", lambda e, kf=kf, b=b: e.dma_start(
                            out=o_k[ti * 128:(ti + 1) * 128, b * 512:(b + 1) * 512], in_=kf[:]),
                            reads=[kfk], writes=[("o_k", ti, b)], dma=True)
                    else:
                        S.add("pool", lambda e, tmp=tmp, cs=cs, gb_=gb_: e.tensor_tensor(
                            out=out_bf[:, cs].rearrange("p (h c) -> p h c", h=4),
                            in0=tmp[:].rearrange("p (h c) -> p h c", h=4), in1=gb_, op=ALU.mult),
                            reads=[tmpk, gaink], writes=[okb + (b,)])

            def transposes_to(src, srck, dstring, dram_ap, dkey):
                dt_, dtk = dstring.next()
                for h in range(8):
                    S.add("pe", lambda e, h=h: e.transpose(out=pqk[:, h, :], in_=src[:, h * 128:(h + 1) * 128],
                                                           identity=identb[:]),
                          reads=[srck + (0,), srck + (1,), ("identb",)], writes=[("pqk",)])
                S.add("dve", lambda e: e.tensor_copy(out=dt_[:], in_=pqk[:]), reads=[("pqk",)], writes=[dtk])
                S.add("sp", lambda e: e.dma_start(out=dram_ap, in_=dt_[:]), reads=[dtk], writes=[dkey], dma=True)

            qb, qbk = qbr.next()
            qk_norm((0, 1), qgs, ("qgs",), False, qb, qbk)
            kb, kbk = kbr.next()
            qk_norm((2, 3), kg_t, kgk, True, kb, kbk)
            if pending_ob is not None:
                emit_ob_transposes(pending_ob)
                pending_ob = None
            gv, gvk = gvr.next()
            for b, cb in enumerate((10, 11)):
                pz, pzk = proj(cb)
                S.add("act", lambda e, pz=pz, b=b, gv=gv: e.activation(out=gv[:, b * 512:(b + 1) * 512], in_=pz[:],
                                                                      func=AF.Gelu),
                      reads=[pzk], writes=[gvk + (b,), gvk + ("n",), gvk + ("g",), gvk + ("vb",)] if b == 0 else [gvk + (b,)])
            sta3, sk3 = stat.next()
            for b in range(2):
                S.add("dve", lambda e, b=b, gv=gv, sta3=sta3: e.bn_stats(out=sta3[:, 32 + b * 6:32 + (b + 1) * 6],
                                                                        in_=gv[:, b * 512:(b + 1) * 512]),
                      reads=[gvk + (b,)], writes=[sk3 + ("bs", b)])
            S.add("dve", lambda e, sta3=sta3: e.bn_aggr(out=sta3[:, 48:50], in_=sta3[:, 32:44]),
                  reads=[sk3 + ("bs", 0), sk3 + ("bs", 1)], writes=[sk3 + ("mv",)])
            rstd_from_ss(S, sta3[:, 49:50], sta3[:, 50:51], 1, eps_t, sk3 + ("mv",), sk3 + ("lr",))
            S.add("dve", lambda e, gv=gv, sta3=sta3: e.tensor_scalar(out=gv[:], in0=gv[:], scalar1=sta3[:, 48:49],
                                                                    scalar2=sta3[:, 50:51], op0=ALU.subtract,
                                                                    op1=ALU.mult),
                  reads=[gvk + (0,), gvk + (1,), sk3 + ("lr",), sk3 + ("mv",)], writes=[gvk + ("n",)])
            S.add("pool", lambda e, gv=gv: e.tensor_tensor(out=gv[:], in0=gv[:], in1=lng, op=ALU.mult),
                  reads=[gvk + ("n",), lngk], writes=[gvk + ("g",)])
            S.add("pool", lambda e, gv=gv: e.tensor_tensor(out=gv[:], in0=gv[:], in1=lnb, op=ALU.add),
                  reads=[gvk + ("g",), lnbk], writes=[gvk + ("vb",)])
            vbb, vbbk = vbbr.next()
            S.add("pool", lambda e, gv=gv, vbb=vbb: e.tensor_copy(out=vbb[:], in_=gv[:]), reads=[gvk + ("vb",)],
                  writes=[vbbk])
            if sample:
                S.add("sp", lambda e, gv=gv: e.dma_start(out=o_vb[(ti - NPT) * 128:(ti - NPT + 1) * 128, :], in_=gv[:]),
                      reads=[gvk + ("vb",)], writes=[("o_vb", ti)], dma=True)
            u, uk = ur.next()
            for b, cb in enumerate((8, 9)):
                pz, pzk = proj(cb)
                S.add("act", lambda e, pz=pz, b=b, u=u: e.activation(out=u[:, b * 512:(b + 1) * 512], in_=pz[:],
                                                                    func=AF.Gelu),
                      reads=[pzk], writes=[uk + (b,)])
            transposes_to(qb, qbk, qtr, QT[:, ti, :, :], ("QT", ti))
            if ti + 1 < NT:
                prep(ti + 1)
            gbt, gbk = gbr.next()
            for b, cb in enumerate((12, 13)):
                pz, pzk = proj(cb)
                S.add("act", lambda e, pz=pz, b=b, gbt=gbt: e.activation(out=gbt[:, b * 512:(b + 1) * 512], in_=pz[:],
                                                                        func=AF.Silu),
                      reads=[pzk], writes=[gbk + (b,)])
            transposes_to(kb, kbk, ktr, KT[:, :, ti * 128:(ti + 1) * 128], ("KT", ti))
            ve, vek = ver.next()
            for b, cb in enumerate((4, 5)):
                pz, pzk = proj(cb)
                vf, vfk = vfr.next()
                S.add("act", lambda e, pz=pz, vf=vf: e.activation(out=vf[:], in_=pz[:], func=AF.Copy),
                      reads=[pzk], writes=[vfk])
                S.add("sp", lambda e, vf=vf, b=b: e.dma_start(out=o_v[ti * 128:(ti + 1) * 128, b * 512:(b + 1) * 512],
                                                             in_=vf[:]),
                      reads=[vfk], writes=[("o_v", ti, b)], dma=True)
                S.add("pool", lambda e, vf=vf, ve=ve, b=b: e.tensor_copy(
                    out=ve[:, b * 4:(b + 1) * 4, 0:128], in_=vf[:].rearrange("p (h e) -> p h e", h=4)),
                    reads=[vfk], writes=[vek + (b,)])
            S.add("sp", lambda e, ve=ve: e.dma_start(out=VS[:, :, ti, :].rearrange("h p e -> p h e"), in_=ve[:]),
                  reads=[vek + (0,), vek + (1,), vek + ("ones",)], writes=[("VS", ti)], dma=True)
            wm = wspsb if sample else wspb
            for gi in range(8):
                S.add("pe", lambda e, gi=gi, vbb=vbb: e.matmul(pmx[:, gi * 128:(gi + 1) * 128], lhsT=wm[:, gi, :],
                                                               rhs=vbb[:, gi * 128:(gi + 1) * 128], start=True,
                                                               stop=True),
                      reads=[vbbk, ("wspb",), ("wspsb",)], writes=[("pmx", gi // 4)])
            m1, m1k = m1r.next()
            spb = spbTs_t[:] if sample else spbT_t[:]
            S.add("dve", lambda e, m1=m1: e.tensor_tensor(out=m1[:].rearrange("p (g c) -> p g c", g=8),
                                                          in0=pmx[:].rearrange("p (g c) -> p g c", g=8),
                                                          in1=spb.unsqueeze(2).to_broadcast([128, 8, 128]), op=ALU.add),
                  reads=[("pmx", 0), ("pmx", 1), spbTk, spbTsk], writes=[m1k, m1k + ("u",)])
            S.add("pool", lambda e, m1=m1, u=u: e.tensor_tensor(out=m1[:], in0=m1[:], in1=u[:], op=ALU.mult),
                  reads=[m1k, uk + (0,), uk + (1,)], writes=[m1k + ("u",)])
            ob, obk = obr.next()
            S.add("pool", lambda e, m1=m1, gbt=gbt, ob=ob: e.tensor_tensor(out=ob[:], in0=m1[:], in1=gbt[:],
                                                                           op=ALU.mult),
                  reads=[m1k + ("u",), gbk + (0,), gbk + (1,)], writes=[obk])
            pending_ob = (ob, obk, ti)
            sg, sgk = sgr.next()
            for b, cb in enumerate((6, 7)):
                pz, pzk = proj(cb)
                S.add("act", lambda e, pz=pz, b=b, sg=sg: e.activation(out=sg[:, b * 512:(b + 1) * 512], in_=pz[:],
                                                                      func=AF.Silu),
                      reads=[pzk], writes=[sgk + (b,)])
            S.add("sp", lambda e, sg=sg: e.dma_start(out=SGA[ti * 128:(ti + 1) * 128, :], in_=sg[:]),
                  reads=[sgk + (0,), sgk + (1,)], writes=[("SGA", ti)], dma=True)

        prep(0)
        for ti_ in range(NT):
            tile_body(ti_)
        emit_ob_transposes(pending_ob)
        S.emit()


def dram_view(ap, offset, pattern):
    return bass.AP(tensor=ap.tensor, offset=offset, ap=[list(x) for x in pattern])


def phase_b(nc, S, sb, ps, ldpar, ldcst, identb, eps_t, neglam, g, write_y):
    xin, w_out0, o_y = g["xin"], g["w_out0"], g["o_y"]
    QT, KT, VS, SGA, OBT, X1, GV = g["QT"], g["KT"], g["VS"], g["SGA"], g["OBT"], g["X1"], g["GV"]
    NQT = NPT // 4
    with ExitStack() as st:
        Wo = sb(st, "Wo", [128, 16, D], BF16)
        for kc in range(16):
            S.add("pool", lambda e, kc=kc: e.dma_start(out=Wo[:, kc, :], in_=w_out0[kc * 128:(kc + 1) * 128, :]),
                  writes=[("Wo", kc)], dma=True)
        wokeys = [("Wo", kc) for kc in range(16)]
        M = sb(st, "M", [128, 8, 1024], BF16)
        mstage = sb(st, "mstage", [128, 1024], F32)
        for h in range(8):
            S.add("sp", lambda e, h=h: e.dma_start(out=mstage[:], in_=dram_view(GV, h * NG + 1023, [[1, 128], [-1, 1024]])),
                  writes=[("mstage",)], dma=True)
            S.add("dve", lambda e, h=h: e.tensor_copy(out=M[:, h, :], in_=mstage[:]), reads=[("mstage",)],
                  writes=[("M", h)])
        subg, subgk = ldpar(st, "subg")
        S.add("dve", lambda e: e.tensor_scalar(out=subg[:], in0=subg[:], scalar1=1.0 - LAM_INIT, scalar2=None,
                                               op0=ALU.mult), reads=[subgk], writes=[subgk])

        def ring(name, n, shape, dt):
            return Ring([sb(st, "%s%d" % (name, i), shape, dt) for i in range(n)], name)

        qTr = ring("bqT", 2, [128, 4, 8, 128], BF16)
        sgar = ring("bsga", 1, [128, 4, D], BF16)
        kTr = ring("bkT", 2, [128, SEQ], BF16)
        vEr = ring("bvE", 2, [128, NPT, 130], BF16)
        Ptr = ring("bPt", 3, [128, 2, 512], BF16)
        osbr = ring("bosb", 2, [128, 3, 512], F32)
        rlr = ring("brl", 2, [128, 16], F32)
        Ar = ring("bA", 4, [128, 128], F32)
        A2r = ring("bA2", 4, [128, 128], F32)
        str_ = ring("bst", 8, [128, 4], F32)
        junk = sb(st, "bjunk", [128, 128], F32)
        oar = ring("boa", 1, [128, 4, D], BF16)
        oaTr = ring("boaT", 2, [128, 8, 128], BF16)
        obTr = ring("bobT", 2, [128, 8, 128], BF16)
        xr = ring("bx", 2, [128, D], F32)
        x1r = ring("bx1", 2, [128, 512], F32)

        pS2 = Ring([ps(st, "bpS%d" % i, [128, 2, 512], F32) for i in range(2)], "bpS")
        pO = ps(st, "bpO", [128, 3, 512], F32)

        class _Half:
            def __init__(self):
                self.cur = None
                self.h = 2

            def next(self):
                if self.h == 2:
                    self.cur = pS2.next()
                    self.h = 0
                t_, k_ = self.cur
                r = (t_[:, self.h, :], k_)
                self.h += 1
                return r
        pS = _Half()
        pqk = ps(st, "bpqk", [128, 8, 128], BF16)

        def acc(a):
            return pO[:, a // 3, (a % 3) * 129:(a % 3) * 129 + 129]

        def unit(t, h, qT, qTk, sga, sgak, oa, oak):
            nkb = 4 * t + 4
            kT, kTk = kTr.next()
            vE, vEk = vEr.next()
            S.add("sp", lambda e: e.dma_start(out=kT[:, 0:nkb * 128], in_=KT[:, h, 0:nkb * 128]),
                  reads=[("KTs",)], writes=[kTk], dma=True)
            S.add("sp", lambda e: e.dma_start(out=vE[:, 0:nkb, :], in_=VS[h, :, 0:nkb, :]),
                  reads=[("VSs",)], writes=[vEk], dma=True)
            def qk(j):
                near = j >= 4 * t - 1
                pz, pzk = pS2.next()
                for c in range(2):
                    S.add("pe", lambda e, c=c, pz=pz, j=j: e.matmul(
                        pz[:, c, :], lhsT=kT[c * 64:(c + 1) * 64, j * 128:(j + 1) * 128],
                        rhs=qT[c * 64:(c + 1) * 64, :, h, :], start=True, stop=True),
                        reads=[kTk, qTk], writes=[pzk])
                Pt, Ptk = Ptr.next()
                S.add("act", lambda e, pz=pz, Pt=Pt: e.activation(out=Pt[:], in_=pz[:], func=AF.Exp),
                      reads=[pzk], writes=[Ptk + (0,), Ptk + (1,)])
                if near:
                    u0 = 512 * t - 128 * j + 384
                    S.add("dve", lambda e, Pt=Pt, u0=u0: e.tensor_tensor(
                        out=Pt[:], in0=Pt[:], in1=M[:, h, u0:u0 + 512].unsqueeze(1).to_broadcast([128, 2, 512]),
                        op=ALU.mult), reads=[Ptk + (0,), Ptk + (1,), ("M", h)], writes=[Ptk + (0,), Ptk + (1,)])
                return Pt, Ptk

            def av(j, Pt, Ptk):
                for c in range(2):
                    for s_ in range(4):
                        if j > 4 * t + s_:
                            continue
                        a = c * 4 + s_
                        S.add("pe", lambda e, c=c, s_=s_, a=a, Pt=Pt, j=j: e.matmul(
                            acc(a), lhsT=Pt[:, c, s_ * 128:(s_ + 1) * 128], rhs=vE[:, j, 0:129],
                            start=(j == 0 and a % 3 == 0), stop=(j == 4 * t + s_), skip_group_check=True),
                            reads=[Ptk + (c,), vEk], writes=[("bpO", a // 3)])

            pend = qk(0)
            for j in range(nkb):
                nxt = qk(j + 1) if j + 1 < nkb else None
                av(j, *pend)
                pend = nxt
            osb, osbk = osbr.next()
            rl, rlk = rlr.next()
            for b in range(3):
                eng = "dve" if b != 1 else "pool_na"
                w = 387 if b < 2 else 258
                S.add("dve", lambda e, b=b, w=w, osb=osb: e.tensor_copy(out=osb[:, b, 0:w], in_=pO[:, b, 0:w]),
                      reads=[("bpO", b)], writes=[osbk + (b,)])
                n = 3 if b < 2 else 2
                S.add("dve", lambda e, b=b, n=n, osb=osb, rl=rl: e.reciprocal(
                    out=rl[:, b * 3:b * 3 + n],
                    in_=osb[:, b, 0:n * 129].rearrange("p (a e) -> p a e", e=129)[:, :, 128]),
                    reads=[osbk + (b,)], writes=[rlk + (b,)])
            S.add("dve", lambda e, rl=rl: e.tensor_scalar(out=rl[:, 8:12], in0=rl[:, 4:8], scalar1=neglam[:, 0:1],
                                                         scalar2=None, op0=ALU.mult),
                  reads=[rlk + (1,), rlk + (2,), ("neglam",)], writes=[rlk + ("n",)])

            def oview(a, osb=osb):
                return osb[:, a // 3, (a % 3) * 129:(a % 3) * 129 + 128]

            for s_ in range(4):
                A, Ak = Ar.next()
                A2, A2k = A2r.next()
                sta, stk = str_.next()
                a1, a2 = s_, 4 + s_
                S.add("dve", lambda e, A=A, a1=a1, s_=s_, rl=rl: e.tensor_scalar(
                    out=A[:], in0=oview(a1), scalar1=rl[:, a1:a1 + 1], scalar2=None, op0=ALU.mult),
                    reads=[osbk + (a1 // 3,), rlk + (a1 // 3,)], writes=[Ak])
                S.add("dve", lambda e, A=A, A2=A2, a2=a2, s_=s_, rl=rl: e.scalar_tensor_tensor(
                    out=A2[:], in0=oview(a2), scalar=rl[:, 8 + s_:9 + s_], in1=A[:], op0=ALU.mult, op1=ALU.add),
                    reads=[osbk + (a2 // 3,), rlk + ("n",), Ak], writes=[A2k, A2k + ("n",)])
                S.add("act", lambda e, A2=A2, sta=sta: e.activation(out=junk[:], in_=A2[:], func=AF.Square,
                                                                    accum_out=sta[:, 0:1]),
                      reads=[A2k], writes=[stk + ("ss",), ("bjunk",)])
                rstd_from_ss(S, sta[:, 0:1], sta[:, 1:2], 128, eps_t, stk + ("ss",), stk + ("r",))
                S.add("dve", lambda e, A2=A2, sta=sta: e.scalar_tensor_tensor(
                    out=A2[:], in0=A2[:], scalar=sta[:, 1:2], in1=subg[:], op0=ALU.mult, op1=ALU.mult),
                    reads=[A2k, stk + ("r",), subgk], writes=[A2k + ("n",)])
                S.add("pool", lambda e, A2=A2, s_=s_: e.tensor_tensor(
                    out=oa[:, s_, h * 128:(h + 1) * 128], in0=A2[:], in1=sga[:, s_, h * 128:(h + 1) * 128],
                    op=ALU.mult), reads=[A2k + ("n",), sgak], writes=[oak + (s_, h)])

        def out_proj(t, oa, oak):
            for s_ in range(4):
                ti = 4 * t + s_
                oaT, oaTk = oaTr.next()
                obT, obTk = obTr.next()
                xt, xk = xr.next()
                S.add("sp", lambda e, ti=ti, obT=obT: e.dma_start(out=obT[:], in_=OBT[:, ti, :, :]),
                      reads=[("OBTs",)], writes=[obTk], dma=True)
                S.add("sp", lambda e, ti=ti, xt=xt: e.dma_start(out=xt[:], in_=xin[ti * 128:(ti + 1) * 128, :]),
                      writes=[xk], dma=True)
                for h in range(8):
                    S.add("pe", lambda e, h=h, s_=s_: e.transpose(out=pqk[:, h, :], in_=oa[:, s_, h * 128:(h + 1) * 128],
                                                                  identity=identb[:]),
                          reads=[oak + (s_, h), ("identb",)], writes=[("bpqk",)])
                S.add("dve", lambda e, oaT=oaT: e.tensor_copy(out=oaT[:], in_=pqk[:]), reads=[("bpqk",)],
                      writes=[oaTk])
                for nb in range(2):
                    pz, pzk = pS.next()
                    for kc in range(16):
                        lt = oaT[:, kc, :] if kc < 8 else obT[:, kc - 8, :]
                        S.add("pe", lambda e, kc=kc, lt=lt, pz=pz, nb=nb: e.matmul(
                            pz[:], lhsT=lt, rhs=Wo[:, kc, nb * 512:(nb + 1) * 512], start=(kc == 0), stop=(kc == 15)),
                            reads=[oaTk, obTk] + (wokeys if (t == 0 and s_ == 0) else []), writes=[pzk])
                    x1, x1k = x1r.next()
                    S.add("dve", lambda e, pz=pz, nb=nb, xt=xt, x1=x1: e.tensor_tensor(
                        out=x1[:], in0=pz[:], in1=xt[:, nb * 512:(nb + 1) * 512], op=ALU.add),
                        reads=[pzk, xk], writes=[x1k])
                    S.add("sp", lambda e, x1=x1, ti=ti, nb=nb: e.dma_start(
                        out=X1[ti * 128:(ti + 1) * 128, nb * 512:(nb + 1) * 512], in_=x1[:]),
                        reads=[x1k], writes=[("X1", ti, nb)], dma=True)
                    if write_y:
                        S.add("sp", lambda e, x1=x1, ti=ti, nb=nb: e.dma_start(
                            out=o_y[ti * 128:(ti + 1) * 128, nb * 512:(nb + 1) * 512], in_=x1[:]),
                            reads=[x1k], writes=[("o_y", ti, nb)], dma=True)

        for t in range(NQT):
            qT, qTk = qTr.next()
            sga, sgak = sgar.next()
            oa, oak = oar.next()
            S.add("sp", lambda e, t=t, qT=qT: e.dma_start(out=qT[:], in_=QT[:, 4 * t:4 * t + 4, :, :]),
                  writes=[qTk], dma=True)
            S.add("sp", lambda e, t=t, sga=sga: e.dma_start(
                out=sga[:], in_=SGA[t * 512:(t + 1) * 512, :].rearrange("(s p) f -> p s f", p=128)),
                writes=[sgak], dma=True)
            for h in range(8):
                unit(t, h, qT, qTk, sga, sgak, oa, oak)
            out_proj(t, oa, oak)
        S.emit()


def phase_b2(nc, S, sb, ps, ldpar, ldcst, identb, eps_t, neglam, g, write_y):
    xin, w_out0, o_y, cache_k, cache_v, ptab = g["xin"], g["w_out0"], g["o_y"], g["cache_k"], g["cache_v"], g["ptab"]
    QT, KT, VS, SGA, OBT, X1, GV = g["QT"], g["KT"], g["VS"], g["SGA"], g["OBT"], g["X1"], g["GV"]
    NSEQ = NST * SSEQ
    with ExitStack() as st:
        Wo = sb(st, "sWo", [128, 16, D], BF16)
        for kc in range(16):
            S.add("pool", lambda e, kc=kc: e.dma_start(out=Wo[:, kc, :], in_=w_out0[kc * 128:(kc + 1) * 128, :]),
                  writes=[("Wo", kc)], dma=True)
        wokeys = [("Wo", kc) for kc in range(16)]
        subg, subgk = ldpar(st, "subg", 8)
        S.add("dve", lambda e: e.tensor_scalar(out=subg[:], in0=subg[:], scalar1=1.0 - LAM_INIT, scalar2=None,
                                               op0=ALU.mult), reads=[subgk], writes=[subgk])
        iot, iotk = ldcst(st, "iota")
        sel8, sel8k = ldcst(st, "sel8", 16)
        c01, c01k = ldcst(st, "c01", 16)
        SMn = sb(st, "SMn", [128, 8, 8], F32)
        SN = sb(st, "SN", [8, 8, 8], F32)
        for h in range(8):
            S.add("sp", lambda e, h=h: e.dma_start(out=SMn[:, h, :],
                                                  in_=dram_view(GV, h * NG + GOFF - 128, [[1, 128], [-1, 8]])),
                  writes=[("SMn", h)], dma=True)
            S.add("sp", lambda e, h=h: e.dma_start(out=SN[:, h, :],
                                                  in_=dram_view(GV, h * NG + GOFF, [[1, 8], [-1, 8]])),
                  writes=[("SN", h)], dma=True)
        mkeys = [("SMn", h) for h in range(8)] + [("SN", h) for h in range(8)]
        cw = sb(st, "cw", [16, 1], F32)
        S.add("dve", lambda e: e.tensor_scalar(out=cw[:], in0=neglam[0:16, :], scalar1=-1.0, scalar2=None, op0=ALU.add),
              reads=[("neglam",)], writes=[("cw",)])
        S.add("dve", lambda e: e.tensor_tensor(out=cw[:], in0=cw[:], in1=c01[:], op=ALU.mult), reads=[("cw",), c01k],
              writes=[("cw",)])
        S.add("dve", lambda e: e.tensor_scalar(out=cw[:], in0=cw[:], scalar1=1.0, scalar2=None, op0=ALU.add),
              reads=[("cw",)], writes=[("cw",)])
        pt_i = sb(st, "pt_i", [128, NSEQ * NPAGES], I32)
        idx = sb(st, "idx", [128, NSEQ * NPAGES], I32)
        S.add("sp", lambda e: e.dma_start(out=pt_i[:], in_=ptab), writes=[("pt_i",)], dma=True)
        S.add("dve", lambda e: e.tensor_scalar(out=idx[:], in0=pt_i[:], scalar1=128.0, scalar2=iot[:, 0:1], op0=ALU.mult,
                                               op1=ALU.add), reads=[("pt_i",), iotk], writes=[("idx",)])

        def ring(name, n, shape, dt):
            return Ring([sb(st, "%s%d" % (name, i), shape, dt) for i in range(n)], name)

        kpfr = ring("skpf", 2, [128, 1024], F32)
        vpfr = ring("svpf", 2, [128, 1024], F32)
        kpgr = ring("skpg", 3, [128, 1024], BF16)
        vpgr = ring("svpg", 3, [128, 8, 130], BF16)
        for i, r in enumerate(vpgr.tiles):
            S.add("pool", lambda e, r=r: e.memset(r[:, :, 128:130], 1.0), writes=[("svpg", i, "ones")])
        kTpr = ring("skTp", 2, [128, 8, 128], BF16)
        Ppr = ring("sPp", 3, [128, 128], BF16)
        Qblk = sb(st, "sQblk", [128, SSEQ, 8, 16], BF16)
        qTs = sb(st, "sqTs", [128, 8, 128], BF16)
        kTs = sb(st, "skTs", [128, 8, 128], BF16)
        vnr = ring("svn", 2, [8, 8, 130], BF16)
        sgsr = ring("ssg", 2, [8, D], BF16)
        Pnr = ring("sPn", 2, [8, 128], BF16)
        osr = ring("sos", 2, [16, 8, 129], F32)
        rlr = ring("srl", 2, [16, 8], F32)
        os2r = ring("sos2", 2, [16, 8, 128], F32)
        ar = ring("sa", 2, [8, D], F32)
        sqr = ring("ssq", 1, [8, D], F32)
        str_ = ring("sst", 4, [8, 16], F32)
        oar = ring("soa", 2, [8, D], BF16)
        oaTr = ring("soaT", 1, [128, 8, 128], BF16)
        obTr = ring("sobT", 1, [128, 8, 128], BF16)
        xr = ring("sx", 1, [128, D], F32)
        x1r = ring("sx1", 2, [128, 512], F32)

        pS = Ring([ps(st, "spS%d" % i, [128, 512], F32) for i in range(4)], "spA")
        pO = ps(st, "spO", [128, 3, 512], F32)
        pqk = ps(st, "spT", [128, 8, 128], BF16)

        def acc(a):
            return pO[0:16, a // 3, (a % 3) * 129:(a % 3) * 129 + 129]

        for stl in range(NST):
            ti = NPT + stl
            S.add("sp", lambda e, ti=ti: e.dma_start(out=qTs[:], in_=QT[:, ti, :, :]), writes=[("qTs",)], dma=True)
            S.add("sp", lambda e, ti=ti: e.dma_start(out=kTs[:], in_=KT[:, :, ti * 128:(ti + 1) * 128]),
                  writes=[("kTs",)], dma=True)
            S.add("pool", lambda e: e.memset(Qblk[:], 0.0), writes=[("Qblk",)])
            for c in range(2):
                for b in range(SSEQ):
                    S.add("pool", lambda e, c=c, b=b: e.tensor_copy(
                        out=Qblk[c * 64:(c + 1) * 64, b, :, c * 8:(c + 1) * 8],
                        in_=qTs[c * 64:(c + 1) * 64, :, b * 8:(b + 1) * 8]), reads=[("qTs",), ("Qblk",)],
                        writes=[("Qblk",)])
            oaT, oaTk = oaTr.next()
            for b in range(SSEQ):
                sq_ = stl * SSEQ + b
                vn, vnk = vnr.next()
                sgs, sgsk = sgsr.next()
                S.add("sp", lambda e, ti=ti, b=b, vn=vn: e.dma_start(
                    out=vn[:], in_=VS[:, b * 8:(b + 1) * 8, ti, :].rearrange("h p e -> p h e")), writes=[vnk], dma=True)
                S.add("sp", lambda e, ti=ti, b=b, sgs=sgs: e.dma_start(
                    out=sgs[:], in_=SGA[ti * 128 + b * 8:ti * 128 + b * 8 + 8, :]), writes=[sgsk], dma=True)
                pz = None
                for j in range(NPAGES):
                    n = sq_ * NPAGES + j
                    kpg, kpgk = kpgr.next()
                    vpg, vpgk = vpgr.next()
                    kpf, kpfk = kpfr.next()
                    vpf, vpfk = vpfr.next()
                    S.add("pool", lambda e, n=n, kpf=kpf: e.indirect_dma_start(
                        out=kpf[:], out_offset=None, in_=cache_k,
                        in_offset=bass.IndirectOffsetOnAxis(ap=idx[:, n:n + 1], axis=0)),
                        reads=[("idx",)], writes=[kpfk], dma=True)
                    S.add("pool", lambda e, n=n, vpf=vpf: e.indirect_dma_start(
                        out=vpf[:], out_offset=None, in_=cache_v,
                        in_offset=bass.IndirectOffsetOnAxis(ap=idx[:, n:n + 1], axis=0)),
                        reads=[("idx",)], writes=[vpfk], dma=True)
                    S.add("act", lambda e, kpf=kpf, kpg=kpg: e.activation(out=kpg[:], in_=kpf[:], func=AF.Copy),
                          reads=[kpfk], writes=[kpgk])
                    S.add("dve", lambda e, vpf=vpf, vpg=vpg: e.tensor_copy(
                        out=vpg[:, :, 0:128], in_=vpf[:].rearrange("p (h e) -> p h e", h=8)), reads=[vpfk],
                        writes=[vpgk])
                    kTp, kTpk = kTpr.next()
                    for h in range(8):
                        S.add("pe", lambda e, h=h, kpg=kpg: e.transpose(out=pqk[:, h, :], in_=kpg[:, h * 128:(h + 1) * 128],
                                                                        identity=identb[:]),
                              reads=[kpgk, ("identb",)], writes=[("spT",)])
                    S.add("dve" if j % 2 == 0 else "act",
                          (lambda e, kTp=kTp: e.tensor_copy(out=kTp[:], in_=pqk[:])) if j % 2 == 0 else
                          (lambda e, kTp=kTp: e.activation(out=kTp[:], in_=pqk[:], func=AF.Copy)),
                          reads=[("spT",)], writes=[kTpk])
                    if j % 4 == 0:
                        pz, pzk = pS.next()
                    for h in range(8):
                        S.add("pe", lambda e, h=h, j=j, b=b, kTp=kTp, pz=pz: e.matmul(
                            pz[:, (j % 4) * 128 + h * 16:(j % 4) * 128 + (h + 1) * 16], lhsT=kTp[:, h, :],
                            rhs=Qblk[:, b, h, :], start=True, stop=True, skip_group_check=True),
                            reads=[kTpk, ("Qblk",)], writes=[pzk])
                    Pp, Ppk = Ppr.next()
                    S.add("act", lambda e, j=j, pz=pz, Pp=Pp: e.activation(out=Pp[:], in_=pz[:, (j % 4) * 128:(j % 4 + 1) * 128],
                                                                          func=AF.Exp), reads=[pzk], writes=[Ppk])
                    if j == NPAGES - 1:
                        S.add("dve", lambda e, Pp=Pp: e.tensor_tensor(
                            out=Pp[:].rearrange("p (h c i) -> p h c i", h=8, c=2),
                            in0=Pp[:].rearrange("p (h c i) -> p h c i", h=8, c=2),
                            in1=SMn[:].unsqueeze(2).to_broadcast([128, 8, 2, 8]), op=ALU.mult),
                            reads=[Ppk] + mkeys, writes=[Ppk])
                    for h in range(8):
                        S.add("pe", lambda e, h=h, j=j, Pp=Pp, vpg=vpg: e.matmul(
                            acc(h), lhsT=Pp[:, h * 16:(h + 1) * 16], rhs=vpg[:, h, 0:129],
                            start=(j == 0 and h % 3 == 0), stop=False, skip_group_check=True),
                            reads=[Ppk, vpgk, vpgk + ("ones",)], writes=[("spO", h // 3)])
                pz, pzk = pS.next()
                for h in range(8):
                    S.add("pe", lambda e, h=h, b=b, pz=pz: e.matmul(pz[0:8, h * 16:(h + 1) * 16],
                                                                   lhsT=kTs[:, h, b * 8:(b + 1) * 8], rhs=Qblk[:, b, h, :],
                                                                   start=True, stop=True, skip_group_check=True),
                          reads=[("kTs",), ("Qblk",)], writes=[pzk])
                Pn, Pnk = Pnr.next()
                S.add("act", lambda e, pz=pz, Pn=Pn: e.activation(out=Pn[:], in_=pz[0:8, 0:128], func=AF.Exp),
                      reads=[pzk], writes=[Pnk])
                S.add("dve", lambda e, Pn=Pn: e.tensor_tensor(
                    out=Pn[:].rearrange("p (h c i) -> p h c i", h=8, c=2),
                    in0=Pn[:].rearrange("p (h c i) -> p h c i", h=8, c=2),
                    in1=SN[:].unsqueeze(2).to_broadcast([8, 8, 2, 8]), op=ALU.mult),
                      reads=[Pnk] + mkeys, writes=[Pnk])
                for h in range(8):
                    S.add("pe", lambda e, h=h, Pn=Pn, vn=vn: e.matmul(acc(h), lhsT=Pn[:, h * 16:(h + 1) * 16],
                                                                      rhs=vn[:, h, 0:129], start=False, stop=True,
                                                                      skip_group_check=True),
                          reads=[Pnk, vnk], writes=[("spO", h // 3)])
                osb, osbk = osr.next()
                rl, rlk = rlr.next()
                for bk in range(3):
                    n_ = 3 if bk < 2 else 2
                    S.add("dve", lambda e, bk=bk, n_=n_, osb=osb: e.tensor_copy(
                        out=osb[:, bk * 3:bk * 3 + n_, :],
                        in_=pO[0:16, bk, 0:n_ * 129].rearrange("p (a e) -> p a e", e=129)),
                        reads=[("spO", bk)], writes=[osbk + (bk,)])
                S.add("dve", lambda e, osb=osb, rl=rl: e.reciprocal(out=rl[:], in_=osb[:, :, 128]),
                      reads=[osbk + (0,), osbk + (1,), osbk + (2,)], writes=[rlk])
                S.add("dve", lambda e, rl=rl: e.tensor_scalar(out=rl[:], in0=rl[:], scalar1=cw[:, 0:1], scalar2=None,
                                                              op0=ALU.mult), reads=[rlk, ("cw",)], writes=[rlk + ("w",)])
                os2, os2k = os2r.next()
                S.add("dve", lambda e, osb=osb, rl=rl, os2=os2: e.tensor_tensor(
                    out=os2[:], in0=osb[:, :, 0:128], in1=rl[:].unsqueeze(2).to_broadcast([16, 8, 128]), op=ALU.mult),
                    reads=[osbk + (0,), osbk + (1,), osbk + (2,), rlk + ("w",)], writes=[os2k])
                a_, ak = ar.next()
                for nb in range(2):
                    pa, pak = pS.next()
                    S.add("pe", lambda e, nb=nb, pa=pa, os2=os2: e.matmul(
                        pa[0:8, :], lhsT=sel8[:], rhs=os2[:].rearrange("p h e -> p (h e)")[:, nb * 512:(nb + 1) * 512],
                        start=True, stop=True), reads=[os2k, sel8k], writes=[pak])
                    S.add("dve", lambda e, nb=nb, pa=pa, a_=a_: e.tensor_copy(out=a_[:, nb * 512:(nb + 1) * 512],
                                                                            in_=pa[0:8, :]),
                          reads=[pak], writes=[ak + (nb,)])
                sq, sqk = sqr.next()
                sta, stk = str_.next()
                S.add("act", lambda e, a_=a_, sq=sq: e.activation(out=sq[:], in_=a_[:], func=AF.Square),
                      reads=[ak + (0,), ak + (1,)], writes=[sqk])
                S.add("dve", lambda e, sq=sq, sta=sta: e.tensor_reduce(out=sta[:, 0:8],
                                                                       in_=sq[:].rearrange("p (h e) -> p h e", h=8),
                                                                       axis=AX.X, op=ALU.add), reads=[sqk],
                      writes=[stk + ("ss",)])
                S.add("act", lambda e, sta=sta: e.activation(out=sta[:, 8:16], in_=sta[:, 0:8], func=AF.Sqrt,
                                                             bias=eps_t[0:8, :], scale=1.0 / 128),
                      reads=[stk + ("ss",), ("eps",)], writes=[stk + ("sd",)])
                S.add("dve", lambda e, sta=sta: e.reciprocal(out=sta[:, 8:16], in_=sta[:, 8:16]), reads=[stk + ("sd",)],
                      writes=[stk + ("r",)])
                S.add("dve", lambda e, a_=a_, sta=sta: e.tensor_tensor(
                    out=a_[:].rearrange("p (h e) -> p h e", h=8), in0=a_[:].rearrange("p (h e) -> p h e", h=8),
                    in1=sta[:, 8:16].unsqueeze(2).to_broadcast([8, 8, 128]), op=ALU.mult),
                    reads=[ak + (0,), ak + (1,), stk + ("r",)], writes=[ak + ("n",)])
                S.add("pool", lambda e, a_=a_: e.tensor_tensor(
                    out=a_[:].rearrange("p (h e) -> p h e", h=8), in0=a_[:].rearrange("p (h e) -> p h e", h=8),
                    in1=subg[:].unsqueeze(1).to_broadcast([8, 8, 128]), op=ALU.mult),
                    reads=[ak + ("n",), subgk], writes=[ak + ("g",)])
                oa, oak = oar.next()
                S.add("pool", lambda e, a_=a_, sgs=sgs, oa=oa: e.tensor_tensor(out=oa[:], in0=a_[:], in1=sgs[:],
                                                                              op=ALU.mult),
                      reads=[ak + ("g",), sgsk], writes=[oak])
                for h in range(8):
                    S.add("pe", lambda e, h=h, oa=oa: e.transpose(out=pqk[:, h, 0:8], in_=oa[:, h * 128:(h + 1) * 128],
                                                                  identity=identb[0:8, 0:8]),
                          reads=[oak, ("identb",)], writes=[("spT",)])
                S.add("dve", lambda e, b=b: e.tensor_copy(out=oaT[:, :, b * 8:(b + 1) * 8], in_=pqk[:, :, 0:8]),
                      reads=[("spT",)], writes=[oaTk + (b,)])
            obT, obTk = obTr.next()
            xt, xk = xr.next()
            S.add("sp", lambda e, ti=ti, obT=obT: e.dma_start(out=obT[:], in_=OBT[:, ti, :, :]), writes=[obTk], dma=True)
            S.add("sp", lambda e, ti=ti, xt=xt: e.dma_start(out=xt[:], in_=xin[ti * 128:(ti + 1) * 128, :]), writes=[xk],
                  dma=True)
            for nb in range(2):
                pz, pzk = pS.next()
                for kc in range(16):
                    lt = oaT[:, kc, :] if kc < 8 else obT[:, kc - 8, :]
                    S.add("pe", lambda e, kc=kc, lt=lt, pz=pz, nb=nb: e.matmul(
                        pz[:], lhsT=lt, rhs=Wo[:, kc, nb * 512:(nb + 1) * 512], start=(kc == 0), stop=(kc == 15)),
                        reads=[oaTk + (b_,) for b_ in range(SSEQ)] + [obTk] + (wokeys if stl == 0 else []), writes=[pzk])
                x1, x1k = x1r.next()
                S.add("dve", lambda e, pz=pz, nb=nb, xt=xt, x1=x1: e.tensor_tensor(
                    out=x1[:], in0=pz[:], in1=xt[:, nb * 512:(nb + 1) * 512], op=ALU.add), reads=[pzk, xk], writes=[x1k])
                S.add("sp", lambda e, x1=x1, ti=ti, nb=nb: e.dma_start(
                    out=X1[ti * 128:(ti + 1) * 128, nb * 512:(nb + 1) * 512], in_=x1[:]), reads=[x1k],
                    writes=[("X1", ti, nb)], dma=True)
                if write_y:
                    S.add("sp", lambda e, x1=x1, ti=ti, nb=nb: e.dma_start(
                        out=o_y[ti * 128:(ti + 1) * 128, nb * 512:(nb + 1) * 512], in_=x1[:]), reads=[x1k],
                        writes=[("o_y", ti, nb)], dma=True)
        S.emit()


def phase_c(nc, S, sb, ps, ldpar, ldcst, identb, eps_t, g):
    KCUT = float(os.environ.get("KCUT", "99"))
    w_in1, w_out1, o_y, o_gp, o_gs, state, X1 = (g["w_in1"], g["w_out1"], g["o_y"], g["o_gp"], g["o_gs"], g["state"],
                                                  g["X1"])
    par = g["par"]
    with ExitStack() as st:
        W1 = sb(st, "W1", [128, 8, IN1], BF16)
        Wo1 = sb(st, "Wo1", [128, 8, D], BF16)
        wg = sb(st, "wg", [16, 512], BF16)
        for kc in range(8):
            for cb in range(2):
                S.add("pool", lambda e, kc=kc, cb=cb: e.dma_start(
                    out=W1[:, kc, cb * 1544:(cb + 1) * 1544],
                    in_=w_in1[kc * 128:(kc + 1) * 128, cb * 1544:(cb + 1) * 1544]), writes=[("W1", kc, cb)], dma=True)
            S.add("pool", lambda e, kc=kc: e.dma_start(out=Wo1[:, kc, :], in_=w_out1[kc * 128:(kc + 1) * 128, :]),
                  writes=[("Wo1", kc)], dma=True)
        o_wg = PAR["wgate"][0]
        S.add("pool", lambda e: e.dma_start(out=wg[:], in_=par[0:16, o_wg:o_wg + 512]), writes=[("wg",)], dma=True)
        wkeys = [("W1", kc, cb) for kc in range(8) for cb in range(2)] + [("Wo1", kc) for kc in range(8)] + [("wg",)]
        g1T, g1Tk = ldpar(st, "g1T")
        bgate, bgk = ldpar(st, "bgate")
        glag, glk = ldpar(st, "glag")
        Lp, Lpk = ldcst(st, "Lp")
        Ls, Lsk = ldcst(st, "Ls")
        Ap, Apk = ldcst(st, "Ap")
        As, Ask = ldcst(st, "As")
        selp, selpk = ldcst(st, "selp")
        sels, selsk = ldcst(st, "sels")
        ohs, ohsk = ldcst(st, "ohs")
        cks = [g1Tk, bgk, glk, Lpk, Lsk, Apk, Ask, selpk, selsk, ohsk]

        def ring(name, n, shape, dt):
            return Ring([sb(st, "%s%d" % (name, i), shape, dt) for i in range(n)], name)

        Sst = sb(st, "Sst", [128, 4, 256], F32)
        Sb = sb(st, "Sbb", [128, 4, 256], BF16)
        S.add("dve", lambda e: e.memset(Sst[:], 0.0), writes=[("S",)])
        S.add("dve", lambda e: e.memset(Sb[:], 0.0), writes=[("Sb",)])
        xr = ring("cx", 2, [128, D], F32)
        xsr = ring("cxs", 1, [128, D], BF16)
        xtr = ring("cxT", 2, [128, 8, 128], BF16)
        junk = sb(st, "cjunk", [128, D], BF16)
        stat = ring("cst", 6, [128, 16], F32)
        zaTr = ring("czaT", 1, [16, 128], BF16)
        t1r = ring("ct1", 1, [128, 512], F32)
        bcsr = ring("cbcs", 1, [128, 512], F32)
        ebr = ring("ceb", 1, [128, 512], F32)
        enbr = ring("cenb", 1, [128, 512], F32)
        qtr_ = ring("cqt", 1, [128, 512], BF16)
        ktr_ = ring("ckt", 1, [128, 512], BF16)
        qTr = ring("cqT", 1, [128, 4, 128], BF16)
        kTr = ring("ckT", 1, [128, 4, 128], BF16)
        ELr = ring("cEL", 1, [128, 4, 16], F32)
        vr = ring("cv", 1, [128, D], BF16)
        sgr = ring("csg", 1, [128, D], BF16)
        attr_ = ring("catt", 1, [128, 4, 128], BF16)
        tmpSr = ring("ctmpS", 1, [128, 4, 256], F32)
        sqr = ring("csq", 1, [128, D], F32)
        o1r = ring("co1", 1, [128, D], BF16)
        oTr = ring("coT", 1, [128, 8, 128], BF16)
        yr = ring("cy", 2, [128, 512], F32)
        QZ = sb(st, "cQZ", [128, 16 * 136], BF16)
        KZr = ring("cKZ", 2, [128, 16, 128], BF16)
        s0r = ring("cs0", 8, [128, 1, 256], F32)
        s0br = ring("cs0b", 8, [128, 1, 256], BF16)
        snr = ring("csn", 6, [128, 1, 256], F32)
        S.add("pool", lambda e: e.memset(QZ[:], 0.0), writes=[("QZ",)])

        ptr = ps(st, "cptr", [128, 8, 128], BF16)
        pzr = Ring([ps(st, "cpz%d" % i, [128, 512], F32) for i in range(2)], "cpz")
        pA = ps(st, "cpA", [128, 4, 128], F32)
        pOg = ps(st, "cpOg", [128, 4, 256], F32)
        pD = ps(st, "cpD", [128, 4, 256], F32)

        def load_x(ti):
            xt, xk = xr.next()
            S.add("sp", lambda e: e.dma_start(out=xt[:], in_=X1[ti * 128:(ti + 1) * 128, :]), writes=[xk], dma=True)
            return xt, xk

        xq = [load_x(0)]

        def tile_body(ti):
            sample = ti >= NPT
            first = ti == 0
            sq0 = (ti - NPT) * SSEQ
            if ti + 1 < NT:
                xq.append(load_x(ti + 1))
            xt, xk = xq.pop(0)
            sta, sk = stat.next()
            xs, xsk = xsr.next()
            xT, xTk = xtr.next()
            S.add("act", lambda e: e.activation(out=junk[:], in_=xt[:], func=AF.Square, accum_out=sta[:, 0:1]),
                  reads=[xk], writes=[sk + ("ss",), ("cjunk",)])
            rstd_from_ss(S, sta[:, 0:1], sta[:, 8:9], D, eps_t, sk + ("ss",), sk + ("r",))
            S.add("act", lambda e: e.activation(out=xs[:], in_=xt[:], func=AF.Copy, scale=sta[:, 8:9]),
                  reads=[xk, sk + ("r",)], writes=[xsk])
            for kc in range(8):
                S.add("pe", lambda e, kc=kc: e.transpose(out=ptr[:, kc, :], in_=xs[:, kc * 128:(kc + 1) * 128],
                                                         identity=identb[:]),
                      reads=[xsk, ("identb",)], writes=[("cptr",)])
            S.add("dve", lambda e: e.tensor_tensor(out=xT[:], in0=ptr[:], in1=g1T[:].unsqueeze(2).to_broadcast([128, 8, 128]),
                                                   op=ALU.mult), reads=[("cptr",), g1Tk], writes=[xTk])

            def proj(c0, w, M_=128):
                pz, pzk = pzr.next()
                for kc in range(8):
                    S.add("pe", lambda e, kc=kc: e.matmul(pz[0:M_, 0:w], lhsT=xT[:, kc, :], rhs=W1[:, kc, c0:c0 + w],
                                                          start=(kc == 0), stop=(kc == 7)),
                          reads=[xTk] + (wkeys + cks if first else []), writes=[pzk])
                return pz, pzk

            if KCUT <= 1:
                return
            pz, pzk = pzr.next()
            for kc in range(8):
                S.add("pe", lambda e, kc=kc: e.matmul(pz[0:16, 0:128], lhsT=W1[:, kc, 3072:3088], rhs=xT[:, kc, :],
                                                      start=(kc == 0), stop=(kc == 7)),
                      reads=[xTk] + (wkeys + cks if first else []), writes=[pzk])
            zaT, zaTk = zaTr.next()
            S.add("dve", lambda e: e.tensor_copy(out=zaT[:], in_=pz[0:16, 0:128]), reads=[pzk], writes=[zaTk])
            if KCUT <= 1.2:
                return
            pg, pgk = pzr.next()
            S.add("pe", lambda e: e.matmul(pg[:], lhsT=zaT[:], rhs=wg[:], start=True, stop=True), reads=[zaTk],
                  writes=[pgk])
            if KCUT <= 1.3:
                return
            t1, t1k = t1r.next()
            S.add("dve", lambda e: e.tensor_tensor(out=t1[:], in0=pg[:], in1=bgate[:], op=ALU.add), reads=[pgk, bgk],
                  writes=[t1k])
            if KCUT <= 1.4:
                return
            S.add("act", lambda e: e.activation(out=t1[:], in_=t1[:], func=AF.Exp, scale=-1.0), reads=[t1k],
                  writes=[t1k + ("e",)])
            S.add("act", lambda e: e.activation(out=t1[:], in_=t1[:], func=AF.Ln, bias=1.0), reads=[t1k + ("e",)],
                  writes=[t1k + ("l",)])
            if KCUT <= 1.6:
                return
            Lm = Ls if sample else Lp
            pb_, pbk = pzr.next()
            S.add("pe", lambda e: e.matmul(pb_[:], lhsT=Lm[:], rhs=t1[:], start=True, stop=True),
                  reads=[t1k + ("l",)], writes=[pbk])
            if KCUT <= 1.8:
                return
            bcs, bcsk = bcsr.next()
            eb, ebk = ebr.next()
            enb, enbk = enbr.next()
            S.add("dve", lambda e: e.tensor_copy(out=bcs[:], in_=pb_[:]), reads=[pbk], writes=[bcsk])
            S.add("act", lambda e: e.activation(out=eb[:], in_=pb_[:], func=AF.Exp), reads=[pbk], writes=[ebk])
            S.add("act", lambda e: e.activation(out=enb[:], in_=pb_[:], func=AF.Exp, scale=-1.0), reads=[pbk],
                  writes=[enbk])
            if KCUT <= 2:
                return
            pq, pqk_ = proj(0, 512)
            qt, qtk = qtr_.next()
            S.add("dve", lambda e: e.scalar_tensor_tensor(out=qt[:], in0=pq[:], scalar=128.0 ** -0.5, in1=eb[:],
                                                          op0=ALU.mult, op1=ALU.mult), reads=[pqk_, ebk], writes=[qtk])
            pk, pkk = proj(512, 512)
            kt, ktk = ktr_.next()
            S.add("dve", lambda e: e.tensor_tensor(out=kt[:], in0=pk[:], in1=enb[:], op=ALU.mult), reads=[pkk, enbk],
                  writes=[ktk])
            if KCUT <= 3:
                return
            nl = 16 if sample else 2
            selm = sels if sample else selp
            pl, plk = pzr.next()
            for h in range(4):
                S.add("pe", lambda e, h=h: e.matmul(pl[:, h * 16:h * 16 + nl], lhsT=bcs[:, h * 128:(h + 1) * 128],
                                                    rhs=selm[:, 0:nl], start=True, stop=True),
                      reads=[bcsk], writes=[plk])
            EL, ELk = ELr.next()
            S.add("act", lambda e: e.activation(out=EL[:, :, 0:nl],
                                                in_=pl[:, 0:64].rearrange("p (h n) -> p h n", h=4)[:, :, 0:nl],
                                                func=AF.Exp), reads=[plk], writes=[ELk])
            if KCUT <= 4:
                return
            qT, qTk = qTr.next()
            kT, kTk = kTr.next()
            for h in range(4):
                S.add("pe", lambda e, h=h: e.transpose(out=ptr[:, h, :], in_=qt[:, h * 128:(h + 1) * 128],
                                                       identity=identb[:]), reads=[qtk, ("identb",)],
                      writes=[("cptr",)])
            for h in range(4):
                S.add("pe", lambda e, h=h: e.transpose(out=ptr[:, 4 + h, :], in_=kt[:, h * 128:(h + 1) * 128],
                                                       identity=identb[:]), reads=[ktk, ("identb",)],
                      writes=[("cptr",)])
            S.add("dve", lambda e: e.tensor_copy(out=qT[:], in_=ptr[:, 0:4, :]), reads=[("cptr",)], writes=[qTk])
            S.add("dve", lambda e: e.tensor_copy(out=kT[:], in_=ptr[:, 4:8, :]), reads=[("cptr",)], writes=[kTk])
            v, vk = vr.next()
            sg, sgk = sgr.next()
            for b in range(2):
                pz, pzk = proj(1024 + b * 512, 512)
                S.add("act", lambda e, b=b, pz=pz: e.activation(out=v[:, b * 512:(b + 1) * 512], in_=pz[:],
                                                                func=AF.Copy), reads=[pzk], writes=[vk + (b,)])
            for b in range(2):
                pz, pzk = proj(2048 + b * 512, 512)
                S.add("act", lambda e, b=b, pz=pz: e.activation(out=sg[:, b * 512:(b + 1) * 512], in_=pz[:],
                                                                func=AF.Silu), reads=[pzk], writes=[sgk + (b,)])
            vks = [vk + (0,), vk + (1,)]
            if KCUT <= 5:
                return
            for h in range(4):
                S.add("pe", lambda e, h=h: e.matmul(pA[:, h, :], lhsT=kT[:, h, :], rhs=qT[:, h, :], start=True,
                                                    stop=True), reads=[kTk, qTk], writes=[("cpA",)])
            att, attk = attr_.next()
            Am = As if sample else Ap
            S.add("dve", lambda e: e.tensor_tensor(out=att[:], in0=pA[:], in1=Am[:].unsqueeze(1).to_broadcast([128, 4, 128]),
                                                   op=ALU.mult), reads=[("cpA",)], writes=[attk])
            for h in range(4):
                S.add("pe", lambda e, h=h: e.matmul(pOg[:, h, :], lhsT=att[:, h, :], rhs=v[:, h * 256:(h + 1) * 256],
                                                    start=(h % 2 == 0), stop=False, skip_group_check=True),
                      reads=[attk] + vks, writes=[("cpOg", h // 2)])
            if KCUT <= 6:
                return
            if not sample:
                for c in range(2):
                    cs = slice(c * 64, (c + 1) * 64)
                    for h in range(4):
                        S.add("pe", lambda e, h=h, cs=cs: e.matmul(pOg[cs, h, :], lhsT=qT[:, h, cs], rhs=Sb[:, h, :],
                                                                   start=False, stop=(c == 1), skip_group_check=True),
                              reads=[qTk, ("Sb",)], writes=[("cpOg", h // 2)])
                    for h in range(4):
                        S.add("pe", lambda e, h=h, cs=cs: e.matmul(pD[:, h, :], lhsT=kt[cs, h * 128:(h + 1) * 128],
                                                                   rhs=v[cs, h * 256:(h + 1) * 256], start=(h % 2 == 0),
                                                                   stop=True, skip_group_check=True),
                              reads=[ktk] + vks, writes=[("cpD", h // 2)])
                    tmpS, tmpSk = tmpSr.next()
                    S.add("dve", lambda e, tmpS=tmpS: e.tensor_tensor(out=tmpS[:], in0=pD[:], in1=Sst[:], op=ALU.add),
                          reads=[("cpD", 0), ("cpD", 1), ("S",)], writes=[tmpSk])
                    for h in range(4):
                        S.add("dve", lambda e, h=h, c=c, tmpS=tmpS: e.tensor_scalar(
                            out=Sst[:, h, :], in0=tmpS[:, h, :], scalar1=EL[:, h, c:c + 1], scalar2=None, op0=ALU.mult),
                            reads=[tmpSk, ELk], writes=[("S",)])
                    S.add("pool", lambda e: e.tensor_copy(out=Sb[:], in_=Sst[:]), reads=[("S",)], writes=[("Sb",)])
                if ti == NPT - 1:
                    S.add("sp", lambda e: e.dma_start(out=o_gp.rearrange("h k v -> k h v"), in_=Sst[:]),
                          reads=[("S",)], writes=[("o_gp",)], dma=True)
            elif KCUT > 7:
                for h in range(4):
                    S.add("pool", lambda e, h=h: e.tensor_copy(
                        out=QZ[:].rearrange("p (b x) -> p b x", x=136)[:, :, 0:8],
                        in_=qT[:, h, :].rearrange("p (b i) -> p b i", i=8)), reads=[qTk, ("QZ",)], writes=[("QZ",)])
                    KZ, KZk = KZr.next()
                    S.add("dve", lambda e, h=h, KZ=KZ: e.tensor_tensor(
                        out=KZ[:], in0=kt[:, h * 128:(h + 1) * 128].unsqueeze(1).to_broadcast([128, 16, 128]),
                        in1=ohs[:].unsqueeze(2).to_broadcast([128, 16, 128]), op=ALU.mult), reads=[ktk], writes=[KZk])
                    for b in range(SSEQ):
                        s0, s0k = s0r.next()
                        s0b, s0bk = s0br.next()
                        S.add("sp", lambda e, b=b, h=h, s0=s0: e.dma_start(out=s0[:, 0, :], in_=state[sq0 + b, h, :, :]),
                              writes=[s0k], dma=True)
                        S.add("pool", lambda e, s0=s0, s0b=s0b: e.tensor_copy(out=s0b[:, 0, :], in_=s0[:, 0, :]),
                              reads=[s0k], writes=[s0bk])
                        S.add("pe", lambda e, b=b, h=h, s0b=s0b: e.matmul(
                            pOg[:, h, :], lhsT=QZ[:, b * 128:(b + 1) * 128], rhs=s0b[:, 0, :], start=False,
                            stop=(b == SSEQ - 1), skip_group_check=True),
                            reads=[("QZ",), s0bk], writes=[("cpOg", h // 2)])
                        S.add("pe", lambda e, b=b, h=h, KZ=KZ: e.matmul(
                            pD[:, b % 4, :], lhsT=KZ[:, b, :], rhs=v[:, h * 256:(h + 1) * 256], start=True, stop=True,
                            skip_group_check=True), reads=[KZk] + vks, writes=[("cpD", (b % 4) // 2)])
                        sn, snk = snr.next()
                        S.add("dve", lambda e, b=b, s0=s0, sn=sn: e.tensor_tensor(out=sn[:, 0, :], in0=pD[:, b % 4, :],
                                                                                in1=s0[:, 0, :], op=ALU.add),
                              reads=[("cpD", (b % 4) // 2), s0k], writes=[snk, snk + ("f",)])
                        S.add("dve", lambda e, b=b, h=h, sn=sn: e.tensor_scalar(
                            out=sn[:, 0, :], in0=sn[:, 0, :], scalar1=EL[:, h, b:b + 1], scalar2=None, op0=ALU.mult),
                            reads=[snk, ELk], writes=[snk + ("f",)])
                        S.add("pool", lambda e, b=b, h=h, sn=sn: e.dma_start(out=o_gs[sq0 + b, h, :, :], in_=sn[:, 0, :]),
                              reads=[snk + ("f",)], writes=[("o_gs", ti, b, h)], dma=True)
            if KCUT <= 8:
                return
            sq, sqk = sqr.next()
            sta2, sk2 = stat.next()
            S.add("act", lambda e: e.activation(out=sq[:], in_=pOg[:].rearrange("p h v -> p (h v)"), func=AF.Square),
                  reads=[("cpOg", 0), ("cpOg", 1)], writes=[sqk, sqk + ("n",), sqk + ("g",)])
            S.add("dve", lambda e: e.tensor_reduce(out=sta2[:, 0:4], in_=sq[:].rearrange("p (h v) -> p h v", h=4),
                                                   axis=AX.X, op=ALU.add), reads=[sqk], writes=[sk2 + ("ss",)])
            rstd_from_ss(S, sta2[:, 0:4], sta2[:, 8:12], 256, eps_t, sk2 + ("ss",), sk2 + ("r",))
            S.add("dve", lambda e: e.tensor_tensor(out=sq[:].rearrange("p (h v) -> p h v", h=4), in0=pOg[:],
                                                   in1=sta2[:, 8:12].unsqueeze(2).to_broadcast([128, 4, 256]),
                                                   op=ALU.mult),
                  reads=[("cpOg", 0), ("cpOg", 1), sk2 + ("r",), sqk], writes=[sqk + ("n",)])
            S.add("pool", lambda e: e.tensor_tensor(out=sq[:], in0=sq[:], in1=glag[:], op=ALU.mult),
                  reads=[sqk + ("n",), glk], writes=[sqk + ("g",)])
            o1, o1k = o1r.next()
            S.add("pool", lambda e: e.tensor_tensor(out=o1[:], in0=sq[:], in1=sg[:], op=ALU.mult),
                  reads=[sqk + ("g",), sgk + (0,), sgk + (1,)], writes=[o1k])
            oT, oTk = oTr.next()
            for kc in range(8):
                S.add("pe", lambda e, kc=kc: e.transpose(out=ptr[:, kc, :], in_=o1[:, kc * 128:(kc + 1) * 128],
                                                         identity=identb[:]), reads=[o1k, ("identb",)],
                      writes=[("cptr",)])
            S.add("dve", lambda e: e.tensor_copy(out=oT[:], in_=ptr[:]), reads=[("cptr",)], writes=[oTk])
            for nb in range(2):
                pz, pzk = pzr.next()
                for kc in range(8):
                    S.add("pe", lambda e, kc=kc, nb=nb, pz=pz: e.matmul(pz[:], lhsT=oT[:, kc, :],
                                                                       rhs=Wo1[:, kc, nb * 512:(nb + 1) * 512],
                                                                       start=(kc == 0), stop=(kc == 7)),
                          reads=[oTk], writes=[pzk])
                y, yk = yr.next()
                S.add("dve", lambda e, nb=nb, pz=pz, y=y: e.tensor_tensor(out=y[:], in0=pz[:],
                                                                        in1=xt[:, nb * 512:(nb + 1) * 512], op=ALU.add),
                      reads=[pzk, xk], writes=[yk])
                S.add("sp", lambda e, nb=nb, y=y: e.dma_start(out=o_y[ti * 128:(ti + 1) * 128, nb * 512:(nb + 1) * 512],
                                                              in_=y[:]), reads=[yk], writes=[("o_y", ti, nb)], dma=True)

        for ti_ in range(NT):
            tile_body(ti_)
        S.emit()


_NC_CACHE = {}


def kernel(**inp):
    phases = inp.pop("_phases", ("A", "B", "S", "C"))
    ck = np.asarray(inp["cache_k"]).reshape(-1, 1024)
    cvv = np.asarray(inp["cache_v"]).reshape(-1, 1024)
    PP = ck.shape[0] // 128
    key = (tuple(phases), PP)
    if key not in _NC_CACHE:
        _NC_CACHE[key] = build_nc(phases, PP)
    nc = _NC_CACHE[key]
    xp = np.asarray(inp["x_prompt"], np.float32)
    xs = np.asarray(inp["x_sample"], np.float32)
    par = host_params(inp)
    cst = host_consts()
    spw = np.asarray(inp["spatial_w"], np.float32)[0]
    spwT = np.ascontiguousarray(spw.transpose(2, 0, 1))
    small = spw[:, :8, :8]
    spwTs = np.ascontiguousarray(np.tile(small.transpose(2, 0, 1), (16, 1, 16)))
    ck = np.ascontiguousarray(ck, np.float32)
    cvv = np.ascontiguousarray(cvv, np.float32)
    pt = np.asarray(inp["page_table"], np.int32)
    st = np.asarray(inp["state_gla"], np.float32)[0]
    NSQ = NST * SSEQ
    in_maps = []
    for c in range(NCORES):
        xin = np.concatenate([xp[c], xs[c * NSQ:(c + 1) * NSQ].reshape(NST * 128, D)], axis=0)
        in_maps.append({
            "xin": np.ascontiguousarray(xin), "par": par, "cst": cst,
            "w_in0": np.asarray(inp["w_in0"], np.float32)[0], "w_out0": np.asarray(inp["w_out0"], np.float32)[0],
            "w_in1": np.asarray(inp["w_in1"], np.float32)[0], "w_out1": np.asarray(inp["w_out1"], np.float32)[0],
            "spwT": spwT, "spwTs": spwTs, "cache_k": ck, "cache_v": cvv,
            "state": np.ascontiguousarray(st[c * NSQ:(c + 1) * NSQ]),
            "ptab": np.ascontiguousarray(np.broadcast_to(pt[c * NSQ:(c + 1) * NSQ].reshape(1, -1), (128, NSQ * NPAGES))),
        })
    res = run_bass_kernel_spmd(nc, in_maps, core_ids=list(range(NCORES)))
    R = res.results

    def prompt(name):
        return np.stack([R[c][name][:SEQ] for c in range(NCORES)])

    def samp(name):
        return np.concatenate([R[c][name][SEQ:].reshape(NSQ, 8, D) for c in range(NCORES)])

    y_p = prompt("o_y")
    y_s = samp("o_y")
    k_p = prompt("o_k").reshape(1, 4, SEQ, 8, 2, 64)
    v_p = prompt("o_v").reshape(1, 4, SEQ, 8, 128)
    k_s = samp("o_k").reshape(1, 128, 8, 8, 2, 64)
    v_s = samp("o_v").reshape(1, 128, 8, 8, 128)
    vb_s = np.concatenate([R[c]["o_vb"].reshape(NSQ, 8, D) for c in range(NCORES)])[None]
    g_p = np.stack([R[c]["o_gp"] for c in range(NCORES)])[None]
    g_s = np.concatenate([R[c]["o_gs"] for c in range(NCORES)])[None]
    return (y_p, y_s, k_p, v_p, k_s, v_s, vb_s, g_p, g_s)
```
